# Optimizing a Trainium2 kernel written in Bass

```python
import math
import jax, jax.numpy as jnp
from jax import lax
import numpy as np

D_MODEL = 1024
BATCH = 4
SEQ = 8192
DEPTH = 4

CTX_LEN = 256
GRID_W = 64
HEAD_DIM = 64
RW_W = 3 * D_MODEL // 8
RW_HEADS = RW_W // HEAD_DIM
RW_DECAY_LORA = 64
RW_ICLR_LORA = 64
RW_GATE_LORA = 128
GD_W = 3 * D_MODEL // 8
GD_HEADS = GD_W // HEAD_DIM
GD_CONV = 3
GD_CHUNK = 64
S5_W = D_MODEL - RW_W - GD_W
S5_CH = 16
S5_GROUPS = S5_W // S5_CH
S5_STATE = 64
S5_DT_MIN = 1e-3
S5_DT_MAX = 1e-1
FFN_HIDDEN = 256 * math.ceil(8 * D_MODEL / (3 * 256))
IN_RW = 3 * RW_W + RW_DECAY_LORA + RW_ICLR_LORA + RW_GATE_LORA
IN_GD = 3 * GD_W + 4 * GD_HEADS + GD_W
IN_TOTAL = IN_RW + IN_GD + S5_W
NORM_EPS = 1e-6
LNX_EPS = 64e-5
F32 = jnp.float32

kernel_name = 'hybrid_rwkv7_gdn_s5_adaln_prefix'


def rms_norm(x, g):
    xf = x.astype(F32)
    y = xf * lax.rsqrt(jnp.mean(xf * xf, axis=-1, keepdims=True) + NORM_EPS)
    return (y * g).astype(x.dtype)


def l2_normalize(t):
    t = t.astype(F32)
    return t * lax.rsqrt(jnp.sum(t * t, axis=-1, keepdims=True) + NORM_EPS)


def centred_shift(p):
    prev = jnp.pad(p[:, :-1], ((0, 0), (1, 0), (0, 0)))
    nxt = jnp.pad(p[:, 1:], ((0, 0), (0, 1), (0, 0)))
    return 0.5 * (prev + nxt)


def depthwise_conv_centred(x, w):
    pad = w.shape[0] // 2
    return lax.conv_general_dilated(x, w[:, None, :].astype(x.dtype), window_strides=(1,),
                                    padding=[(pad, pad)], dimension_numbers=('NWC', 'WIO', 'NWC'),
                                    feature_group_count=x.shape[-1])


def raster_to_column_major(t):
    bsz, n, ch = t.shape
    rows = n // GRID_W
    return t.reshape(bsz, rows, GRID_W, ch).transpose(0, 2, 1, 3).reshape(bsz, n, ch)


def column_major_to_raster(t):
    bsz, n, ch = t.shape
    rows = n // GRID_W
    return t.reshape(bsz, GRID_W, rows, ch).transpose(0, 2, 1, 3).reshape(bsz, n, ch)


def head_group_norm(y, gain, bias):
    mu = jnp.mean(y, axis=-1, keepdims=True)
    var = jnp.mean(jnp.square(y - mu), axis=-1, keepdims=True)
    yn = (y - mu) * lax.rsqrt(var + LNX_EPS)
    bsz, n, h, d = y.shape
    return yn.reshape(bsz, n, h * d) * gain + bias


def rwkv7_scan(r, w, k, v, a, b, s0, reverse):
    def step(S, inp):
        r_t, w_t, k_t, v_t, a_t, b_t = inp
        sa = jnp.einsum('bhvk,bhk->bhv', S, a_t)
        S = S * w_t[:, :, None, :] + sa[..., None] * b_t[:, :, None, :] + v_t[..., None] * k_t[:, :, None, :]
        return S, jnp.einsum('bhvk,bhk->bhv', S, r_t)
    xs = tuple(jnp.moveaxis(t.astype(F32), 1, 0) for t in (r, w, k, v, a, b))
    s_final, ys = lax.scan(step, s0, xs, reverse=reverse)
    return jnp.moveaxis(ys, 0, 1), s_final


def rwkv7_mixer(z, P, s0):
    bsz, n, _ = z.shape
    z = z + (centred_shift(z) - z) * P['rw_mu']
    r, k, v, wd, ad, gd = jnp.split(z, [RW_W, 2 * RW_W, 3 * RW_W, 3 * RW_W + RW_DECAY_LORA,
                                        3 * RW_W + RW_DECAY_LORA + RW_ICLR_LORA], axis=-1)
    heads = lambda t: t.reshape(bsz, n, RW_HEADS, HEAD_DIM)
    gate = jax.nn.sigmoid(gd) @ P['rw_gup']
    rh = heads(r).astype(F32)
    vh = heads(v).astype(F32)
    kk = l2_normalize(heads(k * P['rw_kk']))
    wd_t = jnp.tanh(wd.astype(F32))
    ys = []
    finals = []
    bonus = jnp.zeros_like(vh)
    for d in range(2):
        w_log = -jax.nn.softplus(-(P['rw_w0'][d] + wd_t @ P['rw_wup'][d])) - 0.5
        decay = jnp.exp(-jnp.exp(w_log))
        a = jax.nn.sigmoid(P['rw_a0'][d] + ad.astype(F32) @ P['rw_aup'][d])
        kd = heads(k.astype(F32) * (1.0 + (a - 1.0) * P['rw_ka']))
        y, s_fin = rwkv7_scan(rh, heads(decay), kd, vh, -kk, kk * heads(a), s0[d], d == 1)
        ys.append(y)
        finals.append(s_fin)
        bonus = bonus + jnp.sum(rh * kd * P['rw_rk'], axis=-1, keepdims=True) * vh
    y = head_group_norm(ys[0] + ys[1], P['rw_lnx_g'], P['rw_lnx_b']) + bonus.reshape(bsz, n, RW_W)
    return (y * gate).astype(z.dtype), jnp.stack(finals)


def gated_delta_chunked(q, k, v, beta, g, s0):
    bsz, n, h, d = q.shape
    nc = n // GD_CHUNK

    def to_chunks(t):
        return jnp.swapaxes(t.reshape((bsz, nc, GD_CHUNK) + t.shape[2:]), 2, 3)

    q, k, v, beta, g = (to_chunks(t) for t in (q, k, v, beta, g))
    G = jnp.cumsum(g, axis=-1)
    idx = jnp.arange(GD_CHUNK)
    incl = idx[:, None] >= idx[None, :]
    strict = idx[:, None] > idx[None, :]
    decay = jnp.exp(jnp.where(incl, G[..., :, None] - G[..., None, :], -jnp.inf))
    kk = jnp.einsum('bnhik,bnhjk->bnhij', k, k)
    m = jnp.where(strict, beta[..., :, None] * kk * decay, 0.0)
    eye = jnp.eye(GD_CHUNK, dtype=F32)
    rhs = jnp.concatenate([v * beta[..., None], k * (beta * jnp.exp(G))[..., None]], axis=-1)
    sol = lax.linalg.triangular_solve(eye + m, rhs, left_side=True, lower=True, unit_diagonal=True)
    w_v, w_k = jnp.split(sol, 2, axis=-1)
    a_qk = jnp.einsum('bnhik,bnhjk->bnhij', q, k) * decay
    q_g = q * jnp.exp(G)[..., None]
    k_d = k * jnp.exp(G[..., -1:] - G)[..., None]
    g_c = jnp.exp(G[..., -1])

    def step(S, inp):
        w_v_c, w_k_c, a_c, q_c, k_c, gc = inp
        U = w_v_c - jnp.einsum('bhck,bhkv->bhcv', w_k_c, S)
        O = jnp.einsum('bhck,bhkv->bhcv', q_c, S) + jnp.einsum('bhij,bhjv->bhiv', a_c, U)
        S = gc[..., None, None] * S + jnp.einsum('bhck,bhcv->bhkv', k_c, U)
        return S, O

    xs = tuple(jnp.moveaxis(t, 1, 0) for t in (w_v, w_k, a_qk, q_g, k_d, g_c))
    s_final, O = lax.scan(step, s0, xs)
    O = jnp.swapaxes(jnp.moveaxis(O, 0, 1), 2, 3).reshape(bsz, n, h, d)
    return O, s_final


def gdn_mixer(z, P, s0):
    bsz, n, _ = z.shape
    qkv, beta_in, a_in, gate = jnp.split(z, [3 * GD_W, 3 * GD_W + 2 * GD_HEADS, 3 * GD_W + 4 * GD_HEADS], axis=-1)
    qkv = jax.nn.silu(depthwise_conv_centred(qkv, P['gd_conv']))
    q, k, v = jnp.split(qkv, 3, axis=-1)
    heads = lambda t: t.reshape(bsz, n, GD_HEADS, HEAD_DIM)
    q = l2_normalize(heads(q)) * (HEAD_DIM ** -0.5)
    k = l2_normalize(heads(k))
    v = heads(v).astype(F32)
    beta = jax.nn.sigmoid(beta_in.astype(F32)).reshape(bsz, n, 2, GD_HEADS)
    g = -jnp.exp(P['gd_a_log'].astype(F32)) * jax.nn.softplus(
        a_in.astype(F32).reshape(bsz, n, 2, GD_HEADS) + P['gd_dt_bias'])
    flip = lambda t: jnp.flip(t, 1)
    o_f, s_f = gated_delta_chunked(q, k, v, beta[:, :, 0], g[:, :, 0], s0[0])
    o_b, s_b = gated_delta_chunked(flip(q), flip(k), flip(v), flip(beta[:, :, 1]), flip(g[:, :, 1]), s0[1])
    o = o_f + flip(o_b)
    o = o * lax.rsqrt(jnp.mean(o * o, axis=-1, keepdims=True) + NORM_EPS) * P['gd_norm_g']
    o = o * jax.nn.silu(heads(gate).astype(F32))
    return o.reshape(bsz, n, GD_W).astype(z.dtype), jnp.stack([s_f, s_b])


def complex_affine_combine(e1, e2):
    a1r, a1i, b1r, b1i = e1
    a2r, a2i, b2r, b2i = e2
    return (a1r * a2r - a1i * a2i, a1r * a2i + a1i * a2r,
            a2r * b1r - a2i * b1i + b2r, a2r * b1i + a2i * b1r + b2i)


def s5_scan(u, lam_re, lam_im, log_dt, b_re, b_im, c_re, c_im, h0):
    lam_re, lam_im, b_re, b_im, c_re, c_im = (t.astype(F32) for t in (lam_re, lam_im, b_re, b_im, c_re, c_im))
    dt = jnp.exp(log_dt.astype(F32))[:, None]
    mag = jnp.exp(lam_re * dt)
    ab_re = mag * jnp.cos(lam_im * dt)
    ab_im = mag * jnp.sin(lam_im * dt)
    den = lam_re * lam_re + lam_im * lam_im
    f_re = ((ab_re - 1.0) * lam_re + ab_im * lam_im) / den
    f_im = (ab_im * lam_re - (ab_re - 1.0) * lam_im) / den
    bb_re = f_re[..., None] * b_re - f_im[..., None] * b_im
    bb_im = f_re[..., None] * b_im + f_im[..., None] * b_re
    bu_re = jnp.einsum('btgh,gph->btgp', u, bb_re)
    bu_im = jnp.einsum('btgh,gph->btgp', u, bb_im)
    h0_re, h0_im = h0[0], h0[1]
    bu_re = bu_re.at[:, 0].add(ab_re * h0_re - ab_im * h0_im)
    bu_im = bu_im.at[:, 0].add(ab_re * h0_im + ab_im * h0_re)
    n = u.shape[1]
    a_re = jnp.broadcast_to(ab_re, (1, n) + ab_re.shape)
    a_im = jnp.broadcast_to(ab_im, (1, n) + ab_im.shape)
    _, _, h_re, h_im = lax.associative_scan(complex_affine_combine, (a_re, a_im, bu_re, bu_im), axis=1)
    y = jnp.einsum('btgp,ghp->btgh', h_re, c_re) - jnp.einsum('btgp,ghp->btgh', h_im, c_im)
    return y, jnp.stack([h_re[:, -1], h_im[:, -1]])


def s5_mixer(u, P, s0, column_major):
    bsz, n, _ = u.shape
    uf = u.astype(F32)
    us = raster_to_column_major(uf) if column_major else uf
    ug = us.reshape(bsz, n, S5_GROUPS, S5_CH)
    ys = []
    finals = []
    for d in range(2):
        seq = ug if d == 0 else jnp.flip(ug, 1)
        y, h_fin = s5_scan(seq, P['s5_lam_re'][d], P['s5_lam_im'][d], P['s5_log_dt'][d],
                           P['s5_b_re'][d], P['s5_b_im'][d], P['s5_c_re'][d], P['s5_c_im'][d], s0[d])
        ys.append(y if d == 0 else jnp.flip(y, 1))
        finals.append(h_fin)
    y = (ys[0] + ys[1]).reshape(bsz, n, S5_W)
    if column_major:
        y = column_major_to_raster(y)
    y = jax.nn.gelu(y + P['s5_d'] * uf)
    y = y * jax.nn.sigmoid(y @ P['s5_glu_w'] + P['s5_glu_b'])
    return y.astype(u.dtype), jnp.stack(finals)


def token_mixers(h, P, states0, column_major):
    z = h @ P['w_in']
    z_rw, z_gd, z_s5 = jnp.split(z, [IN_RW, IN_RW + IN_GD], axis=-1)
    y_rw, s_rw = rwkv7_mixer(z_rw, P, states0[0])
    y_gd, s_gd = gdn_mixer(z_gd, P, states0[1])
    y_s5, s_s5 = s5_mixer(z_s5, P, states0[2], column_major)
    return jnp.concatenate([y_rw, y_gd, y_s5], axis=-1), (s_rw, s_gd, s_s5)


def ada_modulation(cond, w, b):
    return jnp.split(jax.nn.silu(cond) @ w + b, 6, axis=-1)


def swiglu(h, w_gate, w_up, w_down):
    return (jax.nn.silu(h @ w_gate) * (h @ w_up)) @ w_down


def setup_inputs(seed: int = 0) -> dict:
    key = jax.random.key(seed)
    ks = iter(jax.random.split(key, 64))

    def nrm(shape, scale):
        return jax.random.normal(next(ks), shape, F32) * scale

    def unif(shape, lo, hi):
        return jax.random.uniform(next(ks), shape, F32, lo, hi)

    L, D, F = DEPTH, D_MODEL, FFN_HIDDEN
    G, P, H = S5_GROUPS, S5_STATE, S5_CH
    gd_dt = jnp.exp(unif((L, 2, GD_HEADS), math.log(1e-3), math.log(1e-1)))
    return {
        'x': nrm((BATCH, SEQ, D), 1.0),
        'c': nrm((BATCH, D), 1.0),
        'ctx': nrm((BATCH, CTX_LEN, D), 1.0),
        'c_ctx': nrm((D,), 1.0),
        'norm1_g': 1.0 + nrm((L, D), 0.02),
        'norm2_g': 1.0 + nrm((L, D), 0.02),
        'final_g': 1.0 + nrm((D,), 0.02),
        'ada_w': nrm((L, D, 6 * D), D ** -0.5),
        'ada_b': nrm((L, 6 * D), 0.01),
        'w_in': nrm((L, D, IN_TOTAL), D ** -0.5),
        'w_out': nrm((L, D, D), D ** -0.5),
        'rw_mu': unif((L, IN_RW), 0.0, 1.0),
        'rw_w0': nrm((L, 2, RW_W), 0.5),
        'rw_wup': nrm((L, 2, RW_DECAY_LORA, RW_W), 0.1),
        'rw_a0': nrm((L, 2, RW_W), 0.5),
        'rw_aup': nrm((L, 2, RW_ICLR_LORA, RW_W), RW_ICLR_LORA ** -0.5),
        'rw_gup': nrm((L, RW_GATE_LORA, RW_W), RW_GATE_LORA ** -0.5),
        'rw_kk': 0.85 + nrm((L, RW_W), 0.05),
        'rw_ka': 1.0 + nrm((L, RW_W), 0.05),
        'rw_rk': nrm((L, RW_HEADS, HEAD_DIM), 0.1),
        'rw_lnx_g': 1.0 + nrm((L, RW_W), 0.02),
        'rw_lnx_b': nrm((L, RW_W), 0.01),
        'gd_conv': nrm((L, GD_CONV, 3 * GD_W), GD_CONV ** -0.5),
        'gd_a_log': jnp.log(unif((L, 2, GD_HEADS), 1.0, 16.0)),
        'gd_dt_bias': gd_dt + jnp.log(-jnp.expm1(-gd_dt)),
        'gd_norm_g': 1.0 + nrm((L, HEAD_DIM), 0.02),
        's5_lam_re': -0.5 + nrm((L, 2, G, P), 0.01),
        's5_lam_im': jnp.pi * jnp.arange(P, dtype=F32) + nrm((L, 2, G, P), 0.01),
        's5_log_dt': unif((L, 2, G), math.log(S5_DT_MIN), math.log(S5_DT_MAX)),
        's5_b_re': nrm((L, 2, G, P, H), (2 * H) ** -0.5),
        's5_b_im': nrm((L, 2, G, P, H), (2 * H) ** -0.5),
        's5_c_re': nrm((L, 2, G, H, P), P ** -0.5),
        's5_c_im': nrm((L, 2, G, H, P), P ** -0.5),
        's5_d': nrm((L, S5_W), 1.0),
        's5_glu_w': nrm((L, S5_W, S5_W), S5_W ** -0.5),
        's5_glu_b': nrm((L, S5_W), 0.01),
        'ffn_w_gate': nrm((L, D, F), D ** -0.5),
        'ffn_w_up': nrm((L, D, F), D ** -0.5),
        'ffn_w_down': nrm((L, F, D), F ** -0.5),
    }


def reference(x, c, ctx, c_ctx, norm1_g, norm2_g, final_g, ada_w, ada_b, w_in, w_out,
              rw_mu, rw_w0, rw_wup, rw_a0, rw_aup, rw_gup, rw_kk, rw_ka, rw_rk, rw_lnx_g, rw_lnx_b,
              gd_conv, gd_a_log, gd_dt_bias, gd_norm_g,
              s5_lam_re, s5_lam_im, s5_log_dt, s5_b_re, s5_b_im, s5_c_re, s5_c_im, s5_d, s5_glu_w, s5_glu_b,
              ffn_w_gate, ffn_w_up, ffn_w_down):
    bsz = x.shape[0]
    zero_states = (jnp.zeros((2, bsz, RW_HEADS, HEAD_DIM, HEAD_DIM), F32),
                   jnp.zeros((2, bsz, GD_HEADS, HEAD_DIM, HEAD_DIM), F32),
                   jnp.zeros((2, 2, bsz, S5_GROUPS, S5_STATE), F32))
    for l in range(DEPTH):
        P = {
            'w_in': w_in[l], 'rw_mu': rw_mu[l], 'rw_w0': rw_w0[l], 'rw_wup': rw_wup[l], 'rw_a0': rw_a0[l],
            'rw_aup': rw_aup[l], 'rw_gup': rw_gup[l], 'rw_kk': rw_kk[l], 'rw_ka': rw_ka[l], 'rw_rk': rw_rk[l],
            'rw_lnx_g': rw_lnx_g[l], 'rw_lnx_b': rw_lnx_b[l], 'gd_conv': gd_conv[l], 'gd_a_log': gd_a_log[l],
            'gd_dt_bias': gd_dt_bias[l], 'gd_norm_g': gd_norm_g[l], 's5_lam_re': s5_lam_re[l],
            's5_lam_im': s5_lam_im[l], 's5_log_dt': s5_log_dt[l], 's5_b_re': s5_b_re[l], 's5_b_im': s5_b_im[l],
            's5_c_re': s5_c_re[l], 's5_c_im': s5_c_im[l], 's5_d': s5_d[l], 's5_glu_w': s5_glu_w[l],
            's5_glu_b': s5_glu_b[l],
        }
        sh1c, sc1c, g1c, sh2c, sc2c, g2c = ada_modulation(c_ctx[None, None, :], ada_w[l], ada_b[l])
        hc = rms_norm(ctx, norm1_g[l]) * (1.0 + sc1c) + sh1c
        mix_c, ctx_states = token_mixers(hc, P, zero_states, False)
        sh1x, sc1x, g1x, sh2x, sc2x, g2x = ada_modulation(c[:, None, :], ada_w[l], ada_b[l])
        hx = rms_norm(x, norm1_g[l]) * (1.0 + sc1x) + sh1x
        mix_x, _ = token_mixers(hx, P, ctx_states, True)
        x = x + g1x * (mix_x @ w_out[l])
        hx2 = rms_norm(x, norm2_g[l]) * (1.0 + sc2x) + sh2x
        x = x + g2x * swiglu(hx2, ffn_w_gate[l], ffn_w_up[l], ffn_w_down[l])
        if l < DEPTH - 1:
            ctx = ctx + g1c * (mix_c @ w_out[l])
            hc2 = rms_norm(ctx, norm2_g[l]) * (1.0 + sc2c) + sh2c
            ctx = ctx + g2c * swiglu(hc2, ffn_w_gate[l], ffn_w_up[l], ffn_w_down[l])
    return rms_norm(x, final_g)
```

```python
import contextlib
import math
import numpy as np
import concourse.bass as bass
import concourse.mybir as mybir
from concourse.bass_utils import run_bass_kernel_spmd

F32 = mybir.dt.float32
BF16 = mybir.dt.bfloat16
AF = mybir.ActivationFunctionType
ALU = mybir.AluOpType

ENGS = ("pe", "dve", "act", "pool", "sp")

D = 1024
HD = 64
RW_W = 384
GD_W = 384
S5_W = 256
IN_RW = 1408
IN_TOTAL = 3224
FFN = 2816
NORM_EPS = 1e-6
LNX_EPS = 64e-5
Z_GDQ = 1408
Z_GDBA = 2560
Z_GDG = 2584
Z_S5 = 2968


class Buf:
    __slots__ = ("name", "w", "r", "multi")

    def __init__(self, name="", multi=False):
        self.name = name
        self.multi = multi
        self.w = {}
        self.r = {}


class Prog:
    EPOCH = 30000
    NDS = 24

    def __init__(self, nc):
        self.nc = nc
        self.stack = contextlib.ExitStack()
        self.ops = {e: [] for e in ENGS}
        self.cnt = {e: 0 for e in ENGS}
        self.seen = {e: {} for e in ENGS}
        self.sems = {}
        self.dcnt = {q: [0] * self.NDS for q in ENGS}
        self.dnext = {q: 0 for q in ENGS}
        self.dma_out = {}

    def sem(self, key):
        s = self.sems.get(key)
        if s is None:
            s = self.stack.enter_context(self.nc.semaphore("s_%s_%s" % key))
            self.sems[key] = s
        return s

    def _waits(self, eng, reads, writes):
        need = {}
        for b in reads:
            for k, v in b.w.items():
                if need.get(k, 0) < v:
                    need[k] = v
        for b in writes:
            for k, v in b.r.items():
                if need.get(k, 0) < v:
                    need[k] = v
            if not b.multi:
                for k, v in b.w.items():
                    if need.get(k, 0) < v:
                        need[k] = v
        out = []
        seen = self.seen[eng]
        for k, v in need.items():
            if seen.get(k, 0) < v:
                seen[k] = v
                out.append((k, v))
        return out

    def _commit(self, tok, reads, writes):
        k, v = tok
        for b in reads:
            if b.r.get(k, 0) < v:
                b.r[k] = v
        for b in writes:
            if b.r or not b.multi:
                b.r = {}
                b.w = {k: v}
            elif b.w.get(k, 0) < v:
                b.w[k] = v

    def op(self, eng, fn, reads=(), writes=()):
        waits = self._waits(eng, reads, writes)
        i = self.cnt[eng]
        self.cnt[eng] = i + 1
        tok = ((eng, i // self.EPOCH), i % self.EPOCH + 1)
        self.ops[eng].append((waits, fn, tok[0], 1))
        self._commit(tok, reads, writes)
        return tok

    def dma(self, out_ap, in_ap, reads=(), writes=(), q="sp"):
        s = self.dnext[q]
        self.dnext[q] = (s + 1) % self.NDS
        key = ("dma" + q, s)
        prev = 16 * self.dcnt[q][s]
        self.dcnt[q][s] += 1
        val = prev + 16
        waits = self._waits(q, reads, writes)
        if prev and self.seen[q].get(key, 0) < prev:
            self.seen[q][key] = prev
            waits.append((key, prev))
        self.ops[q].append((waits, lambda e: e.dma_start(out=out_ap, in_=in_ap), key, 16))
        tok = (key, val)
        self._commit(tok, reads, writes)
        self.dma_out[key] = val
        return tok

    def mark(self):
        pass

    def barrier(self):
        toks = dict(self.dma_out)
        for e in ENGS:
            i = self.cnt[e]
            if i:
                toks[(e, (i - 1) // self.EPOCH)] = (i - 1) % self.EPOCH + 1
        for e in ENGS:
            waits = []
            seen = self.seen[e]
            for k, v in toks.items():
                if seen.get(k, 0) < v:
                    seen[k] = v
                    waits.append((k, v))
            if waits:
                self.ops[e].append((waits, None, None, 0))

    def emit(self):
        self.barrier()
        prog = self

        def run(e, name):
            for waits, fn, key, n in prog.ops[name]:
                for k, v in waits:
                    e.wait_ge(prog.sem(k), v)
                if fn is not None:
                    fn(e).then_inc(prog.sem(key), n)

        for name in ENGS:
            for waits, fn, key, n in self.ops[name]:
                for k, v in waits:
                    self.sem(k)
                if key is not None:
                    self.sem(key)
        with self.nc.Block() as block:
            @block.sync
            def _(e):
                run(e, "sp")

            @block.tensor
            def _(e):
                run(e, "pe")

            @block.vector
            def _(e):
                run(e, "dve")

            @block.scalar
            def _(e):
                run(e, "act")

            @block.gpsimd
            def _(e):
                run(e, "pool")
        self.stack.close()


def MM(out, l, r, start=True, stop=True):
    return lambda e: e.matmul(out, l, r, start=start, stop=stop)


def ACT(out, in_, func, bias=None, scale=None):
    kw = {}
    if bias is not None:
        kw["bias"] = bias
    if scale is not None:
        kw["scale"] = scale
    return lambda e: e.activation(out=out, in_=in_, func=func, **kw)


def TT(out, a, b, op):
    return lambda e: e.tensor_tensor(out, a, b, op)


def TS(out, a, s1, op0, s2=None, op1=None):
    if op1 is None:
        return lambda e: e.tensor_scalar(out, a, s1, None, op0)
    return lambda e: e.tensor_scalar(out, a, s1, s2, op0, op1)


def STT(out, a, s, b, op0, op1):
    return lambda e: e.scalar_tensor_tensor(out, a, s, b, op0, op1)


def CP(out, in_):
    return lambda e: e.tensor_copy(out, in_)


def MS(out, val):
    return lambda e: e.memset(out, val)


class Rec:
    def __init__(self):
        self.items = []

    def op(self, eng, fn, reads=(), writes=()):
        self.items.append(("op", eng, fn, tuple(reads), tuple(writes)))

    def dma(self, out_ap, in_ap, reads=(), writes=(), q="sp"):
        self.items.append(("dma", out_ap, in_ap, tuple(reads), tuple(writes), q))

    def mark(self):
        self.items.append(("mark",))

    def segments(self):
        segs, cur = [], []
        for it in self.items:
            if it[0] == "mark":
                if cur:
                    segs.append(cur)
                    cur = []
            else:
                cur.append(it)
        if cur:
            segs.append(cur)
        return segs


def replay_interleaved(P, recs):
    seglists = [r.segments() for r in recs]
    idx = [0] * len(seglists)
    live = True
    while live:
        live = False
        for i, sl in enumerate(seglists):
            if idx[i] < len(sl):
                live = True
                for it in sl[idx[i]]:
                    if it[0] == "op":
                        P.op(it[1], it[2], reads=it[3], writes=it[4])
                    else:
                        P.dma(it[1], it[2], reads=it[3], writes=it[4], q=it[5])
                idx[i] += 1


class Ctx:
    def __init__(self, nc, cfg):
        self.nc = nc
        self.cfg = cfg
        self.P = Prog(nc)
        self.dram = {}
        self.dbuf = {}
        self.ncnt = 0

    def din(self, name, shape, dt=F32):
        self.dram[name] = self.nc.dram_tensor(name, list(shape), dt, kind="ExternalInput").ap()
        self.dbuf[name] = Buf(name, True)
        return self.dram[name]

    def dout(self, name, shape, dt=F32):
        self.dram[name] = self.nc.dram_tensor(name, list(shape), dt, kind="ExternalOutput").ap()
        self.dbuf[name] = Buf(name, True)
        return self.dram[name]

    def dscr(self, name, shape, dt=F32):
        kind = "ExternalOutput" if name in self.cfg.get("debug", ()) else "Internal"
        self.dram[name] = self.nc.dram_tensor(name, list(shape), dt, kind=kind).ap()
        self.dbuf[name] = Buf(name, True)
        return self.dram[name]

    def sb(self, st, shape, dt=F32, name=None, multi=False):
        self.ncnt += 1
        nm = "%s_%d" % (name or "t", self.ncnt)
        t = st.enter_context(self.nc.sbuf_tensor(nm, list(shape), dt))
        return t, Buf(nm, multi)

    def ps(self, st, shape=(128, 512), dt=F32, name=None):
        self.ncnt += 1
        nm = "%s_%d" % (name or "ps", self.ncnt)
        t = st.enter_context(self.nc.psum_tensor(nm, list(shape), dt))
        return t, Buf(nm, True)

    def ld(self, out_ap, ob, dname, in_ap):
        self.P.dma(out_ap, in_ap, reads=[self.dbuf[dname]], writes=[ob], q="sp")

    def stq(self, dname, out_ap, in_ap, ib):
        self.P.dma(out_ap, in_ap, reads=[ib], writes=[self.dbuf[dname]], q="pool")


def token_tiles(cfg, n=512):
    C, T = cfg["C"], cfg["T"]
    tiles = []
    t = 0
    while t < C:
        m = min(n, C - t)
        tiles.append((t, m, 1))
        t += m
    t = C
    while t < C + T:
        m = min(n, C + T - t)
        tiles.append((t, m, 0))
        t += m
    return tiles


def stage_mods(K, gst):
    nc, P, cfg = K.nc, K.P, K.cfg
    L = cfg["L"]
    mod, bmod = K.sb(gst, [128, L, 48, 2], F32, "mod")
    A1, bA1 = K.sb(gst, [128, L, 8, 2], F32, "A1")
    A2, bA2 = K.sb(gst, [128, L, 8, 2], F32, "A2")
    with contextlib.ExitStack() as st:
        cond, bcond = K.sb(st, [128, 8, 2], F32, "cond")
        sc, bsc = K.sb(st, [128, 8, 2], F32, "scond")
        adb, badb = K.sb(st, [128, L, 48], F32, "adab")
        n1, bn1 = K.sb(st, [128, L, 8], F32, "n1g")
        n2, bn2 = K.sb(st, [128, L, 8], F32, "n2g")
        K.ld(cond[:], bcond, "condT", K.dram["condT"])
        K.ld(adb[:], badb, "ada_b", K.dram["ada_b"])
        K.ld(n1[:], bn1, "norm1_g", K.dram["norm1_g"])
        K.ld(n2[:], bn2, "norm2_g", K.dram["norm2_g"])
        P.op("act", ACT(sc[:], cond[:], AF.Silu), reads=[bcond], writes=[bsc])
        wts = [K.sb(st, [128, 8, 512], F32, "adaw") for _ in range(2)]
        pss = [K.ps(st) for _ in range(2)]
        it = 0
        for l in range(L):
            for cb in range(12):
                w, bw = wts[it % 2]
                ps, bps = pss[it % 2]
                it += 1
                K.ld(w[:], bw, "ada_w",
                     K.dram["ada_w"][l, :, cb * 512:(cb + 1) * 512].rearrange("(k p) n -> p k n", p=128))
                for o in range(4):
                    for k in range(8):
                        P.op("pe", MM(ps[:, o * 2:o * 2 + 2], w[:, k, o * 128:(o + 1) * 128], sc[:, k, :],
                                      k == 0, k == 7), reads=[bw, bsc], writes=[bps])
                for o in range(4):
                    oc = cb * 4 + o
                    P.op("dve", TS(mod[:, l, oc, :], ps[:, o * 2:o * 2 + 2], adb[:, l, oc:oc + 1], ALU.add),
                         reads=[bps, badb], writes=[bmod])
        for l in range(L):
            for m in range(2):
                P.op("dve", STT(A1[:, l, :, m], mod[:, l, 8:16, m], 1.0, n1[:, l, :], ALU.add, ALU.mult),
                     reads=[bmod, bn1], writes=[bA1])
                P.op("dve", STT(A2[:, l, :, m], mod[:, l, 32:40, m], 1.0, n2[:, l, :], ALU.add, ALU.mult),
                     reads=[bmod, bn2], writes=[bA2])
    P.barrier()
    K.mod, K.bmod, K.A1, K.bA1, K.A2, K.bA2 = mod, bmod, A1, bA1, A2, bA2


def norm_mod(K, st_bufs, x, bx, n, A, bA, boff, l, m, h, bh, ones, bones, pss):
    P = K.P
    sq, bsq, rstd, brstd, tmp, btmp = st_bufs
    ps, bps = pss
    P.op("act", ACT(sq[:, :, :n], x[:, :, :n], AF.Square), reads=[bx], writes=[bsq])
    for k in range(8):
        P.op("pe", MM(ps[:, :n], ones[:, :], sq[:, k, :n], k == 0, k == 7), reads=[bsq, bones], writes=[bps])
    P.op("dve", TS(rstd[:, :n], ps[:, :n], 1.0 / D, ALU.mult, NORM_EPS, ALU.add), reads=[bps], writes=[brstd])
    P.op("act", ACT(rstd[:, :n], rstd[:, :n], AF.Ln), reads=[brstd], writes=[brstd]); P.op("act", ACT(rstd[:, :n], rstd[:, :n], AF.Exp, scale=-0.5), reads=[brstd], writes=[brstd])
    for k in range(8):
        P.op("dve", STT(tmp[:, k, :n], x[:, k, :n], A[:, l, k, m:m + 1], rstd[:, :n], ALU.mult, ALU.mult),
             reads=[bx, bA, brstd], writes=[btmp])
    for k in range(8):
        P.op("act", ACT(h[:, k, :n], tmp[:, k, :n], AF.Identity, bias=K.mod[:, l, boff + k, m:m + 1], scale=1.0),
             reads=[btmp, K.bmod], writes=[bh])


def stage_A(K, l, xin):
    nc, P, cfg = K.nc, K.P, K.cfg
    segs = []
    for (c0, nco) in ((0, IN_RW), (Z_GDQ, 1152), (Z_GDBA, 24), (Z_GDG, 384), (Z_S5, 256)):
        o = 0
        while o < nco:
            m = min(128, nco - o)
            segs.append((c0 + o, m))
            o += m
    with contextlib.ExitStack() as st:
        W, bW = K.sb(st, [128, 8, IN_TOTAL], BF16, "win", multi=True)
        ones, bones = K.sb(st, [128, 128], F32, "ones")
        P.op("dve", MS(ones[:], 1.0), writes=[bones])
        wst = [K.sb(st, [128, 8, 512], F32, "wst") for _ in range(2)]
        nb = (IN_TOTAL + 511) // 512
        for cb in range(nb):
            w, bw = wst[cb % 2]
            c0 = cb * 512
            m = min(512, IN_TOTAL - c0)
            K.ld(w[:, :, :m], bw, "w_in", K.dram["w_in"][l, :, c0:c0 + m].rearrange("(k p) n -> p k n", p=128))
            if cb % 2:
                P.op("act", ACT(W[:, :, c0:c0 + m], w[:, :, :m], AF.Copy), reads=[bw], writes=[bW])
            else:
                P.op("dve", CP(W[:, :, c0:c0 + m], w[:, :, :m]), reads=[bw], writes=[bW])
        xs = [K.sb(st, [128, 8, 512], F32, "x") for _ in range(2)]
        hs = [K.sb(st, [128, 8, 512], BF16, "h") for _ in range(2)]
        sq, bsq = K.sb(st, [128, 8, 512], F32, "sq")
        tmp, btmp = K.sb(st, [128, 8, 512], F32, "tmp")
        rstd, brstd = K.sb(st, [128, 512], F32, "rstd")
        psn = K.ps(st)
        pz = [K.ps(st) for _ in range(4)]
        zo = [K.sb(st, [128, 512], F32, "zo") for _ in range(4)]
        xT3 = K.dram[xin].rearrange("(k p) t -> p k t", p=128)
        for ti, (t0, n, m) in enumerate(token_tiles(cfg)):
            x, bx = xs[ti % 2]
            h, bh = hs[ti % 2]
            K.ld(x[:, :, :n], bx, xin, xT3[:, :, t0:t0 + n])
            norm_mod(K, (sq, bsq, rstd, brstd, tmp, btmp), x, bx, n, K.A1, K.bA1, 0, l, m, h, bh, ones, bones, psn)
            for si, (c0, mc) in enumerate(segs):
                ps, bps = pz[si % 4]
                z, bz = zo[si % 4]
                for k in range(8):
                    P.op("pe", MM(ps[:mc, :n], W[:, k, c0:c0 + mc], h[:, k, :n], k == 0, k == 7),
                         reads=[bW, bh], writes=[bps])
                if si % 2:
                    P.op("act", ACT(z[:mc, :n], ps[:mc, :n], AF.Copy), reads=[bps], writes=[bz])
                else:
                    P.op("dve", CP(z[:mc, :n], ps[:mc, :n]), reads=[bps], writes=[bz])
                K.stq("Z", K.dram["Z"][c0:c0 + mc, t0:t0 + n], z[:mc, :n], bz)
    P.barrier()


def seg_bounds(cfg, t0):
    C, T = cfg["C"], cfg["T"]
    return (0, C) if t0 < C else (C, C + T)


def load_halo(K, zt, bzt, dname, rows_ap, t0, n, cfg):
    P = K.P
    s0, s1 = seg_bounds(cfg, t0)
    a = max(t0 - 1, s0)
    b = min(t0 + n + 1, s1)
    if a > t0 - 1:
        P.op("dve", MS(zt[:, :, 0:1], 0.0), writes=[bzt])
    if b < t0 + n + 1:
        P.op("dve", MS(zt[:, :, n + 1:n + 2], 0.0), writes=[bzt])
    K.ld(zt[:, :, a - (t0 - 1):b - (t0 - 1)], bzt, dname, rows_ap[:, :, a:b])


def stage_rw_pre(K, l):
    nc, P, cfg = K.nc, K.P, K.cfg
    dr = K.dram
    NEG = -math.exp(-0.5)
    with contextlib.ExitStack() as st:
        mu, bmu = K.sb(st, [128, 11], F32, "mu")
        w0, bw0 = K.sb(st, [128, 2, 3], F32, "w0")
        a0, ba0 = K.sb(st, [128, 2, 3], F32, "a0")
        wup, bwup = K.sb(st, [128, 2, 384], F32, "wup")
        gup, bgup = K.sb(st, [128, 384], F32, "gup")
        vec, bvec = K.sb(st, [128, 5, 3], F32, "vec")
        bones, bbones = K.sb(st, [128, 128], F32, "bones")
        K.ld(mu[:], bmu, "rw_mu", dr["rw_mu"][:, l, :])
        K.ld(w0[:], bw0, "rw_w0", dr["rw_w0"][:, l])
        K.ld(a0[:], ba0, "rw_a0", dr["rw_a0"][:, l])
        K.ld(wup[0:64], bwup, "rw_wup", dr["rw_wup"][:, l])
        K.ld(wup[64:128], bwup, "rw_aup", dr["rw_aup"][:, l])
        K.ld(gup[:], bgup, "rw_gup", dr["rw_gup"][:, l, :])
        K.ld(vec[:, 0, :], bvec, "rw_kk", dr["rw_kk"][:, l, :])
        K.ld(vec[:, 1, :], bvec, "rw_ka", dr["rw_ka"][:, l, :])
        K.ld(vec[:, 2, :], bvec, "rw_rk", dr["rw_rk"][:, l, :])
        K.ld(bones[:], bbones, "c_bones", dr["c_bones"])
        P.op("dve", TS(vec[:, 3, :], vec[:, 1, :], -1.0, ALU.mult, 1.0, ALU.add), reads=[bvec], writes=[bvec])
        zts = [K.sb(st, [128, 11, 514], F32, "zt") for _ in range(2)]
        zms = [K.sb(st, [128, 11, 512], F32, "zm") for _ in range(2)]
        s_, bs_ = K.sb(st, [128, 11, 512], F32, "s")
        tw, btw = K.sb(st, [128, 512], F32, "tw")
        sg, bsg = K.sb(st, [128, 512], F32, "sg")
        NT = 6
        tmps = [K.sb(st, [128, 512], F32, "tmp") for _ in range(NT)]
        outs = [K.sb(st, [128, 512], F32, "out") for _ in range(8)]
        kds = [K.sb(st, [128, 512], F32, "kd") for _ in range(2)]
        pss = [K.ps(st) for _ in range(6)]
        ctr = {"t": 0, "o": 0, "p": 0}

        def T_():
            ctr["t"] += 1
            return tmps[ctr["t"] % NT]

        def O_():
            ctr["o"] += 1
            return outs[ctr["o"] % 8]

        def PS_():
            ctr["p"] += 1
            return pss[ctr["p"] % 6]

        Zr = dr["Z"][0:IN_RW, :].rearrange("(k p) t -> p k t", p=128)
        for ti, (t0, n, m) in enumerate(token_tiles(cfg)):
            zt, bzt = zts[ti % 2]
            zm, bzm = zms[ti % 2]
            load_halo(K, zt, bzt, "Z", Zr, t0, n, cfg)
            zc = zt[:, :, 1:n + 1]
            P.op("dve", TT(s_[:, :, :n], zt[:, :, 0:n], zt[:, :, 2:n + 2], ALU.add), reads=[bzt], writes=[bs_])
            P.op("dve", STT(s_[:, :, :n], s_[:, :, :n], 0.5, zc, ALU.mult, ALU.subtract), reads=[bs_, bzt], writes=[bs_])
            for k in range(11):
                P.op("dve", STT(zm[:, k, :n], s_[:, k, :n], mu[:, k:k + 1], zt[:, k, 1:n + 1], ALU.mult, ALU.add),
                     reads=[bs_, bzt, bmu], writes=[bzm])
            P.op("act", ACT(tw[0:64, :n], zm[0:64, 9, :n], AF.Tanh), reads=[bzm], writes=[btw])
            P.op("act", ACT(sg[:, :n], zm[:, 10, :n], AF.Sigmoid), reads=[bzm], writes=[bsg])
            for c3 in range(3):
                cs = slice(c3 * 128, (c3 + 1) * 128)
                rows = slice(c3 * 128, (c3 + 1) * 128)
                r_ = zm[:, c3, :n]
                k_ = zm[:, 3 + c3, :n]
                v_ = zm[:, 6 + c3, :n]
                K.stq("RW_R", dr["RW_R"][rows, t0:t0 + n], r_, bzm)
                K.stq("RW_V", dr["RW_V"][rows, t0:t0 + n], v_, bzm)
                ps, bps = PS_()
                P.op("pe", MM(ps[:, :n], gup[:, cs], sg[:, :n]), reads=[bgup, bsg], writes=[bps])
                o, bo = O_()
                P.op("act", ACT(o[:, :n], ps[:, :n], AF.Copy), reads=[bps], writes=[bo])
                K.stq("RW_GATE", dr["RW_GATE"][rows, t0:t0 + n], o[:, :n], bo)
                for d in range(2):
                    ps, bps = PS_()
                    P.op("pe", MM(ps[:, :n], wup[0:64, d, cs], tw[0:64, :n]), reads=[bwup, btw], writes=[bps])
                    t1, bt1 = T_()
                    P.op("act", ACT(t1[:, :n], ps[:, :n], AF.Sigmoid, bias=w0[:, d, c3:c3 + 1], scale=1.0),
                         reads=[bps, bw0], writes=[bt1])
                    o, bo = O_()
                    P.op("dve", TS(o[:, :n], t1[:, :n], NEG, ALU.mult), reads=[bt1], writes=[bo])
                    K.stq("RW_LAM", dr["RW_LAM"][d, rows, t0:t0 + n], o[:, :n], bo)
                    ps, bps = PS_()
                    P.op("pe", MM(ps[:, :n], wup[64:128, d, cs], zm[64:128, 9, :n]), reads=[bwup, bzm], writes=[bps])
                    o, bo = O_()
                    P.op("act", ACT(o[:, :n], ps[:, :n], AF.Sigmoid, bias=a0[:, d, c3:c3 + 1], scale=1.0),
                         reads=[bps, ba0], writes=[bo])
                    K.stq("RW_AS", dr["RW_AS"][d, rows, t0:t0 + n], o[:, :n], bo)
                    t1, bt1 = T_()
                    P.op("dve", TS(t1[:, :n], o[:, :n], vec[:, 1, c3:c3 + 1], ALU.mult, vec[:, 3, c3:c3 + 1], ALU.add),
                         reads=[bo, bvec], writes=[bt1])
                    kd, bkd = kds[d]
                    P.op("dve", TT(kd[:, :n], k_, t1[:, :n], ALU.mult), reads=[bzm, bt1], writes=[bkd])
                    K.stq("RW_KD", dr["RW_KD"][d, rows, t0:t0 + n], kd[:, :n], bkd)
                t1, bt1 = T_()
                P.op("dve", TS(t1[:, :n], k_, vec[:, 0, c3:c3 + 1], ALU.mult), reads=[bzm, bvec], writes=[bt1])
                t2, bt2 = T_()
                P.op("act", ACT(t2[:, :n], t1[:, :n], AF.Square), reads=[bt1], writes=[bt2])
                ps, bps = PS_()
                P.op("pe", MM(ps[:, :n], bones[:, :], t2[:, :n]), reads=[bbones, bt2], writes=[bps])
                t3, bt3 = T_()
                P.op("dve", TS(t3[:, :n], ps[:, :n], NORM_EPS, ALU.add), reads=[bps], writes=[bt3])
                P.op("act", ACT(t3[:, :n], t3[:, :n], AF.Ln), reads=[bt3], writes=[bt3]); P.op("act", ACT(t3[:, :n], t3[:, :n], AF.Exp, scale=-0.5), reads=[bt3], writes=[bt3])
                o, bo = O_()
                P.op("dve", TT(o[:, :n], t1[:, :n], t3[:, :n], ALU.mult), reads=[bt1, bt3], writes=[bo])
                K.stq("RW_KK", dr["RW_KK"][rows, t0:t0 + n], o[:, :n], bo)
                t1, bt1 = T_()
                P.op("dve", TT(t1[:, :n], kds[0][0][:, :n], kds[1][0][:, :n], ALU.add),
                     reads=[kds[0][1], kds[1][1]], writes=[bt1])
                t2, bt2 = T_()
                P.op("dve", STT(t2[:, :n], t1[:, :n], vec[:, 2, c3:c3 + 1], r_, ALU.mult, ALU.mult),
                     reads=[bt1, bvec, bzm], writes=[bt2])
                ps, bps = PS_()
                P.op("pe", MM(ps[:, :n], bones[:, :], t2[:, :n]), reads=[bbones, bt2], writes=[bps])
                o, bo = O_()
                P.op("dve", TT(o[:, :n], ps[:, :n], v_, ALU.mult), reads=[bps, bzm], writes=[bo])
                K.stq("RW_BONUS", dr["RW_BONUS"][rows, t0:t0 + n], o[:, :n], bo)
    P.barrier()


class ChunkEnv:
    def __init__(self, K, st):
        self.K = K
        P, dr = K.P, K.dram
        self.ident, self.bident = K.sb(st, [128, 128], F32, "ident")
        self.masks, self.bmasks = K.sb(st, [64, 4, 512], F32, "masks")
        self.negm, self.bnegm = K.sb(st, [64, 4, 512], F32, "negm")
        self.rmask, self.brmask = K.sb(st, [64, 512], F32, "rmask")
        K.ld(self.ident[:], self.bident, "c_ident", dr["c_ident"])
        K.ld(self.masks[:], self.bmasks, "c_masks", dr["c_masks"])
        K.ld(self.negm[:], self.bnegm, "c_negm", dr["c_negm"])
        K.ld(self.rmask[:], self.brmask, "c_rmask", dr["c_rmask"])
        self.bd, self.bbd = K.sb(st, [64, 2, 512], F32, "bdmask")
        K.ld(self.bd[:], self.bbd, "c_bd", dr["c_bd"])
        self.pss = [K.ps(st, (64, 512)) for _ in range(8)]
        self.pi = 0
        self.slot = 0
        self.nslots = 1
        self.pis = [0, 0]
        self.ei = 0
        self.pool = {}
        self.st = st

    def PS(self):
        if self.nslots == 1:
            self.pi += 1
            return self.pss[self.pi % 8]
        self.pis[self.slot] += 1
        return self.pss[self.slot * 4 + self.pis[self.slot] % 4]

    def tile(self, name, shape, nbuf=2):
        if self.nslots > 1:
            name = "s%d_%s" % (self.slot, name)
            nbuf = 1
        ent = self.pool.get(name)
        if ent is None:
            ent = [[self.K.sb(self.st, shape, F32, name) for _ in range(nbuf)], 0]
            self.pool[name] = ent
        ent[1] += 1
        return ent[0][ent[1] % nbuf]

    def copy(self, out, bo, in_, bi, extra_reads=()):
        self.ei += 1
        if self.ei % 2:
            self.K.P.op("dve", CP(out, in_), reads=[bi, *extra_reads], writes=[bo])
        else:
            self.K.P.op("act", ACT(out, in_, AF.Copy), reads=[bi, *extra_reads], writes=[bo])
        self.K.P.mark()


def v3(t, nb):
    return t[:, 0:nb * 64].rearrange("p (c t) -> p c t", t=64)


def chunk_template(E, nb, order, ops, H, bH, Yout, bY):
    K = E.K
    P = K.P
    n = nb * 64
    L, bL = ops["L"]
    LT, bLT = ops["LT"]
    MqT, bMq = ops["MqT"]
    X, bX = ops["X0"]
    rF, brF = ops["rF"]
    btT, bbt = ops["btT"]
    decC, bdec = ops["decC"]
    extra = "MrkT" in ops

    Ld, bLd = E.tile("Ld", [64, 512])
    LdT, bLdT = E.tile("LdT", [64, 512])
    LoT, bLoT = E.tile("LoT", [64, 512])
    P.op("dve", TT(Ld[:, :n], L[:, :n], E.bd[:, 0, :n], ALU.mult), reads=[bL, E.bbd], writes=[bLd])
    P.op("dve", TT(LdT[:, :n], LT[:, :n], E.bd[:, 0, :n], ALU.mult), reads=[bLT, E.bbd], writes=[bLdT])
    P.op("dve", TT(LoT[:, :n], LT[:, :n], E.bd[:, 1, :n], ALU.mult), reads=[bLT, E.bbd], writes=[bLoT])
    X0k, bX0k = E.tile("X0k", [64, 8, 128])
    E.copy(X0k[:, 0:nb, :], bX0k, X[:, 0:nb, :], bX)

    def xupdate(PT, bPT, base=None):
        for half in range(0, nb, 4):
            ps, bps = E.PS()
            hc = min(4, nb - half)
            for c in range(half, half + hc):
                P.op("pe", MM(ps[:, (c - half) * 128:(c - half + 1) * 128], PT[:, c * 64:(c + 1) * 64], X[:, c, :]),
                     reads=[bPT, bX], writes=[bps])
            src, bsrc = (X, bX) if base is None else base
            P.op("dve", TT(X[:, half:half + hc, :], src[:, half:half + hc, :],
                           ps[:, 0:hc * 128].rearrange("p (c t) -> p c t", t=128), ALU.add),
                 reads=[bps, bX, bsrc], writes=[bX])
            P.mark()

    pows = [(LdT, bLdT)]
    Pc, bPc, PTc, bPTc = Ld, bLd, LdT, bLdT
    for j in range(1, 4):
        ps, bps = E.PS()
        for c in range(nb):
            cs = slice(c * 64, (c + 1) * 64)
            P.op("pe", MM(ps[:, cs], Pc[:, cs], PTc[:, cs]), reads=[bPc, bPTc], writes=[bps])
        PTn, bPTn = E.tile("PT%d" % j, [64, 512])
        E.copy(PTn[:, :n], bPTn, ps[:, :n], bps)
        if j < 3:
            ps, bps = E.PS()
            for c in range(nb):
                cs = slice(c * 64, (c + 1) * 64)
                P.op("pe", MM(ps[:, cs], PTc[:, cs], Pc[:, cs]), reads=[bPc, bPTc], writes=[bps])
            Pn, bPn = E.tile("Pp%d" % j, [64, 512])
            E.copy(Pn[:, :n], bPn, ps[:, :n], bps)
        else:
            Pn, bPn = None, None
        Pc, bPc, PTc, bPTc = Pn, bPn, PTn, bPTn
        pows.append((PTn, bPTn))
    for sweep in range(4):
        if sweep:
            xupdate(LoT, bLoT, base=(X0k, bX0k))
        for (PTj, bPTj) in pows:
            xupdate(PTj, bPTj)
    QeT, bQe = E.tile("QeT", [64, 512])
    ps, bps = E.PS()
    for c in range(nb):
        cs = slice(c * 64, (c + 1) * 64)
        P.op("pe", MM(ps[:, cs], X[:, c, 0:64], MqT[:, cs]), reads=[bX, bMq], writes=[bps])
    P.op("dve", TT(QeT[:, :n], ps[:, :n], rF[:, :n], ALU.add), reads=[bps, brF], writes=[bQe])
    P.mark()
    YcT, bYc = E.tile("YcT", [64, 512])
    ps, bps = E.PS()
    for c in range(nb):
        cs = slice(c * 64, (c + 1) * 64)
        P.op("pe", MM(ps[:, cs], X[:, c, 64:128], MqT[:, cs], True, not extra), reads=[bX, bMq], writes=[bps])
        if extra:
            P.op("pe", MM(ps[:, cs], ops["vT"][0][:, cs], ops["MrkT"][0][:, cs], False, True),
                 reads=[ops["vT"][1], ops["MrkT"][1]], writes=[bps])
    E.copy(YcT[:, :n], bYc, ps[:, :n], bps)
    AeT, bAe = E.tile("AeT", [64, 512])
    ps, bps = E.PS()
    for c in range(nb):
        cs = slice(c * 64, (c + 1) * 64)
        P.op("pe", MM(ps[:, cs], X[:, c, 0:64], btT[:, cs]), reads=[bX, bbt], writes=[bps])
    dg, bdg = E.tile("dg", [64, 512])
    P.op("dve", TT(v3(dg, nb), E.ident[0:64, 0:64].unsqueeze(1).broadcast_to([64, nb, 64]),
                   decC[:, 0:nb].unsqueeze(2).broadcast_to([64, nb, 64]), ALU.mult),
         reads=[E.bident, bdec], writes=[bdg])
    P.op("dve", TT(AeT[:, :n], ps[:, :n], dg[:, :n], ALU.add), reads=[bps, bdg], writes=[bAe])
    P.mark()
    Hc, bHc = E.tile("Hc", [64, 512])
    ps, bps = E.PS()
    for c in range(nb):
        cs = slice(c * 64, (c + 1) * 64)
        P.op("pe", MM(ps[:, cs], btT[:, cs], X[:, c, 64:128], True, not extra), reads=[bX, bbt], writes=[bps])
        if extra:
            P.op("pe", MM(ps[:, cs], ops["ktT"][0][:, cs], ops["vT"][0][:, cs], False, True),
                 reads=[ops["ktT"][1], ops["vT"][1]], writes=[bps])
    E.copy(Hc[:, :n], bHc, ps[:, :n], bps)
    for c in order:
        cs = slice(c * 64, (c + 1) * 64)
        psy, bpsy = E.PS()
        P.op("pe", MM(psy[:, 0:64], H[:, :], QeT[:, cs]), reads=[bH, bQe], writes=[bpsy])
        psh, bpsh = E.PS()
        P.op("pe", MM(psh[:, 0:64], AeT[:, cs], H[:, :]), reads=[bH, bAe], writes=[bpsh])
        P.op("dve", TT(Yout[:, cs], psy[:, 0:64], YcT[:, cs], ALU.add), reads=[bpsy, bYc], writes=[bY])
        P.op("dve", TT(H[:, :], psh[:, 0:64], Hc[:, cs], ALU.add), reads=[bpsh, bHc], writes=[bH])
        P.mark()


def transpose_batch(E, nb, src, bsrc, dst3, bdst):
    P = E.K.P
    ps, bps = E.PS()
    for c in range(nb):
        cs = slice(c * 64, (c + 1) * 64)
        P.op("pe", MM(ps[:, cs], src[:, cs], E.ident[0:64, 0:64]), reads=[bsrc, E.bident], writes=[bps])
    E.copy(dst3, bdst, ps[:, 0:nb * 64].rearrange("p (c t) -> p c t", t=64), bps)


def stream_batches(cfg, d):
    C, T = cfg["C"], cfg["T"]
    out = []
    for (s0, s1) in ((0, C), (C, C + T)):
        bs = []
        t = s0
        while t < s1:
            nb = min(8, (s1 - t) // 64)
            bs.append((t, nb))
            t += nb * 64
        if d == 1:
            bs = bs[::-1]
        for (t0, nb) in bs:
            out.append((t0, nb, list(range(nb)) if d == 0 else list(range(nb - 1, -1, -1))))
    return out


def rw_frontend(E, l, h, d, t0, nb):
    K = E.K
    P, dr = K.P, K.dram
    n = nb * 64
    rows = slice(h * 64, (h + 1) * 64)
    ld = {}
    for nm, src in (("r", dr["RW_R"][rows, t0:t0 + n]), ("v", dr["RW_V"][rows, t0:t0 + n]),
                    ("kk", dr["RW_KK"][rows, t0:t0 + n]), ("kd", dr["RW_KD"][d, rows, t0:t0 + n]),
                    ("as", dr["RW_AS"][d, rows, t0:t0 + n]), ("lam", dr["RW_LAM"][d, rows, t0:t0 + n])):
        t, b = E.tile("in_" + nm, [64, 512])
        K.ld(t[:, :n], b, "RW_R" if nm == "r" else {"v": "RW_V", "kk": "RW_KK", "kd": "RW_KD", "as": "RW_AS", "lam": "RW_LAM"}[nm], src)
        ld[nm] = (t, b)
    lam, blam = ld["lam"]
    Lm, bLm = E.tile("Lam", [64, 512])
    if d == 0:
        P.op("dve", lambda e, Lm=Lm, lam=lam, n=n: e.tensor_tensor_scan(
            Lm[:, :n], E.rmask[:, :n], lam[:, :n], 0.0, ALU.mult, ALU.add), reads=[blam, E.brmask], writes=[bLm])
        last = 63
    else:
        pre, bpre = E.tile("pre", [64, 512])
        P.op("dve", lambda e, pre=pre, lam=lam, n=n: e.tensor_tensor_scan(
            pre[:, :n], E.rmask[:, :n], lam[:, :n], 0.0, ALU.mult, ALU.add), reads=[blam, E.brmask], writes=[bpre])
        P.op("dve", TT(v3(Lm, nb), v3(pre, nb)[:, :, 63:64].broadcast_to([64, nb, 64]), v3(pre, nb), ALU.subtract),
             reads=[bpre], writes=[bLm])
        P.op("dve", TT(Lm[:, :n], Lm[:, :n], lam[:, :n], ALU.add), reads=[bLm, blam], writes=[bLm])
        last = 0
    Lp, bLp = E.tile("Lp", [64, 512])
    P.op("dve", TT(Lp[:, :n], Lm[:, :n], lam[:, :n], ALU.subtract), reads=[bLm, blam], writes=[bLp])
    Ep, bEp = E.tile("Ep", [64, 512])
    Em, bEm = E.tile("Em", [64, 512])
    Epr, bEpr = E.tile("Epr", [64, 512])
    P.op("act", ACT(Ep[:, :n], Lm[:, :n], AF.Exp), reads=[bLm], writes=[bEp])
    P.op("act", ACT(Em[:, :n], Lm[:, :n], AF.Exp, scale=-1.0), reads=[bLm], writes=[bEm])
    P.op("act", ACT(Epr[:, :n], Lp[:, :n], AF.Exp), reads=[bLp], writes=[bEpr])
    decC, bdec = E.tile("decC", [64, 8])
    P.op("dve", CP(decC[:, 0:nb], v3(Ep, nb)[:, :, last]), reads=[bEp], writes=[bdec])
    decb = decC[:, 0:nb].unsqueeze(2).broadcast_to([64, nb, 64])
    kk, bkk = ld["kk"]
    aF, baF = E.tile("aF", [64, 512])
    P.op("dve", STT(aF[:, :n], kk[:, :n], -1.0, Epr[:, :n], ALU.mult, ALU.mult), reads=[bkk, bEpr], writes=[baF])
    bF, bbF = E.tile("bF", [64, 512])
    P.op("dve", TT(bF[:, :n], kk[:, :n], ld["as"][0][:, :n], ALU.mult), reads=[bkk, ld["as"][1]], writes=[bbF])
    P.op("dve", TT(bF[:, :n], bF[:, :n], Em[:, :n], ALU.mult), reads=[bbF, bEm], writes=[bbF])
    kF, bkF = E.tile("kF", [64, 512])
    P.op("dve", TT(kF[:, :n], ld["kd"][0][:, :n], Em[:, :n], ALU.mult), reads=[ld["kd"][1], bEm], writes=[bkF])
    rF, brF = E.tile("rF", [64, 512])
    P.op("dve", TT(rF[:, :n], ld["r"][0][:, :n], Ep[:, :n], ALU.mult), reads=[ld["r"][1], bEp], writes=[brF])
    btF, bbtF = E.tile("btF", [64, 512])
    P.op("dve", TT(v3(btF, nb), v3(bF, nb), decb, ALU.mult), reads=[bbF, bdec], writes=[bbtF])
    ktF, bktF = E.tile("ktF", [64, 512])
    P.op("dve", TT(v3(ktF, nb), v3(kF, nb), decb, ALU.mult), reads=[bkF, bdec], writes=[bktF])
    X0, bX0 = E.tile("X0", [64, 8, 128])
    transpose_batch(E, nb, aF, baF, X0[:, 0:nb, 0:64], bX0)
    btT, bbtT = E.tile("btT", [64, 512])
    transpose_batch(E, nb, btF, bbtF, v3(btT, nb), bbtT)
    ktT, bktT = E.tile("ktT", [64, 512])
    transpose_batch(E, nb, ktF, bktF, v3(ktT, nb), bktT)
    vT, bvT = E.tile("vT", [64, 512])
    transpose_batch(E, nb, ld["v"][0], ld["v"][1], v3(vT, nb), bvT)
    mL, mLT, mM = (0, 1, 3) if d == 0 else (1, 0, 2)

    def pair(name, lt, blt, rt, brt, mi):
        ps, bps = E.PS()
        for c in range(nb):
            cs = slice(c * 64, (c + 1) * 64)
            P.op("pe", MM(ps[:, cs], lt[:, cs], rt[:, cs]), reads=[blt, brt], writes=[bps])
        o, bo = E.tile(name, [64, 512])
        P.op("dve", TT(o[:, :n], ps[:, :n], E.masks[:, mi, :n], ALU.mult), reads=[bps, E.bmasks], writes=[bo])
        P.mark()
        return o, bo

    L_ = pair("L", aF, baF, bF, bbF, mL)
    LT_ = pair("LT", bF, bbF, aF, baF, mLT)
    LakT = pair("LakT", kF, bkF, aF, baF, mLT)
    MqT = pair("MqT", bF, bbF, rF, brF, mM)
    MrkT = pair("MrkT", kF, bkF, rF, brF, mM)
    ps, bps = E.PS()
    for c in range(nb):
        cs = slice(c * 64, (c + 1) * 64)
        P.op("pe", MM(ps[:, cs], LakT[0][:, cs], vT[:, cs]), reads=[LakT[1], bvT], writes=[bps])
    E.copy(X0[:, 0:nb, 64:128], bX0, ps[:, 0:n].rearrange("p (c t) -> p c t", t=64), bps)
    return {"L": L_, "LT": LT_, "MqT": MqT, "X0": (X0, bX0), "rF": (rF, brF), "btT": (btT, bbtT),
            "decC": (decC, bdec), "MrkT": MrkT, "ktT": (ktT, bktT), "vT": (vT, bvT)}


def stage_rw_scan(K, l):
    P, cfg, dr = K.P, K.cfg, K.dram
    with contextlib.ExitStack() as st:
        E = ChunkEnv(K, st)
        for h in range(6):
            for d in range(2):
                H, bH = E.tile("H", [64, 64])
                P.op("dve", MS(H[:, :], 0.0), writes=[bH])
                for (t0, nb, order) in stream_batches(cfg, d):
                    ops = rw_frontend(E, l, h, d, t0, nb)
                    Y, bY = E.tile("Y", [64, 512])
                    chunk_template(E, nb, order, ops, H, bH, Y, bY)
                    K.stq("YRW", dr["YRW"][d, h * 64:(h + 1) * 64, t0:t0 + nb * 64], Y[:, :nb * 64], bY)
    P.barrier()


def stage_gd_pre(K, l):
    P, cfg, dr = K.P, K.cfg, K.dram
    with contextlib.ExitStack() as st:
        cw, bcw = K.sb(st, [128, 3, 9], F32, "convw")
        bones, bbones = K.sb(st, [128, 128], F32, "bones")
        sel, bsel = K.sb(st, [12, 2, 3, 128], F32, "sel")
        prm, bprm = K.sb(st, [12, 4], F32, "gdprm")
        K.ld(cw[:], bcw, "gd_conv", dr["gd_conv"][:, l])
        K.ld(bones[:], bbones, "c_bones", dr["c_bones"])
        K.ld(sel[:], bsel, "c_sel", dr["c_sel"])
        K.ld(prm[:, 0:2], bprm, "gd_prm", dr["gd_prm"][:, l, :])
        P.op("act", ACT(prm[:, 2:3], prm[:, 0:1], AF.Exp), reads=[bprm], writes=[bprm])
        P.op("dve", TS(prm[:, 2:3], prm[:, 2:3], -1.0, ALU.mult), reads=[bprm], writes=[bprm])
        zts = [K.sb(st, [128, 9, 514], F32, "zt") for _ in range(2)]
        tmps = [K.sb(st, [128, 512], F32, "tmp") for _ in range(6)]
        outs = [K.sb(st, [128, 512], F32, "out") for _ in range(6)]
        bas = [K.sb(st, [12, 2, 512], F32, "ba") for _ in range(2)]
        pss = [K.ps(st) for _ in range(4)]
        ctr = {"t": 0, "o": 0, "p": 0}

        def T_():
            ctr["t"] += 1
            return tmps[ctr["t"] % 6]

        def O_():
            ctr["o"] += 1
            return outs[ctr["o"] % 6]

        def PS_():
            ctr["p"] += 1
            return pss[ctr["p"] % 4]

        Zr = dr["Z"][Z_GDQ:Z_GDQ + 1152, :].rearrange("(k p) t -> p k t", p=128)
        for ti, (t0, n, m) in enumerate(token_tiles(cfg)):
            zt, bzt = zts[ti % 2]
            load_halo(K, zt, bzt, "Z", Zr, t0, n, cfg)
            ba, bba = bas[ti % 2]
            K.ld(ba[:, 0, :n], bba, "Z", dr["Z"][Z_GDBA:Z_GDBA + 12, t0:t0 + n])
            K.ld(ba[:, 1, :n], bba, "Z", dr["Z"][Z_GDBA + 12:Z_GDBA + 24, t0:t0 + n])
            P.op("act", ACT(ba[:, 0, :n], ba[:, 0, :n], AF.Sigmoid), reads=[bba], writes=[bba])
            P.op("act", ACT(ba[:, 1, :n], ba[:, 1, :n], AF.Exp, bias=prm[:, 1:2], scale=1.0), reads=[bba, bprm], writes=[bba])
            P.op("act", ACT(ba[:, 1, :n], ba[:, 1, :n], AF.Ln, bias=1.0, scale=1.0), reads=[bba], writes=[bba])
            P.op("dve", TS(ba[:, 1, :n], ba[:, 1, :n], prm[:, 2:3], ALU.mult), reads=[bba, bprm], writes=[bba])
            for d in range(2):
                for c3 in range(3):
                    rows = slice(c3 * 128, (c3 + 1) * 128)
                    for which, nm in ((0, "GD_BETA"), (1, "GD_G")):
                        ps, bps = PS_()
                        P.op("pe", MM(ps[:, :n], sel[:, d, c3, :], ba[:, which, :n]), reads=[bsel, bba], writes=[bps])
                        o, bo = O_()
                        P.op("act", ACT(o[:, :n], ps[:, :n], AF.Copy), reads=[bps], writes=[bo])
                        K.stq(nm, dr[nm][d, rows, t0:t0 + n], o[:, :n], bo)
            for k in range(9):
                c3 = k % 3
                rows = slice(c3 * 128, (c3 + 1) * 128)
                t1, bt1 = T_()
                P.op("dve", TS(t1[:, :n], zt[:, k, 0:n], cw[:, 0, k:k + 1], ALU.mult), reads=[bzt, bcw], writes=[bt1])
                P.op("dve", STT(t1[:, :n], zt[:, k, 1:n + 1], cw[:, 1, k:k + 1], t1[:, :n], ALU.mult, ALU.add),
                     reads=[bzt, bcw, bt1], writes=[bt1])
                P.op("dve", STT(t1[:, :n], zt[:, k, 2:n + 2], cw[:, 2, k:k + 1], t1[:, :n], ALU.mult, ALU.add),
                     reads=[bzt, bcw, bt1], writes=[bt1])
                if k >= 6:
                    o, bo = O_()
                    P.op("act", ACT(o[:, :n], t1[:, :n], AF.Silu), reads=[bt1], writes=[bo])
                    K.stq("GD_V", dr["GD_V"][rows, t0:t0 + n], o[:, :n], bo)
                    continue
                t2, bt2 = T_()
                P.op("act", ACT(t2[:, :n], t1[:, :n], AF.Silu), reads=[bt1], writes=[bt2])
                t3, bt3 = T_()
                P.op("act", ACT(t3[:, :n], t2[:, :n], AF.Square), reads=[bt2], writes=[bt3])
                ps, bps = PS_()
                P.op("pe", MM(ps[:, :n], bones[:, :], t3[:, :n]), reads=[bbones, bt3], writes=[bps])
                P.op("dve", TS(t3[:, :n], ps[:, :n], NORM_EPS, ALU.add), reads=[bps], writes=[bt3])
                P.op("act", ACT(t3[:, :n], t3[:, :n], AF.Ln), reads=[bt3], writes=[bt3]); P.op("act", ACT(t3[:, :n], t3[:, :n], AF.Exp, scale=-0.5), reads=[bt3], writes=[bt3])
                o, bo = O_()
                if k < 3:
                    P.op("dve", STT(o[:, :n], t2[:, :n], 0.125, t3[:, :n], ALU.mult, ALU.mult), reads=[bt2, bt3], writes=[bo])
                    K.stq("GD_Q", dr["GD_Q"][rows, t0:t0 + n], o[:, :n], bo)
                else:
                    P.op("dve", TT(o[:, :n], t2[:, :n], t3[:, :n], ALU.mult), reads=[bt2, bt3], writes=[bo])
                    K.stq("GD_K", dr["GD_K"][rows, t0:t0 + n], o[:, :n], bo)
    P.barrier()


def cumsum_chunks(E, d, nb, lam, blam):
    P = E.K.P
    n = nb * 64
    Lm, bLm = E.tile("Lam", [64, 512])
    if d == 0:
        P.op("dve", lambda e: e.tensor_tensor_scan(Lm[:, :n], E.rmask[:, :n], lam[:, :n], 0.0, ALU.mult, ALU.add),
             reads=[blam, E.brmask], writes=[bLm])
        return Lm, bLm, 63
    pre, bpre = E.tile("pre", [64, 512])
    P.op("dve", lambda e: e.tensor_tensor_scan(pre[:, :n], E.rmask[:, :n], lam[:, :n], 0.0, ALU.mult, ALU.add),
         reads=[blam, E.brmask], writes=[bpre])
    P.op("dve", TT(v3(Lm, nb), v3(pre, nb)[:, :, 63:64].broadcast_to([64, nb, 64]), v3(pre, nb), ALU.subtract),
         reads=[bpre], writes=[bLm])
    P.op("dve", TT(Lm[:, :n], Lm[:, :n], lam[:, :n], ALU.add), reads=[bLm, blam], writes=[bLm])
    return Lm, bLm, 0


def gd_frontend(E, l, h, d, t0, nb):
    K = E.K
    P, dr = K.P, K.dram
    n = nb * 64
    rows = slice(h * 64, (h + 1) * 64)
    ld = {}
    for nm, dn, src in (("q", "GD_Q", dr["GD_Q"][rows, t0:t0 + n]), ("k", "GD_K", dr["GD_K"][rows, t0:t0 + n]),
                        ("v", "GD_V", dr["GD_V"][rows, t0:t0 + n]), ("be", "GD_BETA", dr["GD_BETA"][d, rows, t0:t0 + n]),
                        ("g", "GD_G", dr["GD_G"][d, rows, t0:t0 + n])):
        t, b = E.tile("in_" + nm, [64, 512])
        K.ld(t[:, :n], b, dn, src)
        ld[nm] = (t, b)
    qF, bqF = ld["q"]
    kF, bkF = ld["k"]
    be, bbe = ld["be"]
    Lm, bLm, last = cumsum_chunks(E, d, nb, ld["g"][0], ld["g"][1])
    Ep, bEp = E.tile("Ep", [64, 512])
    P.op("act", ACT(Ep[:, :n], Lm[:, :n], AF.Exp), reads=[bLm], writes=[bEp])
    decC, bdec = E.tile("decC", [64, 8])
    P.op("dve", CP(decC[:, 0:nb], v3(Ep, nb)[:, :, last]), reads=[bEp], writes=[bdec])
    rF, brF = E.tile("rF", [64, 512])
    P.op("dve", TT(rF[:, :n], qF[:, :n], Ep[:, :n], ALU.mult), reads=[bqF, bEp], writes=[brF])
    Et, bEt = E.tile("Et", [64, 512])
    P.op("dve", TT(v3(Et, nb), v3(Lm, nb)[:, :, last:last + 1].broadcast_to([64, nb, 64]), v3(Lm, nb), ALU.subtract),
         reads=[bLm], writes=[bEt])
    P.op("act", ACT(Et[:, :n], Et[:, :n], AF.Exp), reads=[bEt], writes=[bEt])
    btF, bbtF = E.tile("btF", [64, 512])
    P.op("dve", TT(btF[:, :n], kF[:, :n], Et[:, :n], ALU.mult), reads=[bkF, bEt], writes=[bbtF])
    btT, bbtT = E.tile("btT", [64, 512])
    transpose_batch(E, nb, btF, bbtF, v3(btT, nb), bbtT)
    kT, bkT = E.tile("ktT", [64, 512])
    transpose_batch(E, nb, kF, bkF, v3(kT, nb), bkT)
    vT, bvT = E.tile("vT", [64, 512])
    transpose_batch(E, nb, ld["v"][0], ld["v"][1], v3(vT, nb), bvT)
    GT, bGT = E.tile("GT", [64, 512])
    transpose_batch(E, nb, Lm, bLm, v3(GT, nb), bGT)
    bT, bbT = E.tile("bT", [64, 512])
    transpose_batch(E, nb, be, bbe, v3(bT, nb), bbT)
    X0, bX0 = E.tile("X0", [64, 8, 128])
    eg, beg = E.tile("eg", [64, 512])
    P.op("act", ACT(eg[:, :n], GT[:, :n], AF.Exp), reads=[bGT], writes=[beg])
    P.op("dve", TT(eg[:, :n], eg[:, :n], bT[:, :n], ALU.mult), reads=[beg, bbT], writes=[beg])
    P.op("dve", STT(X0[:, 0:nb, 0:64], v3(kT, nb), -1.0, v3(eg, nb), ALU.mult, ALU.mult), reads=[bkT, beg], writes=[bX0])
    P.op("dve", TT(X0[:, 0:nb, 64:128], v3(vT, nb), v3(bT, nb), ALU.mult), reads=[bvT, bbT], writes=[bX0])
    mL, mLT, mM = (0, 1, 3) if d == 0 else (1, 0, 2)
    Dx, bDx = E.tile("Dx", [64, 512])
    P.op("dve", TT(Dx[:, :n], GT[:, :n], Lm[:, :n], ALU.subtract), reads=[bGT, bLm], writes=[bDx])
    D1, bD1 = E.tile("D1", [64, 512])
    P.op("dve", TT(D1[:, :n], Dx[:, :n], E.negm[:, mL, :n], ALU.min), reads=[bDx, E.bnegm], writes=[bD1])
    P.op("act", ACT(D1[:, :n], D1[:, :n], AF.Exp), reads=[bD1], writes=[bD1])
    P.op("dve", TT(D1[:, :n], D1[:, :n], bT[:, :n], ALU.mult), reads=[bD1, bbT], writes=[bD1])
    D2, bD2 = E.tile("D2", [64, 512])
    P.op("dve", STT(D2[:, :n], Dx[:, :n], -1.0, E.negm[:, mLT, :n], ALU.mult, ALU.min), reads=[bDx, E.bnegm], writes=[bD2])
    P.op("act", ACT(D2[:, :n], D2[:, :n], AF.Exp), reads=[bD2], writes=[bD2])
    P.op("dve", TT(D2[:, :n], D2[:, :n], be[:, :n], ALU.mult), reads=[bD2, bbe], writes=[bD2])
    D3, bD3 = E.tile("D3", [64, 512])
    P.op("dve", STT(D3[:, :n], Dx[:, :n], -1.0, E.negm[:, mM, :n], ALU.mult, ALU.min), reads=[bDx, E.bnegm], writes=[bD3])
    P.op("act", ACT(D3[:, :n], D3[:, :n], AF.Exp), reads=[bD3], writes=[bD3])
    ps, bps = E.PS()
    for c in range(nb):
        cs = slice(c * 64, (c + 1) * 64)
        P.op("pe", MM(ps[:, cs], kF[:, cs], kF[:, cs]), reads=[bkF], writes=[bps])
    L_, bL_ = E.tile("L", [64, 512])
    P.op("dve", STT(L_[:, :n], ps[:, :n], -1.0, D1[:, :n], ALU.mult, ALU.mult), reads=[bps, bD1], writes=[bL_])
    LT_, bLT_ = E.tile("LT", [64, 512])
    P.op("dve", STT(LT_[:, :n], ps[:, :n], -1.0, D2[:, :n], ALU.mult, ALU.mult), reads=[bps, bD2], writes=[bLT_])
    P.mark()
    ps, bps = E.PS()
    for c in range(nb):
        cs = slice(c * 64, (c + 1) * 64)
        P.op("pe", MM(ps[:, cs], kF[:, cs], qF[:, cs]), reads=[bkF, bqF], writes=[bps])
    Mq, bMq = E.tile("MqT", [64, 512])
    P.op("dve", TT(Mq[:, :n], ps[:, :n], D3[:, :n], ALU.mult), reads=[bps, bD3], writes=[bMq])
    P.mark()
    return {"L": (L_, bL_), "LT": (LT_, bLT_), "MqT": (Mq, bMq), "X0": (X0, bX0), "rF": (rF, brF),
            "btT": (btT, bbtT), "decC": (decC, bdec)}


def stage_scan(K, l, which):
    cfg, dr = K.cfg, K.dram
    fe = rw_frontend if which == "rw" else gd_frontend
    yn = "YRW" if which == "rw" else "YGD"
    realP = K.P
    with contextlib.ExitStack() as st:
        E = ChunkEnv(K, st)
        E.nslots = 2
        for h in range(6):
            recs = []
            for d in range(2):
                E.slot = d
                K.P = Rec()
                P = K.P
                H, bH = E.tile("H", [64, 64])
                P.op("dve", MS(H[:, :], 0.0), writes=[bH])
                for (t0, nb, order) in stream_batches(cfg, d):
                    ops = fe(E, l, h, d, t0, nb)
                    Y, bY = E.tile("Y", [64, 512])
                    chunk_template(E, nb, order, ops, H, bH, Y, bY)
                    K.stq(yn, dr[yn][d, h * 64:(h + 1) * 64, t0:t0 + nb * 64], Y[:, :nb * 64], bY)
                    P.mark()
                recs.append(K.P)
            K.P = realP
            replay_interleaved(realP, recs)
    K.P = realP
    realP.barrier()


def s5_pieces(cfg):
    T = cfg["T"]
    npc = cfg.get("s5_pieces", max(1, T // 4096))
    wpp = 64 // npc
    return [(i * wpp, wpp) for i in range(npc)]


def stage_s5(K, l):
    P, cfg, dr = K.P, K.cfg, K.dram
    C, T = cfg["C"], cfg["T"]
    R = T // 64
    PI = math.pi
    with contextlib.ExitStack() as st:
        ident, bident = K.sb(st, [128, 128], F32, "ident")
        K.ld(ident[:], bident, "c_ident", dr["c_ident"])
        prm, bprm = K.sb(st, [128, 3, 16], F32, "s5prm")
        K.ld(prm[:], bprm, "s5_prm", dr["s5_prm"][:, l])
        Bp, bBp = K.sb(st, [128, 2, 16, 32], F32, "s5B")
        K.ld(Bp[:], bBp, "s5_B", dr["s5_B"][:, l])
        Cp, bCp = K.sb(st, [128, 2, 16, 32], F32, "s5C")
        K.ld(Cp[:], bCp, "s5_C", dr["s5_C"][:, l])
        P.op("dve", TS(Cp[:, 1], Cp[:, 1], -1.0, ALU.mult), reads=[bCp], writes=[bCp])
        w, bw = K.sb(st, [128, 12, 16], F32, "s5w")
        hpi, bhpi = K.sb(st, [128, 1], F32, "hpi")
        P.op("dve", MS(hpi[:], PI / 2), writes=[bhpi])
        lr, li, ldt = prm[:, 0, :], prm[:, 1, :], prm[:, 2, :]
        dt, xr, xi, mag, sn, cs_, abr, abi, den, fr, fi, t0_ = [w[:, i, :] for i in range(12)]

        def dv(fn):
            P.op("dve", fn, reads=[bw, bprm], writes=[bw])

        def ac(fn):
            P.op("act", fn, reads=[bw, bprm, bhpi], writes=[bw])

        ac(ACT(dt, ldt, AF.Exp))
        dv(TT(xr, lr, dt, ALU.mult))
        dv(TT(xi, li, dt, ALU.mult))
        ac(ACT(mag, xr, AF.Exp))
        ac(ACT(sn, xi, AF.Sin, scale=1.0 / 16))
        ac(ACT(cs_, xi, AF.Sin, bias=hpi[:, 0:1], scale=1.0 / 16))
        for _ in range(4):
            dv(TT(t0_, cs_, sn, ALU.mult))
            dv(TT(cs_, cs_, cs_, ALU.mult))
            dv(TT(sn, sn, sn, ALU.mult))
            dv(TT(cs_, cs_, sn, ALU.subtract))
            dv(TS(sn, t0_, 2.0, ALU.mult))
        dv(TT(abr, mag, cs_, ALU.mult))
        dv(TT(abi, mag, sn, ALU.mult))
        dv(TT(den, lr, lr, ALU.mult))
        dv(TT(t0_, li, li, ALU.mult))
        dv(TT(den, den, t0_, ALU.add))
        P.op("dve", lambda e: e.reciprocal(den, den), reads=[bw], writes=[bw])
        dv(TS(t0_, abr, -1.0, ALU.add))
        dv(TT(fr, t0_, lr, ALU.mult))
        dv(TT(xr, abi, li, ALU.mult))
        dv(TT(fr, fr, xr, ALU.add))
        dv(TT(fr, fr, den, ALU.mult))
        dv(TT(fi, abi, lr, ALU.mult))
        dv(TT(xr, t0_, li, ALU.mult))
        dv(TT(fi, fi, xr, ALU.subtract))
        dv(TT(fi, fi, den, ALU.mult))
        bb, bbb = K.sb(st, [128, 2, 16, 32], F32, "s5bb")
        tmpb, btmpb = K.sb(st, [128, 16, 32], F32, "s5tb")
        frb = w[:, 9, :].unsqueeze(2).broadcast_to([128, 16, 32])
        fib = w[:, 10, :].unsqueeze(2).broadcast_to([128, 16, 32])
        P.op("dve", TT(bb[:, 0], Bp[:, 0], frb, ALU.mult), reads=[bBp, bw], writes=[bbb])
        P.op("dve", TT(tmpb[:], Bp[:, 1], fib, ALU.mult), reads=[bBp, bw], writes=[btmpb])
        P.op("dve", TT(bb[:, 0], bb[:, 0], tmpb[:], ALU.subtract), reads=[bbb, btmpb], writes=[bbb])
        P.op("dve", TT(bb[:, 1], Bp[:, 1], frb, ALU.mult), reads=[bBp, bw], writes=[bbb])
        P.op("dve", TT(tmpb[:], Bp[:, 0], fib, ALU.mult), reads=[bBp, bw, btmpb], writes=[btmpb])
        P.op("dve", TT(bb[:, 1], bb[:, 1], tmpb[:], ALU.add), reads=[bbb, btmpb], writes=[bbb])
        bbT, bbbT = K.sb(st, [32, 2, 16, 128], F32, "s5bbT", multi=True)
        pst = [K.ps(st) for _ in range(2)]
        it = 0
        for ri in range(2):
            for q4 in range(4):
                ps, bps = pst[it % 2]
                it += 1
                for j in range(4):
                    P.op("pe", MM(ps[0:32, j * 128:(j + 1) * 128], bb[:, ri, q4 * 4 + j, :], ident[:, :]),
                         reads=[bbb, bident], writes=[bps])
                P.op("act", ACT(bbT[:, ri, q4 * 4:q4 * 4 + 4, :], ps[0:32, :].rearrange("p (j t) -> p j t", t=128), AF.Copy),
                     reads=[bps], writes=[bbbT])
        segs = [("ctx", 0, C)]
        pcs = s5_pieces(cfg)
        maxn = max(C, pcs[0][1] * R)
        NLV = max(1, (maxn - 1).bit_length())
        pw, bpw = K.sb(st, [128, 16, NLV + 1, 3], F32, "s5pw")
        P.op("dve", CP(pw[:, :, 0, 0], abr), reads=[bw], writes=[bpw])
        P.op("dve", CP(pw[:, :, 0, 1], abi), reads=[bw, bpw], writes=[bpw])
        sq1, bsq1 = K.sb(st, [128, 16, 2], F32, "s5sq")
        for lv in range(NLV):
            P.op("dve", TT(sq1[:, :, 0], pw[:, :, lv, 0], pw[:, :, lv, 0], ALU.mult), reads=[bpw], writes=[bsq1])
            P.op("dve", TT(sq1[:, :, 1], pw[:, :, lv, 1], pw[:, :, lv, 1], ALU.mult), reads=[bpw, bsq1], writes=[bsq1])
            P.op("dve", TT(pw[:, :, lv + 1, 0], sq1[:, :, 0], sq1[:, :, 1], ALU.subtract), reads=[bsq1, bpw], writes=[bpw])
            P.op("dve", STT(pw[:, :, lv + 1, 1], pw[:, :, lv, 0], 2.0, pw[:, :, lv, 1], ALU.mult, ALU.mult),
                 reads=[bpw], writes=[bpw])
        P.op("dve", TS(pw[:, :, :, 2], pw[:, :, :, 1], -1.0, ALU.mult), reads=[bpw], writes=[bpw])
        u, bu = K.sb(st, [32, C + T], F32, "s5u")
        yr, byr = K.sb(st, [32, C + T], F32, "s5yr", multi=True)
        NB = maxn
        bufs = [K.sb(st, [128, NB], F32, "s5h") for _ in range(4)]
        hp, bhp = K.sb(st, [128, 4], F32, "s5hp")
        psx = [K.ps(st) for _ in range(4)]
        pix = [0]

        def PSX():
            pix[0] += 1
            return psx[pix[0] % 4]

        ulat = u[:, C:C + T].rearrange("p (r w) -> p w r", w=64)
        yrlat = yr[:, C:C + T].rearrange("p (r w) -> p w r", w=64)
        for gp in range(8):
            K.ld(u[:, :], bu, "Z", dr["Z"][Z_S5 + gp * 32:Z_S5 + gp * 32 + 32, :])
            for d in range(2):
                q = d * 8 + gp
                seglist = [("ctx", 0, 0)] + [("lat", w0, wn) for (w0, wn) in (pcs if d == 0 else pcs[::-1])]
                first = True
                for (kind, w0, wn) in seglist:
                    n = C if kind == "ctx" else wn * R
                    (re_, bre), (im_, bim), (re2, bre2), (im2, bim2) = bufs
                    blk = 512 if kind == "ctx" else max(1, 512 // R) * R
                    c0 = 0
                    while c0 < n:
                        m = min(blk, n - c0)
                        if kind == "ctx":
                            rhs = u[:, c0:c0 + m]
                        else:
                            rhs = ulat[:, w0 + c0 // R:w0 + (c0 + m) // R, :]
                        for ri, (dst, bdst) in enumerate(((re_, bre), (im_, bim))):
                            ps, bps = PSX()
                            P.op("pe", MM(ps[:, :m], bbT[:, ri, q, :], rhs), reads=[bbbT, bu], writes=[bps])
                            if ri:
                                P.op("act", ACT(dst[:, c0:c0 + m], ps[:, :m], AF.Copy), reads=[bps], writes=[bdst])
                            else:
                                P.op("dve", CP(dst[:, c0:c0 + m], ps[:, :m]), reads=[bps], writes=[bdst])
                        c0 += m
                    e0 = 0 if d == 0 else n - 1
                    e1 = n - 1 if d == 0 else 0
                    if not first:
                        ar, ai, nai = pw[:, q, 0, 0:1], pw[:, q, 0, 1:2], pw[:, q, 0, 2:3]
                        P.op("dve", STT(re_[:, e0:e0 + 1], hp[:, 0:1], ar, re_[:, e0:e0 + 1], ALU.mult, ALU.add),
                             reads=[bhp, bpw, bre], writes=[bre])
                        P.op("dve", STT(re_[:, e0:e0 + 1], hp[:, 1:2], nai, re_[:, e0:e0 + 1], ALU.mult, ALU.add),
                             reads=[bhp, bpw, bre], writes=[bre])
                        P.op("dve", STT(im_[:, e0:e0 + 1], hp[:, 1:2], ar, im_[:, e0:e0 + 1], ALU.mult, ALU.add),
                             reads=[bhp, bpw, bim], writes=[bim])
                        P.op("dve", STT(im_[:, e0:e0 + 1], hp[:, 0:1], ai, im_[:, e0:e0 + 1], ALU.mult, ALU.add),
                             reads=[bhp, bpw, bim], writes=[bim])
                    first = False
                    cur = ((re_, bre), (im_, bim))
                    nxt = ((re2, bre2), (im2, bim2))
                    sft = 1
                    lv = 0
                    while sft < n:
                        (cr, bcr), (ci, bci) = cur
                        (nr, bnr), (ni, bni) = nxt
                        ar, ai, nai = pw[:, q, lv, 0:1], pw[:, q, lv, 1:2], pw[:, q, lv, 2:3]
                        if d == 0:
                            dst, src, keep = slice(sft, n), slice(0, n - sft), slice(0, sft)
                        else:
                            dst, src, keep = slice(0, n - sft), slice(sft, n), slice(n - sft, n)
                        P.op("dve", STT(nr[:, dst], cr[:, src], ar, cr[:, dst], ALU.mult, ALU.add), reads=[bcr, bpw], writes=[bnr])
                        P.op("dve", STT(nr[:, dst], ci[:, src], nai, nr[:, dst], ALU.mult, ALU.add), reads=[bci, bpw, bnr], writes=[bnr])
                        P.op("dve", CP(nr[:, keep], cr[:, keep]), reads=[bcr, bnr], writes=[bnr])
                        P.op("dve", STT(ni[:, dst], ci[:, src], ar, ci[:, dst], ALU.mult, ALU.add), reads=[bci, bpw], writes=[bni])
                        P.op("dve", STT(ni[:, dst], cr[:, src], ai, ni[:, dst], ALU.mult, ALU.add), reads=[bcr, bpw, bni], writes=[bni])
                        P.op("pool", CP(ni[:, keep], ci[:, keep]), reads=[bci, bni], writes=[bni])
                        cur, nxt = nxt, cur
                        sft *= 2
                        lv += 1
                    (cr, bcr), (ci, bci) = cur
                    P.op("dve", CP(hp[:, 0:1], cr[:, e1:e1 + 1]), reads=[bcr], writes=[bhp])
                    P.op("dve", CP(hp[:, 1:2], ci[:, e1:e1 + 1]), reads=[bci, bhp], writes=[bhp])
                    c0 = 0
                    while c0 < n:
                        m = min(blk, n - c0)
                        ps, bps = PSX()
                        P.op("pe", MM(ps[0:32, :m], Cp[:, 0, q, :], cr[:, c0:c0 + m], True, False), reads=[bCp, bcr], writes=[bps])
                        P.op("pe", MM(ps[0:32, :m], Cp[:, 1, q, :], ci[:, c0:c0 + m], False, True), reads=[bCp, bci], writes=[bps])
                        if kind == "ctx":
                            P.op("act", ACT(yr[:, c0:c0 + m], ps[0:32, :m], AF.Copy), reads=[bps], writes=[byr])
                        else:
                            P.op("act", ACT(yrlat[:, w0 + c0 // R:w0 + (c0 + m) // R, :],
                                            ps[0:32, :m].rearrange("p (w r) -> p w r", r=R), AF.Copy), reads=[bps], writes=[byr])
                        c0 += m
                K.stq("YS5", dr["YS5"][d, gp * 32:(gp + 1) * 32, :], yr[:, :], byr)
    P.barrier()


def stage_post(K, l):
    P, cfg, dr = K.P, K.cfg, K.dram
    GC = math.sqrt(2.0 / math.pi)
    with contextlib.ExitStack() as st:
        bavg, bbavg = K.sb(st, [128, 128], F32, "bavg")
        K.ld(bavg[:], bbavg, "c_bones", dr["c_bones"])
        P.op("dve", TS(bavg[:], bavg[:], 1.0 / 64, ALU.mult), reads=[bbavg], writes=[bbavg])
        vec, bvec = K.sb(st, [128, 8, 3], F32, "pvec")
        K.ld(vec[:, 0, :], bvec, "rw_lnx_g", dr["rw_lnx_g"][:, l, :])
        K.ld(vec[:, 1, :], bvec, "rw_lnx_b", dr["rw_lnx_b"][:, l, :])
        K.ld(vec[:, 2, 0:1], bvec, "gd_ng", dr["gd_ng"][l])
        K.ld(vec[:, 3, 0:2], bvec, "s5_d", dr["s5_d"][:, l, :])
        K.ld(vec[:, 4, 0:2], bvec, "s5_glu_b", dr["s5_glu_b"][:, l, :])
        gw, bgw = K.sb(st, [128, 2, 256], F32, "gluw")
        K.ld(gw[:], bgw, "s5_glu_w", dr["s5_glu_w"][:, l])
        NI = 8
        ins = [K.sb(st, [128, 512], F32, "pin") for _ in range(NI)]
        tmps = [K.sb(st, [128, 512], F32, "ptmp") for _ in range(8)]
        outs = [K.sb(st, [128, 512], F32, "pout") for _ in range(4)]
        gl, bgl = K.sb(st, [128, 2, 512], F32, "gel")
        pss = [K.ps(st) for _ in range(4)]
        ctr = {"i": 0, "t": 0, "o": 0, "p": 0}

        def IN_(dname, ap, n):
            ctr["i"] += 1
            t, b = ins[ctr["i"] % NI]
            K.ld(t[:, :n], b, dname, ap)
            return t, b

        def T_():
            ctr["t"] += 1
            return tmps[ctr["t"] % 8]

        def O_():
            ctr["o"] += 1
            return outs[ctr["o"] % 4]

        def PS_():
            ctr["p"] += 1
            return pss[ctr["p"] % 4]

        def rstd_of(src, bsrc, n, eps):
            t2, bt2 = T_()
            P.op("act", ACT(t2[:, :n], src[:, :n], AF.Square), reads=[bsrc], writes=[bt2])
            ps, bps = PS_()
            P.op("pe", MM(ps[:, :n], bavg[:, :], t2[:, :n]), reads=[bbavg, bt2], writes=[bps])
            t3, bt3 = T_()
            P.op("dve", TS(t3[:, :n], ps[:, :n], eps, ALU.add), reads=[bps], writes=[bt3])
            P.op("act", ACT(t3[:, :n], t3[:, :n], AF.Ln), reads=[bt3], writes=[bt3]); P.op("act", ACT(t3[:, :n], t3[:, :n], AF.Exp, scale=-0.5), reads=[bt3], writes=[bt3])
            return t3, bt3

        for ti, (t0, n, m) in enumerate(token_tiles(cfg)):
            ts = slice(t0, t0 + n)
            for c3 in range(3):
                rows = slice(c3 * 128, (c3 + 1) * 128)
                y0, by0 = IN_("YRW", dr["YRW"][0, rows, ts], n)
                y1, by1 = IN_("YRW", dr["YRW"][1, rows, ts], n)
                bon, bbon = IN_("RW_BONUS", dr["RW_BONUS"][rows, ts], n)
                gat, bgat = IN_("RW_GATE", dr["RW_GATE"][rows, ts], n)
                y, by = T_()
                P.op("dve", TT(y[:, :n], y0[:, :n], y1[:, :n], ALU.add), reads=[by0, by1], writes=[by])
                ps, bps = PS_()
                P.op("pe", MM(ps[:, :n], bavg[:, :], y[:, :n]), reads=[bbavg, by], writes=[bps])
                yc, byc = T_()
                P.op("dve", TT(yc[:, :n], y[:, :n], ps[:, :n], ALU.subtract), reads=[by, bps], writes=[byc])
                rs, brs = rstd_of(yc, byc, n, LNX_EPS)
                t1, bt1 = T_()
                P.op("dve", STT(t1[:, :n], yc[:, :n], vec[:, 0, c3:c3 + 1], rs[:, :n], ALU.mult, ALU.mult),
                     reads=[byc, bvec, brs], writes=[bt1])
                P.op("dve", STT(t1[:, :n], t1[:, :n], vec[:, 1, c3:c3 + 1], bon[:, :n], ALU.add, ALU.add),
                     reads=[bt1, bvec, bbon], writes=[bt1])
                o, bo = O_()
                P.op("dve", TT(o[:, :n], t1[:, :n], gat[:, :n], ALU.mult), reads=[bt1, bgat], writes=[bo])
                K.stq("MX", dr["MX"][rows, ts], o[:, :n], bo)
                y0, by0 = IN_("YGD", dr["YGD"][0, rows, ts], n)
                y1, by1 = IN_("YGD", dr["YGD"][1, rows, ts], n)
                gz, bgz = IN_("Z", dr["Z"][Z_GDG + c3 * 128:Z_GDG + (c3 + 1) * 128, ts], n)
                y, by = T_()
                P.op("dve", TT(y[:, :n], y0[:, :n], y1[:, :n], ALU.add), reads=[by0, by1], writes=[by])
                rs, brs = rstd_of(y, by, n, NORM_EPS)
                t1, bt1 = T_()
                P.op("dve", STT(t1[:, :n], y[:, :n], vec[:, 2, 0:1], rs[:, :n], ALU.mult, ALU.mult),
                     reads=[by, bvec, brs], writes=[bt1])
                t2, bt2 = T_()
                P.op("act", ACT(t2[:, :n], gz[:, :n], AF.Silu), reads=[bgz], writes=[bt2])
                o, bo = O_()
                P.op("dve", TT(o[:, :n], t1[:, :n], t2[:, :n], ALU.mult), reads=[bt1, bt2], writes=[bo])
                K.stq("MX", dr["MX"][384 + c3 * 128:384 + (c3 + 1) * 128, ts], o[:, :n], bo)
            for c2 in range(2):
                rows = slice(c2 * 128, (c2 + 1) * 128)
                y0, by0 = IN_("YS5", dr["YS5"][0, rows, ts], n)
                y1, by1 = IN_("YS5", dr["YS5"][1, rows, ts], n)
                uz, buz = IN_("Z", dr["Z"][Z_S5 + c2 * 128:Z_S5 + (c2 + 1) * 128, ts], n)
                y, by = T_()
                P.op("dve", TT(y[:, :n], y0[:, :n], y1[:, :n], ALU.add), reads=[by0, by1], writes=[by])
                P.op("dve", STT(y[:, :n], uz[:, :n], vec[:, 3, c2:c2 + 1], y[:, :n], ALU.mult, ALU.add),
                     reads=[buz, bvec, by], writes=[by])
                t1, bt1 = T_()
                P.op("act", ACT(t1[:, :n], y[:, :n], AF.Square), reads=[by], writes=[bt1])
                P.op("dve", TS(t1[:, :n], t1[:, :n], 0.044715 * GC, ALU.mult, GC, ALU.add), reads=[bt1], writes=[bt1])
                P.op("dve", TT(t1[:, :n], t1[:, :n], y[:, :n], ALU.mult), reads=[bt1, by], writes=[bt1])
                P.op("act", ACT(t1[:, :n], t1[:, :n], AF.Tanh), reads=[bt1], writes=[bt1])
                P.op("dve", TS(t1[:, :n], t1[:, :n], 1.0, ALU.add, 0.5, ALU.mult), reads=[bt1], writes=[bt1])
                P.op("dve", TT(gl[:, c2, :n], t1[:, :n], y[:, :n], ALU.mult), reads=[bt1, by], writes=[bgl])
            for c2 in range(2):
                ps, bps = PS_()
                for k in range(2):
                    P.op("pe", MM(ps[:, :n], gw[:, k, c2 * 128:(c2 + 1) * 128], gl[:, k, :n], k == 0, k == 1),
                         reads=[bgw, bgl], writes=[bps])
                t1, bt1 = T_()
                P.op("act", ACT(t1[:, :n], ps[:, :n], AF.Sigmoid, bias=vec[:, 4, c2:c2 + 1], scale=1.0),
                     reads=[bps, bvec], writes=[bt1])
                o, bo = O_()
                P.op("dve", TT(o[:, :n], t1[:, :n], gl[:, c2, :n], ALU.mult), reads=[bt1, bgl], writes=[bo])
                K.stq("MX", dr["MX"][768 + c2 * 128:768 + (c2 + 1) * 128, ts], o[:, :n], bo)
    P.barrier()


def stage_C(K, l, xin, xout):
    P, cfg, dr = K.P, K.cfg, K.dram
    L = cfg["L"]
    last = (l == L - 1)
    NJ = FFN // 128
    with contextlib.ExitStack() as st:
        ones, bones = K.sb(st, [128, 128], F32, "ones")
        P.op("dve", MS(ones[:], 1.0), writes=[bones])
        Wo, bWo = K.sb(st, [128, 8, 1024], BF16, "wout", multi=True)
        Wd, bWd = K.sb(st, [128, NJ, 1024], BF16, "wdown", multi=True)
        sq, bsq = K.sb(st, [128, 8, 512], F32, "sq")
        tmp, btmp = K.sb(st, [128, 8, 512], F32, "tmp")
        stg = [(sq, bsq), (tmp, btmp)]
        it = 0
        for cb in range(2):
            w, bw = stg[it % 2]
            it += 1
            K.ld(w[:], bw, "w_out", dr["w_out"][l, :, cb * 512:(cb + 1) * 512].rearrange("(k p) n -> p k n", p=128))
            P.op("act" if it % 2 else "dve", ACT(Wo[:, :, cb * 512:(cb + 1) * 512], w[:], AF.Copy) if it % 2 else
                 CP(Wo[:, :, cb * 512:(cb + 1) * 512], w[:]), reads=[bw], writes=[bWo])
        for j0 in range(0, NJ, 4):
            jn = min(4, NJ - j0)
            w, bw = stg[it % 2]
            it += 1
            wv = w[:].rearrange("p k n -> p (k n)")[:, 0:jn * 1024].rearrange("p (j n) -> p j n", n=1024)
            K.ld(wv, bw, "ffn_w_down",
                 dr["ffn_w_down"][l, j0 * 128:(j0 + jn) * 128, :].rearrange("(j p) n -> p j n", p=128))
            P.op("act" if it % 2 else "dve", ACT(Wd[:, j0:j0 + jn, :], wv, AF.Copy) if it % 2 else
                 CP(Wd[:, j0:j0 + jn, :], wv), reads=[bw], writes=[bWd])
        for nm, dn in (("ffn_w_gate", "WGb"), ("ffn_w_up", "WUb")):
            c0 = 0
            while c0 < FFN:
                mcol = min(512, FFN - c0)
                w, bw = stg[it % 2]
                it += 1
                K.ld(w[:, :, :mcol], bw, nm, dr[nm][l, :, c0:c0 + mcol].rearrange("(k p) n -> p k n", p=128))
                wb, bwb = K.sb(st, [128, 8, 512], BF16, "wb") if c0 == 0 and nm == "ffn_w_gate" else (None, None)
                if wb is not None:
                    K._wb = (wb, bwb)
                wb, bwb = K._wb
                P.op("act" if it % 2 else "dve", ACT(wb[:, :, :mcol], w[:, :, :mcol], AF.Copy) if it % 2 else
                     CP(wb[:, :, :mcol], w[:, :, :mcol]), reads=[bw], writes=[bwb])
                K.stq(dn, dr[dn][:, :, c0:c0 + mcol], wb[:, :, :mcol], bwb)
                c0 += mcol
        P.barrier()
        x, bx = K.sb(st, [128, 8, 512], F32, "x")
        mx, bmx = K.sb(st, [128, 8, 512], F32, "mx")
        mxb, bmxb = K.sb(st, [128, 8, 512], BF16, "mxb")
        h2, bh2 = K.sb(st, [128, 8, 512], BF16, "h2")
        act, bact = K.sb(st, [128, NJ, 512], BF16, "act", multi=True)
        rstd, brstd = K.sb(st, [128, 512], F32, "rstd")
        sgs = [K.sb(st, [128, 512], F32, "sg") for _ in range(2)]
        wgs = [K.sb(st, [128, 2, 8, 128], BF16, "wgu") for _ in range(3)]
        fg, bfg = K.sb(st, [128, 8], F32, "fg")
        K.ld(fg[:], bfg, "final_g", dr["final_g"])
        psn = K.ps(st)
        pw = [K.ps(st) for _ in range(2)]
        pg = [K.ps(st) for _ in range(2)]
        pu = [K.ps(st) for _ in range(2)]
        pd = K.ps(st)
        xin3 = dr[xin].rearrange("(k p) t -> p k t", p=128)
        mx3 = dr["MX"].rearrange("(k p) t -> p k t", p=128)
        gi = 0
        C = cfg["C"]
        for ti, (t0, n, m) in enumerate(token_tiles(cfg)):
            if last and m == 1:
                continue
            K.ld(x[:, :, :n], bx, xin, xin3[:, :, t0:t0 + n])
            K.ld(mx[:, :, :n], bmx, "MX", mx3[:, :, t0:t0 + n])
            P.op("act", ACT(mxb[:, 0:4, :n], mx[:, 0:4, :n], AF.Copy), reads=[bmx], writes=[bmxb])
            P.op("dve", CP(mxb[:, 4:8, :n], mx[:, 4:8, :n]), reads=[bmx], writes=[bmxb])
            for oc in range(8):
                ps, bps = pw[oc % 2]
                for k in range(8):
                    P.op("pe", MM(ps[:, :n], Wo[:, k, oc * 128:(oc + 1) * 128], mxb[:, k, :n], k == 0, k == 7),
                         reads=[bWo, bmxb], writes=[bps])
                P.op("dve", STT(x[:, oc, :n], ps[:, :n], K.mod[:, l, 16 + oc, m:m + 1], x[:, oc, :n], ALU.mult, ALU.add),
                     reads=[bps, K.bmod, bx], writes=[bx])
            norm_mod(K, (sq, bsq, rstd, brstd, tmp, btmp), x, bx, n, K.A2, K.bA2, 24, l, m, h2, bh2, ones, bones, psn)
            for j in range(NJ):
                wg, bwg = wgs[gi % 3]
                K.ld(wg[:, 0], bwg, "WGb", dr["WGb"][:, :, j * 128:(j + 1) * 128])
                K.ld(wg[:, 1], bwg, "WUb", dr["WUb"][:, :, j * 128:(j + 1) * 128])
                psg, bpsg = pg[gi % 2]
                psu, bpsu = pu[gi % 2]
                sg, bsg = sgs[gi % 2]
                gi += 1
                for k in range(8):
                    P.op("pe", MM(psg[:, :n], wg[:, 0, k, :], h2[:, k, :n], k == 0, k == 7), reads=[bwg, bh2], writes=[bpsg])
                for k in range(8):
                    P.op("pe", MM(psu[:, :n], wg[:, 1, k, :], h2[:, k, :n], k == 0, k == 7), reads=[bwg, bh2], writes=[bpsu])
                P.op("act", ACT(sg[:, :n], psg[:, :n], AF.Silu), reads=[bpsg], writes=[bsg])
                P.op("dve", TT(act[:, j, :n], sg[:, :n], psu[:, :n], ALU.mult), reads=[bsg, bpsu], writes=[bact])
            for oc in range(8):
                for j in range(NJ):
                    P.op("pe", MM(pd[0][:, :n], Wd[:, j, oc * 128:(oc + 1) * 128], act[:, j, :n], j == 0, j == NJ - 1),
                         reads=[bWd, bact], writes=[pd[1]])
                P.op("dve", STT(x[:, oc, :n], pd[0][:, :n], K.mod[:, l, 40 + oc, m:m + 1], x[:, oc, :n], ALU.mult, ALU.add),
                     reads=[pd[1], K.bmod, bx], writes=[bx])
            if not last:
                K.stq(xout, dr[xout].rearrange("(k p) t -> p k t", p=128)[:, :, t0:t0 + n], x[:, :, :n], bx)
            else:
                P.op("act", ACT(sq[:, :, :n], x[:, :, :n], AF.Square), reads=[bx], writes=[bsq])
                for k in range(8):
                    P.op("pe", MM(psn[0][:, :n], ones[:, :], sq[:, k, :n], k == 0, k == 7), reads=[bsq, bones], writes=[psn[1]])
                P.op("dve", TS(rstd[:, :n], psn[0][:, :n], 1.0 / D, ALU.mult, NORM_EPS, ALU.add), reads=[psn[1]], writes=[brstd])
                P.op("act", ACT(rstd[:, :n], rstd[:, :n], AF.Ln), reads=[brstd], writes=[brstd]); P.op("act", ACT(rstd[:, :n], rstd[:, :n], AF.Exp, scale=-0.5), reads=[brstd], writes=[brstd])
                for k in range(8):
                    P.op("dve", STT(tmp[:, k, :n], x[:, k, :n], fg[:, k:k + 1], rstd[:, :n], ALU.mult, ALU.mult),
                         reads=[bx, bfg, brstd], writes=[btmp])
                K.stq("yT", dr["yT"].rearrange("(k p) t -> p k t", p=128)[:, :, t0 - C:t0 - C + n], tmp[:, :, :n], btmp)
    P.barrier()


def declare(K):
    cfg = K.cfg
    L, TT = cfg["L"], cfg["C"] + cfg["T"]
    K.din("xT", [D, TT])
    K.din("condT", [128, 8, 2])
    K.din("ada_w", [L, D, 6 * D])
    K.din("ada_b", [128, L, 48])
    K.din("norm1_g", [128, L, 8])
    K.din("norm2_g", [128, L, 8])
    K.din("w_in", [L, D, IN_TOTAL])
    K.dscr("Z", [IN_TOTAL, TT])
    K.din("c_bones", [128, 128])
    K.din("rw_mu", [128, L, 11])
    K.din("rw_w0", [128, L, 2, 3])
    K.din("rw_a0", [128, L, 2, 3])
    K.din("rw_wup", [64, L, 2, 384])
    K.din("rw_aup", [64, L, 2, 384])
    K.din("rw_gup", [128, L, 384])
    for nm in ("rw_kk", "rw_ka", "rw_rk", "rw_lnx_g", "rw_lnx_b"):
        K.din(nm, [128, L, 3])
    for nm in ("RW_R", "RW_V", "RW_KK", "RW_BONUS", "RW_GATE"):
        K.dscr(nm, [384, TT])
    for nm in ("RW_LAM", "RW_AS", "RW_KD", "YRW"):
        K.dscr(nm, [2, 384, TT])
    for nm in ("GD_Q", "GD_K", "GD_V"):
        K.dscr(nm, [384, TT])
    for nm in ("GD_BETA", "GD_G", "YGD"):
        K.dscr(nm, [2, 384, TT])
    K.dscr("YS5", [2, 256, TT])
    K.dscr("MX", [D, TT])
    K.dscr("XR0", [D, TT])
    K.dscr("XR1", [D, TT])
    K.dscr("WGb", [128, 8, FFN], BF16)
    K.dscr("WUb", [128, 8, FFN], BF16)
    K.dout("yT", [D, cfg["T"]])
    K.din("w_out", [L, D, D])
    K.din("ffn_w_gate", [L, D, FFN])
    K.din("ffn_w_up", [L, D, FFN])
    K.din("ffn_w_down", [L, FFN, D])
    K.din("final_g", [128, 8])
    K.din("gd_ng", [L, 128, 1])
    K.din("s5_d", [128, L, 2])
    K.din("s5_glu_b", [128, L, 2])
    K.din("s5_glu_w", [128, L, 2, 256])
    K.din("s5_prm", [128, L, 3, 16])
    K.din("s5_B", [128, L, 2, 16, 32])
    K.din("s5_C", [128, L, 2, 16, 32])
    K.din("gd_conv", [128, L, 3, 9])
    K.din("gd_prm", [12, L, 2])
    K.din("c_sel", [12, 2, 3, 128])
    K.din("c_ident", [128, 128])
    K.din("c_masks", [64, 4, 512])
    K.din("c_negm", [64, 4, 512])
    K.din("c_rmask", [64, 512])
    K.din("c_bd", [64, 2, 512])


def build(cfg):
    nc = bass.Bass("TRN2", target_bir_lowering=False)
    K = Ctx(nc, cfg)
    declare(K)
    stop = cfg.get("stop", "")
    skip = cfg.get("skip", "")
    with contextlib.ExitStack() as gst:
        stage_mods(K, gst)
        xin = "xT"
        for l in range(cfg["L"]):
            xout = "XR%d" % (l % 2)
            K.xin = xin
            stage_A(K, l, xin)
            if stop == "A":
                break
            stage_rw_pre(K, l)
            if "rw" not in skip:
                stage_scan(K, l, "rw")
            stage_gd_pre(K, l)
            if "gd" not in skip:
                stage_scan(K, l, "gd")
            stage_s5(K, l)
            if stop == "B":
                break
            stage_post(K, l)
            if stop == "P":
                break
            stage_C(K, l, xin, xout)
            xin = xout
        K.P.emit()
    return nc


OUT_NAMES = ["yT"]


def core_inputs(inp, b, cfg):
    C, T = cfg["C"], cfg["T"]
    f = lambda a: np.ascontiguousarray(a, dtype=np.float32)
    d = {}
    d["xT"] = f(np.concatenate([inp["ctx"][b][:C], inp["x"][b][:T]], 0).T)
    L = cfg["L"]
    pl = lambda a: f(a[:L].reshape(L, -1, 128).transpose(2, 0, 1))
    d["condT"] = f(np.stack([inp["c"][b], inp["c_ctx"]], 1).reshape(8, 128, 2).transpose(1, 0, 2))
    d["ada_b"] = pl(inp["ada_b"])
    d["norm1_g"] = pl(inp["norm1_g"])
    d["norm2_g"] = pl(inp["norm2_g"])
    for k in ("ada_w", "w_in"):
        d[k] = f(inp[k][:L])
    bo = np.zeros((128, 128), np.float32)
    bo[:64, :64] = 1.0
    bo[64:, 64:] = 1.0
    d["c_bones"] = bo
    for k in ("w_out", "ffn_w_gate", "ffn_w_up", "ffn_w_down"):
        d[k] = f(inp[k][:L])
    d["final_g"] = f(inp["final_g"].reshape(8, 128).T)
    d["gd_ng"] = f(np.tile(inp["gd_norm_g"][:L], (1, 2))[:, :, None])
    d["s5_d"] = pl(inp["s5_d"])
    d["s5_glu_b"] = pl(inp["s5_glu_b"])
    d["s5_glu_w"] = f(inp["s5_glu_w"][:L].reshape(L, 2, 128, 256).transpose(2, 0, 1, 3))
    d["c_ident"] = np.eye(128, dtype=np.float32)
    def s5v(a):
        return a[:L].reshape(L, 2, 8, 2, 64).transpose(3, 4, 0, 1, 2).reshape(128, L, 16)
    ldt = np.repeat(inp["s5_log_dt"][:L][..., None], 64, axis=-1)
    d["s5_prm"] = f(np.stack([s5v(inp["s5_lam_re"]), s5v(inp["s5_lam_im"]), s5v(ldt)], 2))
    def s5blk(a, kind):
        out = np.zeros((128, L, 16, 32), np.float32)
        for dd in range(2):
            for gp in range(8):
                for gl in range(2):
                    blk = a[:L, dd, gp * 2 + gl]
                    if kind == "C":
                        blk = blk.transpose(0, 2, 1)
                    out[gl * 64:(gl + 1) * 64, :, dd * 8 + gp, gl * 16:(gl + 1) * 16] = blk.transpose(1, 0, 2)
        return out
    d["s5_B"] = f(np.stack([s5blk(inp["s5_b_re"], "B"), s5blk(inp["s5_b_im"], "B")], 2))
    d["s5_C"] = f(np.stack([s5blk(inp["s5_c_re"], "C"), s5blk(inp["s5_c_im"], "C")], 2))
    d["gd_conv"] = f(inp["gd_conv"][:L].reshape(L, 3, 9, 128).transpose(3, 0, 1, 2))
    d["gd_prm"] = f(np.stack([inp["gd_a_log"][:L].reshape(L, 12), inp["gd_dt_bias"][:L].reshape(L, 12)], -1).transpose(1, 0, 2))
    sel = np.zeros((12, 2, 3, 128), np.float32)
    for dd in range(2):
        for c3 in range(3):
            for p in range(128):
                sel[dd * 6 + 2 * c3 + p // 64, dd, c3, p] = 1.0
    d["c_sel"] = sel
    i = np.arange(64)[:, None]
    j = np.arange(64)[None, :]
    ms = np.stack([j < i, j > i, j <= i, j >= i]).astype(np.float32)
    d["c_masks"] = f(np.tile(ms.transpose(1, 0, 2), (1, 1, 8)))
    d["c_negm"] = f((np.tile(ms.transpose(1, 0, 2), (1, 1, 8)) - 1.0) * 30000.0)
    rm = np.ones((64, 512), np.float32)
    rm[:, ::64] = 0.0
    d["c_rmask"] = rm
    bdm = ((i // 16) == (j // 16)).astype(np.float32)
    d["c_bd"] = f(np.tile(np.stack([bdm, 1.0 - bdm], 1), (1, 1, 8)))
    d["rw_mu"] = pl(inp["rw_mu"])
    pl2 = lambda a: f(a[:L].reshape(L, 2, 3, 128).transpose(3, 0, 1, 2))
    d["rw_w0"] = pl2(inp["rw_w0"])
    d["rw_a0"] = pl2(inp["rw_a0"])
    d["rw_wup"] = f(inp["rw_wup"][:L].transpose(2, 0, 1, 3))
    d["rw_aup"] = f(inp["rw_aup"][:L].transpose(2, 0, 1, 3))
    d["rw_gup"] = f(inp["rw_gup"][:L].transpose(1, 0, 2))
    for nm in ("rw_kk", "rw_ka", "rw_lnx_g", "rw_lnx_b"):
        d[nm] = pl(inp[nm])
    d["rw_rk"] = pl(inp["rw_rk"].reshape(inp["rw_rk"].shape[0], 384))
    return d


FULL_CFG = {"T": 8192, "C": 256, "L": 4}
_NC_CACHE = {}


def kernel(**inputs):
    cfg = dict(FULL_CFG)
    inp = {k: np.asarray(v) for k, v in inputs.items()}
    B = inp["x"].shape[0]
    if "nc" not in _NC_CACHE:
        _NC_CACHE["nc"] = build(cfg)
    nc = _NC_CACHE["nc"]
    per_b = [core_inputs(inp, b, cfg) for b in range(B)]
    in_maps = [per_b[i % B] for i in range(8)]
    res = run_bass_kernel_spmd(nc, in_maps, core_ids=list(range(8)))
    out = np.stack([np.ascontiguousarray(res.results[b]["yT"].T) for b in range(B)], 0)
    return out.astype(np.float32)
```

```python
import contextlib
import math
import numpy as np
import concourse.bass as bass
import concourse.mybir as mybir
from concourse.bass_utils import run_bass_kernel_spmd

F32 = mybir.dt.float32
BF16 = mybir.dt.bfloat16
AF = mybir.ActivationFunctionType
ALU = mybir.AluOpType

ENGS = ("pe", "dve", "act", "pool", "sp")

D = 1024
HD = 64
RW_W = 384
GD_W = 384
S5_W = 256
IN_RW = 1408
IN_TOTAL = 3224
FFN = 2816
NORM_EPS = 1e-6
LNX_EPS = 64e-5
Z_GDQ = 1408
Z_GDBA = 2560
Z_GDG = 2584
Z_S5 = 2968


class Buf:
    __slots__ = ("name", "w", "r", "multi")

    def __init__(self, name="", multi=False):
        self.name = name
        self.multi = multi
        self.w = {}
        self.r = {}


class Prog:
    EPOCH = 30000
    NDS = 24

    def __init__(self, nc):
        self.nc = nc
        self.stack = contextlib.ExitStack()
        self.ops = {e: [] for e in ENGS}
        self.cnt = {e: 0 for e in ENGS}
        self.seen = {e: {} for e in ENGS}
        self.sems = {}
        self.dcnt = {q: [0] * self.NDS for q in ENGS}
        self.dnext = {q: 0 for q in ENGS}
        self.dma_out = {}

    def sem(self, key):
        s = self.sems.get(key)
        if s is None:
            s = self.stack.enter_context(self.nc.semaphore("s_%s_%s" % key))
            self.sems[key] = s
        return s

    def _waits(self, eng, reads, writes):
        need = {}
        for b in reads:
            for k, v in b.w.items():
                if need.get(k, 0) < v:
                    need[k] = v
        for b in writes:
            for k, v in b.r.items():
                if need.get(k, 0) < v:
                    need[k] = v
            if not b.multi:
                for k, v in b.w.items():
                    if need.get(k, 0) < v:
                        need[k] = v
        out = []
        seen = self.seen[eng]
        for k, v in need.items():
            if seen.get(k, 0) < v:
                seen[k] = v
                out.append((k, v))
        return out

    def _commit(self, tok, reads, writes):
        k, v = tok
        for b in reads:
            if b.r.get(k, 0) < v:
                b.r[k] = v
        for b in writes:
            if b.r or not b.multi:
                b.r = {}
                b.w = {k: v}
            elif b.w.get(k, 0) < v:
                b.w[k] = v

    def op(self, eng, fn, reads=(), writes=()):
        waits = self._waits(eng, reads, writes)
        i = self.cnt[eng]
        self.cnt[eng] = i + 1
        tok = ((eng, i // self.EPOCH), i % self.EPOCH + 1)
        self.ops[eng].append((waits, fn, tok[0], 1))
        self._commit(tok, reads, writes)
        return tok

    def dma(self, out_ap, in_ap, reads=(), writes=(), q="sp"):
        s = self.dnext[q]
        self.dnext[q] = (s + 1) % self.NDS
        key = ("dma" + q, s)
        prev = 16 * self.dcnt[q][s]
        self.dcnt[q][s] += 1
        val = prev + 16
        waits = self._waits(q, reads, writes)
        if prev and self.seen[q].get(key, 0) < prev:
            self.seen[q][key] = prev
            waits.append((key, prev))
        self.ops[q].append((waits, lambda e: e.dma_start(out=out_ap, in_=in_ap), key, 16))
        tok = (key, val)
        self._commit(tok, reads, writes)
        self.dma_out[key] = val
        return tok

    def mark(self):
        pass

    def barrier(self):
        toks = dict(self.dma_out)
        for e in ENGS:
            i = self.cnt[e]
            if i:
                toks[(e, (i - 1) // self.EPOCH)] = (i - 1) % self.EPOCH + 1
        for e in ENGS:
            waits = []
            seen = self.seen[e]
            for k, v in toks.items():
                if seen.get(k, 0) < v:
                    seen[k] = v
                    waits.append((k, v))
            if waits:
                self.ops[e].append((waits, None, None, 0))

    def emit(self):
        self.barrier()
        prog = self

        def run(e, name):
            for waits, fn, key, n in prog.ops[name]:
                for k, v in waits:
                    e.wait_ge(prog.sem(k), v)
                if fn is not None:
                    fn(e).then_inc(prog.sem(key), n)

        for name in ENGS:
            for waits, fn, key, n in self.ops[name]:
                for k, v in waits:
                    self.sem(k)
                if key is not None:
                    self.sem(key)
        with self.nc.Block() as block:
            @block.sync
            def _(e):
                run(e, "sp")

            @block.tensor
            def _(e):
                run(e, "pe")

            @block.vector
            def _(e):
                run(e, "dve")

            @block.scalar
            def _(e):
                run(e, "act")

            @block.gpsimd
            def _(e):
                run(e, "pool")
        self.stack.close()


def MM(out, l, r, start=True, stop=True):
    return lambda e: e.matmul(out, l, r, start=start, stop=stop)


def ACT(out, in_, func, bias=None, scale=None):
    kw = {}
    if bias is not None:
        kw["bias"] = bias
    if scale is not None:
        kw["scale"] = scale
    return lambda e: e.activation(out=out, in_=in_, func=func, **kw)


def TT(out, a, b, op):
    return lambda e: e.tensor_tensor(out, a, b, op)


def TS(out, a, s1, op0, s2=None, op1=None):
    if op1 is None:
        return lambda e: e.tensor_scalar(out, a, s1, None, op0)
    return lambda e: e.tensor_scalar(out, a, s1, s2, op0, op1)


def STT(out, a, s, b, op0, op1):
    return lambda e: e.scalar_tensor_tensor(out, a, s, b, op0, op1)


def CP(out, in_):
    return lambda e: e.tensor_copy(out, in_)


def MS(out, val):
    return lambda e: e.memset(out, val)


class Rec:
    def __init__(self):
        self.items = []

    def op(self, eng, fn, reads=(), writes=()):
        self.items.append(("op", eng, fn, tuple(reads), tuple(writes)))

    def dma(self, out_ap, in_ap, reads=(), writes=(), q="sp"):
        self.items.append(("dma", out_ap, in_ap, tuple(reads), tuple(writes), q))

    def mark(self):
        self.items.append(("mark",))

    def segments(self):
        segs, cur = [], []
        for it in self.items:
            if it[0] == "mark":
                if cur:
                    segs.append(cur)
                    cur = []
            else:
                cur.append(it)
        if cur:
            segs.append(cur)
        return segs


def replay_interleaved(P, recs):
    seglists = [r.segments() for r in recs]
    idx = [0] * len(seglists)
    live = True
    while live:
        live = False
        for i, sl in enumerate(seglists):
            if idx[i] < len(sl):
                live = True
                for it in sl[idx[i]]:
                    if it[0] == "op":
                        P.op(it[1], it[2], reads=it[3], writes=it[4])
                    else:
                        P.dma(it[1], it[2], reads=it[3], writes=it[4], q=it[5])
                idx[i] += 1


class Ctx:
    def __init__(self, nc, cfg):
        self.nc = nc
        self.cfg = cfg
        self.P = Prog(nc)
        self.dram = {}
        self.dbuf = {}
        self.ncnt = 0

    def din(self, name, shape, dt=F32):
        self.dram[name] = self.nc.dram_tensor(name, list(shape), dt, kind="ExternalInput").ap()
        self.dbuf[name] = Buf(name, True)
        return self.dram[name]

    def dout(self, name, shape, dt=F32):
        self.dram[name] = self.nc.dram_tensor(name, list(shape), dt, kind="ExternalOutput").ap()
        self.dbuf[name] = Buf(name, True)
        return self.dram[name]

    def dscr(self, name, shape, dt=F32):
        kind = "ExternalOutput" if name in self.cfg.get("debug", ()) else "Internal"
        self.dram[name] = self.nc.dram_tensor(name, list(shape), dt, kind=kind).ap()
        self.dbuf[name] = Buf(name, True)
        return self.dram[name]

    def sb(self, st, shape, dt=F32, name=None, multi=False):
        self.ncnt += 1
        nm = "%s_%d" % (name or "t", self.ncnt)
        t = st.enter_context(self.nc.sbuf_tensor(nm, list(shape), dt))
        return t, Buf(nm, multi)

    def ps(self, st, shape=(128, 512), dt=F32, name=None):
        self.ncnt += 1
        nm = "%s_%d" % (name or "ps", self.ncnt)
        t = st.enter_context(self.nc.psum_tensor(nm, list(shape), dt))
        return t, Buf(nm, True)

    def ld(self, out_ap, ob, dname, in_ap):
        self.P.dma(out_ap, in_ap, reads=[self.dbuf[dname]], writes=[ob], q="sp")

    def stq(self, dname, out_ap, in_ap, ib):
        self.P.dma(out_ap, in_ap, reads=[ib], writes=[self.dbuf[dname]], q="pool")


def token_tiles(cfg, n=512):
    C, T = cfg["C"], cfg["T"]
    tiles = []
    t = 0
    while t < C:
        m = min(n, C - t)
        tiles.append((t, m, 1))
        t += m
    t = C
    while t < C + T:
        m = min(n, C + T - t)
        tiles.append((t, m, 0))
        t += m
    return tiles


def stage_mods(K, gst):
    nc, P, cfg = K.nc, K.P, K.cfg
    L = cfg["L"]
    mod, bmod = K.sb(gst, [128, L, 48, 2], F32, "mod")
    A1, bA1 = K.sb(gst, [128, L, 8, 2], F32, "A1")
    A2, bA2 = K.sb(gst, [128, L, 8, 2], F32, "A2")
    with contextlib.ExitStack() as st:
        cond, bcond = K.sb(st, [128, 8, 2], F32, "cond")
        sc, bsc = K.sb(st, [128, 8, 2], F32, "scond")
        adb, badb = K.sb(st, [128, L, 48], F32, "adab")
        n1, bn1 = K.sb(st, [128, L, 8], F32, "n1g")
        n2, bn2 = K.sb(st, [128, L, 8], F32, "n2g")
        K.ld(cond[:], bcond, "condT", K.dram["condT"])
        K.ld(adb[:], badb, "ada_b", K.dram["ada_b"])
        K.ld(n1[:], bn1, "norm1_g", K.dram["norm1_g"])
        K.ld(n2[:], bn2, "norm2_g", K.dram["norm2_g"])
        P.op("act", ACT(sc[:], cond[:], AF.Silu), reads=[bcond], writes=[bsc])
        wts = [K.sb(st, [128, 8, 512], F32, "adaw") for _ in range(2)]
        pss = [K.ps(st) for _ in range(2)]
        it = 0
        for l in range(L):
            for cb in range(12):
                w, bw = wts[it % 2]
                ps, bps = pss[it % 2]
                it += 1
                K.ld(w[:], bw, "ada_w",
                     K.dram["ada_w"][l, :, cb * 512:(cb + 1) * 512].rearrange("(k p) n -> p k n", p=128))
                for o in range(4):
                    for k in range(8):
                        P.op("pe", MM(ps[:, o * 2:o * 2 + 2], w[:, k, o * 128:(o + 1) * 128], sc[:, k, :],
                                      k == 0, k == 7), reads=[bw, bsc], writes=[bps])
                for o in range(4):
                    oc = cb * 4 + o
                    P.op("dve", TS(mod[:, l, oc, :], ps[:, o * 2:o * 2 + 2], adb[:, l, oc:oc + 1], ALU.add),
                         reads=[bps, badb], writes=[bmod])
        for l in range(L):
            for m in range(2):
                P.op("dve", STT(A1[:, l, :, m], mod[:, l, 8:16, m], 1.0, n1[:, l, :], ALU.add, ALU.mult),
                     reads=[bmod, bn1], writes=[bA1])
                P.op("dve", STT(A2[:, l, :, m], mod[:, l, 32:40, m], 1.0, n2[:, l, :], ALU.add, ALU.mult),
                     reads=[bmod, bn2], writes=[bA2])
    P.barrier()
    K.mod, K.bmod, K.A1, K.bA1, K.A2, K.bA2 = mod, bmod, A1, bA1, A2, bA2


def norm_mod(K, st_bufs, x, bx, n, A, bA, boff, l, m, h, bh, ones, bones, pss):
    P = K.P
    sq, bsq, rstd, brstd, tmp, btmp = st_bufs
    ps, bps = pss
    P.op("act", ACT(sq[:, :, :n], x[:, :, :n], AF.Square), reads=[bx], writes=[bsq])
    for k in range(8):
        P.op("pe", MM(ps[:, :n], ones[:, :], sq[:, k, :n], k == 0, k == 7), reads=[bsq, bones], writes=[bps])
    P.op("dve", TS(rstd[:, :n], ps[:, :n], 1.0 / D, ALU.mult, NORM_EPS, ALU.add), reads=[bps], writes=[brstd])
    P.op("act", ACT(rstd[:, :n], rstd[:, :n], AF.Ln), reads=[brstd], writes=[brstd]); P.op("act", ACT(rstd[:, :n], rstd[:, :n], AF.Exp, scale=-0.5), reads=[brstd], writes=[brstd])
    for k in range(8):
        P.op("dve", STT(tmp[:, k, :n], x[:, k, :n], A[:, l, k, m:m + 1], rstd[:, :n], ALU.mult, ALU.mult),
             reads=[bx, bA, brstd], writes=[btmp])
    for k in range(8):
        P.op("act", ACT(h[:, k, :n], tmp[:, k, :n], AF.Identity, bias=K.mod[:, l, boff + k, m:m + 1], scale=1.0),
             reads=[btmp, K.bmod], writes=[bh])


def stage_A(K, l, xin):
    nc, P, cfg = K.nc, K.P, K.cfg
    segs = []
    for (c0, nco) in ((0, IN_RW), (Z_GDQ, 1152), (Z_GDBA, 24), (Z_GDG, 384), (Z_S5, 256)):
        o = 0
        while o < nco:
            m = min(128, nco - o)
            segs.append((c0 + o, m))
            o += m
    with contextlib.ExitStack() as st:
        W, bW = K.sb(st, [128, 8, IN_TOTAL], BF16, "win", multi=True)
        ones, bones = K.sb(st, [128, 128], F32, "ones")
        P.op("dve", MS(ones[:], 1.0), writes=[bones])
        wst = [K.sb(st, [128, 8, 512], F32, "wst") for _ in range(2)]
        nb = (IN_TOTAL + 511) // 512
        for cb in range(nb):
            w, bw = wst[cb % 2]
            c0 = cb * 512
            m = min(512, IN_TOTAL - c0)
            K.ld(w[:, :, :m], bw, "w_in", K.dram["w_in"][l, :, c0:c0 + m].rearrange("(k p) n -> p k n", p=128))
            if cb % 2:
                P.op("act", ACT(W[:, :, c0:c0 + m], w[:, :, :m], AF.Copy), reads=[bw], writes=[bW])
            else:
                P.op("dve", CP(W[:, :, c0:c0 + m], w[:, :, :m]), reads=[bw], writes=[bW])
        xs = [K.sb(st, [128, 8, 512], F32, "x") for _ in range(2)]
        hs = [K.sb(st, [128, 8, 512], BF16, "h") for _ in range(2)]
        sq, bsq = K.sb(st, [128, 8, 512], F32, "sq")
        tmp, btmp = K.sb(st, [128, 8, 512], F32, "tmp")
        rstd, brstd = K.sb(st, [128, 512], F32, "rstd")
        psn = K.ps(st)
        pz = [K.ps(st) for _ in range(4)]
        zo = [K.sb(st, [128, 512], F32, "zo") for _ in range(4)]
        xT3 = K.dram[xin].rearrange("(k p) t -> p k t", p=128)
        for ti, (t0, n, m) in enumerate(token_tiles(cfg)):
            x, bx = xs[ti % 2]
            h, bh = hs[ti % 2]
            K.ld(x[:, :, :n], bx, xin, xT3[:, :, t0:t0 + n])
            norm_mod(K, (sq, bsq, rstd, brstd, tmp, btmp), x, bx, n, K.A1, K.bA1, 0, l, m, h, bh, ones, bones, psn)
            for si, (c0, mc) in enumerate(segs):
                ps, bps = pz[si % 4]
                z, bz = zo[si % 4]
                for k in range(8):
                    P.op("pe", MM(ps[:mc, :n], W[:, k, c0:c0 + mc], h[:, k, :n], k == 0, k == 7),
                         reads=[bW, bh], writes=[bps])
                if si % 2:
                    P.op("act", ACT(z[:mc, :n], ps[:mc, :n], AF.Copy), reads=[bps], writes=[bz])
                else:
                    P.op("dve", CP(z[:mc, :n], ps[:mc, :n]), reads=[bps], writes=[bz])
                K.stq("Z", K.dram["Z"][c0:c0 + mc, t0:t0 + n], z[:mc, :n], bz)
    P.barrier()


def seg_bounds(cfg, t0):
    C, T = cfg["C"], cfg["T"]
    return (0, C) if t0 < C else (C, C + T)


def load_halo(K, zt, bzt, dname, rows_ap, t0, n, cfg):
    P = K.P
    s0, s1 = seg_bounds(cfg, t0)
    a = max(t0 - 1, s0)
    b = min(t0 + n + 1, s1)
    if a > t0 - 1:
        P.op("dve", MS(zt[:, :, 0:1], 0.0), writes=[bzt])
    if b < t0 + n + 1:
        P.op("dve", MS(zt[:, :, n + 1:n + 2], 0.0), writes=[bzt])
    K.ld(zt[:, :, a - (t0 - 1):b - (t0 - 1)], bzt, dname, rows_ap[:, :, a:b])


def stage_rw_pre(K, l):
    nc, P, cfg = K.nc, K.P, K.cfg
    dr = K.dram
    NEG = -math.exp(-0.5)
    with contextlib.ExitStack() as st:
        mu, bmu = K.sb(st, [128, 11], F32, "mu")
        w0, bw0 = K.sb(st, [128, 2, 3], F32, "w0")
        a0, ba0 = K.sb(st, [128, 2, 3], F32, "a0")
        wup, bwup = K.sb(st, [128, 2, 384], F32, "wup")
        gup, bgup = K.sb(st, [128, 384], F32, "gup")
        vec, bvec = K.sb(st, [128, 5, 3], F32, "vec")
        bones, bbones = K.sb(st, [128, 128], F32, "bones")
        K.ld(mu[:], bmu, "rw_mu", dr["rw_mu"][:, l, :])
        K.ld(w0[:], bw0, "rw_w0", dr["rw_w0"][:, l])
        K.ld(a0[:], ba0, "rw_a0", dr["rw_a0"][:, l])
        K.ld(wup[0:64], bwup, "rw_wup", dr["rw_wup"][:, l])
        K.ld(wup[64:128], bwup, "rw_aup", dr["rw_aup"][:, l])
        K.ld(gup[:], bgup, "rw_gup", dr["rw_gup"][:, l, :])
        K.ld(vec[:, 0, :], bvec, "rw_kk", dr["rw_kk"][:, l, :])
        K.ld(vec[:, 1, :], bvec, "rw_ka", dr["rw_ka"][:, l, :])
        K.ld(vec[:, 2, :], bvec, "rw_rk", dr["rw_rk"][:, l, :])
        K.ld(bones[:], bbones, "c_bones", dr["c_bones"])
        P.op("dve", TS(vec[:, 3, :], vec[:, 1, :], -1.0, ALU.mult, 1.0, ALU.add), reads=[bvec], writes=[bvec])
        zts = [K.sb(st, [128, 11, 514], F32, "zt") for _ in range(2)]
        zms = [K.sb(st, [128, 11, 512], F32, "zm") for _ in range(2)]
        s_, bs_ = K.sb(st, [128, 11, 512], F32, "s")
        tw, btw = K.sb(st, [128, 512], F32, "tw")
        sg, bsg = K.sb(st, [128, 512], F32, "sg")
        NT = 6
        tmps = [K.sb(st, [128, 512], F32, "tmp") for _ in range(NT)]
        outs = [K.sb(st, [128, 512], F32, "out") for _ in range(8)]
        kds = [K.sb(st, [128, 512], F32, "kd") for _ in range(2)]
        pss = [K.ps(st) for _ in range(6)]
        ctr = {"t": 0, "o": 0, "p": 0}

        def T_():
            ctr["t"] += 1
            return tmps[ctr["t"] % NT]

        def O_():
            ctr["o"] += 1
            return outs[ctr["o"] % 8]

        def PS_():
            ctr["p"] += 1
            return pss[ctr["p"] % 6]

        Zr = dr["Z"][0:IN_RW, :].rearrange("(k p) t -> p k t", p=128)
        for ti, (t0, n, m) in enumerate(token_tiles(cfg)):
            zt, bzt = zts[ti % 2]
            zm, bzm = zms[ti % 2]
            load_halo(K, zt, bzt, "Z", Zr, t0, n, cfg)
            zc = zt[:, :, 1:n + 1]
            P.op("dve", TT(s_[:, :, :n], zt[:, :, 0:n], zt[:, :, 2:n + 2], ALU.add), reads=[bzt], writes=[bs_])
            P.op("dve", STT(s_[:, :, :n], s_[:, :, :n], 0.5, zc, ALU.mult, ALU.subtract), reads=[bs_, bzt], writes=[bs_])
            for k in range(11):
                P.op("dve", STT(zm[:, k, :n], s_[:, k, :n], mu[:, k:k + 1], zt[:, k, 1:n + 1], ALU.mult, ALU.add),
                     reads=[bs_, bzt, bmu], writes=[bzm])
            P.op("act", ACT(tw[0:64, :n], zm[0:64, 9, :n], AF.Tanh), reads=[bzm], writes=[btw])
            P.op("act", ACT(sg[:, :n], zm[:, 10, :n], AF.Sigmoid), reads=[bzm], writes=[bsg])
            for c3 in range(3):
                cs = slice(c3 * 128, (c3 + 1) * 128)
                rows = slice(c3 * 128, (c3 + 1) * 128)
                r_ = zm[:, c3, :n]
                k_ = zm[:, 3 + c3, :n]
                v_ = zm[:, 6 + c3, :n]
                K.stq("RW_R", dr["RW_R"][rows, t0:t0 + n], r_, bzm)
                K.stq("RW_V", dr["RW_V"][rows, t0:t0 + n], v_, bzm)
                ps, bps = PS_()
                P.op("pe", MM(ps[:, :n], gup[:, cs], sg[:, :n]), reads=[bgup, bsg], writes=[bps])
                o, bo = O_()
                P.op("act", ACT(o[:, :n], ps[:, :n], AF.Copy), reads=[bps], writes=[bo])
                K.stq("RW_GATE", dr["RW_GATE"][rows, t0:t0 + n], o[:, :n], bo)
                for d in range(2):
                    ps, bps = PS_()
                    P.op("pe", MM(ps[:, :n], wup[0:64, d, cs], tw[0:64, :n]), reads=[bwup, btw], writes=[bps])
                    t1, bt1 = T_()
                    P.op("act", ACT(t1[:, :n], ps[:, :n], AF.Sigmoid, bias=w0[:, d, c3:c3 + 1], scale=1.0),
                         reads=[bps, bw0], writes=[bt1])
                    o, bo = O_()
                    P.op("dve", TS(o[:, :n], t1[:, :n], NEG, ALU.mult), reads=[bt1], writes=[bo])
                    K.stq("RW_LAM", dr["RW_LAM"][d, rows, t0:t0 + n], o[:, :n], bo)
                    ps, bps = PS_()
                    P.op("pe", MM(ps[:, :n], wup[64:128, d, cs], zm[64:128, 9, :n]), reads=[bwup, bzm], writes=[bps])
                    o, bo = O_()
                    P.op("act", ACT(o[:, :n], ps[:, :n], AF.Sigmoid, bias=a0[:, d, c3:c3 + 1], scale=1.0),
                         reads=[bps, ba0], writes=[bo])
                    K.stq("RW_AS", dr["RW_AS"][d, rows, t0:t0 + n], o[:, :n], bo)
                    t1, bt1 = T_()
                    P.op("dve", TS(t1[:, :n], o[:, :n], vec[:, 1, c3:c3 + 1], ALU.mult, vec[:, 3, c3:c3 + 1], ALU.add),
                         reads=[bo, bvec], writes=[bt1])
                    kd, bkd = kds[d]
                    P.op("dve", TT(kd[:, :n], k_, t1[:, :n], ALU.mult), reads=[bzm, bt1], writes=[bkd])
                    K.stq("RW_KD", dr["RW_KD"][d, rows, t0:t0 + n], kd[:, :n], bkd)
                t1, bt1 = T_()
                P.op("dve", TS(t1[:, :n], k_, vec[:, 0, c3:c3 + 1], ALU.mult), reads=[bzm, bvec], writes=[bt1])
                t2, bt2 = T_()
                P.op("act", ACT(t2[:, :n], t1[:, :n], AF.Square), reads=[bt1], writes=[bt2])
                ps, bps = PS_()
                P.op("pe", MM(ps[:, :n], bones[:, :], t2[:, :n]), reads=[bbones, bt2], writes=[bps])
                t3, bt3 = T_()
                P.op("dve", TS(t3[:, :n], ps[:, :n], NORM_EPS, ALU.add), reads=[bps], writes=[bt3])
                P.op("act", ACT(t3[:, :n], t3[:, :n], AF.Ln), reads=[bt3], writes=[bt3]); P.op("act", ACT(t3[:, :n], t3[:, :n], AF.Exp, scale=-0.5), reads=[bt3], writes=[bt3])
                o, bo = O_()
                P.op("dve", TT(o[:, :n], t1[:, :n], t3[:, :n], ALU.mult), reads=[bt1, bt3], writes=[bo])
                K.stq("RW_KK", dr["RW_KK"][rows, t0:t0 + n], o[:, :n], bo)
                t1, bt1 = T_()
                P.op("dve", TT(t1[:, :n], kds[0][0][:, :n], kds[1][0][:, :n], ALU.add),
                     reads=[kds[0][1], kds[1][1]], writes=[bt1])
                t2, bt2 = T_()
                P.op("dve", STT(t2[:, :n], t1[:, :n], vec[:, 2, c3:c3 + 1], r_, ALU.mult, ALU.mult),
                     reads=[bt1, bvec, bzm], writes=[bt2])
                ps, bps = PS_()
                P.op("pe", MM(ps[:, :n], bones[:, :], t2[:, :n]), reads=[bbones, bt2], writes=[bps])
                o, bo = O_()
                P.op("dve", TT(o[:, :n], ps[:, :n], v_, ALU.mult), reads=[bps, bzm], writes=[bo])
                K.stq("RW_BONUS", dr["RW_BONUS"][rows, t0:t0 + n], o[:, :n], bo)
    P.barrier()


class ChunkEnv:
    def __init__(self, K, st):
        self.K = K
        P, dr = K.P, K.dram
        self.ident, self.bident = K.sb(st, [128, 128], F32, "ident")
        self.masks, self.bmasks = K.sb(st, [64, 4, 512], F32, "masks")
        self.negm, self.bnegm = K.sb(st, [64, 4, 512], F32, "negm")
        self.rmask, self.brmask = K.sb(st, [64, 512], F32, "rmask")
        K.ld(self.ident[:], self.bident, "c_ident", dr["c_ident"])
        K.ld(self.masks[:], self.bmasks, "c_masks", dr["c_masks"])
        K.ld(self.negm[:], self.bnegm, "c_negm", dr["c_negm"])
        K.ld(self.rmask[:], self.brmask, "c_rmask", dr["c_rmask"])
        self.bd, self.bbd = K.sb(st, [64, 2, 512], F32, "bdmask")
        K.ld(self.bd[:], self.bbd, "c_bd", dr["c_bd"])
        self.pss = [K.ps(st, (64, 512)) for _ in range(8)]
        self.pi = 0
        self.slot = 0
        self.nslots = 1
        self.pis = [0, 0]
        self.ei = 0
        self.pool = {}
        self.st = st

    def PS(self):
        if self.nslots == 1:
            self.pi += 1
            return self.pss[self.pi % 8]
        self.pis[self.slot] += 1
        return self.pss[self.slot * 4 + self.pis[self.slot] % 4]

    def tile(self, name, shape, nbuf=2):
        name = {"Pp1": "Ep", "Pp2": "Lam", "Pp3": "pre", "TdT": "Epr", "PT1": "Lp"}.get(name, name)
        if self.nslots > 1:
            name = "s%d_%s" % (self.slot, name)
            nbuf = 1
        ent = self.pool.get(name)
        if ent is None:
            ent = [[self.K.sb(self.st, shape, F32, name) for _ in range(nbuf)], 0]
            self.pool[name] = ent
        ent[1] += 1
        return ent[0][ent[1] % nbuf]

    def copy(self, out, bo, in_, bi, extra_reads=()):
        self.ei += 1
        if self.ei % 2:
            self.K.P.op("dve", CP(out, in_), reads=[bi, *extra_reads], writes=[bo])
        else:
            self.K.P.op("act", ACT(out, in_, AF.Copy), reads=[bi, *extra_reads], writes=[bo])
        self.K.P.mark()


def v3(t, nb):
    return t[:, 0:nb * 64].rearrange("p (c t) -> p c t", t=64)


def chunk_template(E, nb, order, ops, H, bH, Yout, bY):
    K = E.K
    P = K.P
    n = nb * 64
    L, bL = ops["L"]
    LT, bLT = ops["LT"]
    MqT, bMq = ops["MqT"]
    X, bX = ops["X0"]
    rF, brF = ops["rF"]
    btT, bbt = ops["btT"]
    decC, bdec = ops["decC"]
    extra = "MrkT" in ops

    Ld, bLd = E.tile("Ld", [64, 512])
    LdT, bLdT = E.tile("LdT", [64, 512])
    LoT, bLoT = E.tile("LoT", [64, 512])
    P.op("dve", TT(Ld[:, :n], L[:, :n], E.bd[:, 0, :n], ALU.mult), reads=[bL, E.bbd], writes=[bLd])
    P.op("dve", TT(LdT[:, :n], LT[:, :n], E.bd[:, 0, :n], ALU.mult), reads=[bLT, E.bbd], writes=[bLdT])
    P.op("dve", TT(LoT[:, :n], LT[:, :n], E.bd[:, 1, :n], ALU.mult), reads=[bLT, E.bbd], writes=[bLoT])
    X0t, bX0t = X, bX
    pws = [(Ld, bLd, LdT, bLdT)]
    for j in range(1, 4):
        Pc, bPc, PTc, bPTc = pws[-1]
        ps, bps = E.PS()
        for c in range(nb):
            cs = slice(c * 64, (c + 1) * 64)
            P.op("pe", MM(ps[:, cs], Pc[:, cs], PTc[:, cs]), reads=[bPc, bPTc], writes=[bps])
        PTn, bPTn = E.tile("PT%d" % j, [64, 512])
        E.copy(PTn[:, :n], bPTn, ps[:, :n], bps)
        ps, bps = E.PS()
        for c in range(nb):
            cs = slice(c * 64, (c + 1) * 64)
            P.op("pe", MM(ps[:, cs], PTc[:, cs], Pc[:, cs]), reads=[bPc, bPTc], writes=[bps])
        Pn, bPn = E.tile("Pp%d" % j, [64, 512])
        E.copy(Pn[:, :n], bPn, ps[:, :n], bps)
        pws.append((Pn, bPn, PTn, bPTn))
    TdT, bTdT = E.tile("TdT", [64, 512])
    P.op("dve", TT(v3(TdT, nb), v3(LdT, nb), E.ident[0:64, 0:64].unsqueeze(1).broadcast_to([64, nb, 64]), ALU.add),
         reads=[bLdT, E.bident], writes=[bTdT])
    P.mark()
    for j in range(1, 4):
        Pj, bPj = pws[j][0], pws[j][1]
        ps, bps = E.PS()
        for c in range(nb):
            cs = slice(c * 64, (c + 1) * 64)
            P.op("pe", MM(ps[:, cs], Pj[:, cs], TdT[:, cs]), reads=[bPj, bTdT], writes=[bps])
        P.op("dve", TT(TdT[:, :n], TdT[:, :n], ps[:, :n], ALU.add), reads=[bps, bTdT], writes=[bTdT])
        P.mark()
    X, bX = E.tile("Xw", [64, 8, 128])
    Yw, bYw = E.tile("Yw", [64, 8, 128])

    def sweep_mm(lt, blt, src, bsrc, dst, bdst, addto=None):
        for half in range(0, nb, 4):
            ps, bps = E.PS()
            hc = min(4, nb - half)
            for c in range(half, half + hc):
                P.op("pe", MM(ps[:, (c - half) * 128:(c - half + 1) * 128], lt[:, c * 64:(c + 1) * 64], src[:, c, :]),
                     reads=[blt, bsrc], writes=[bps])
            pv = ps[:, 0:hc * 128].rearrange("p (c t) -> p c t", t=128)
            if addto is None:
                E.copy(dst[:, half:half + hc, :], bdst, pv, bps)
            else:
                P.op("dve", TT(dst[:, half:half + hc, :], addto[0][:, half:half + hc, :], pv, ALU.add),
                     reads=[bps, addto[1]], writes=[bdst])
                P.mark()

    sweep_mm(TdT, bTdT, X0t, bX0t, X, bX)
    for sweep in range(3):
        sweep_mm(LoT, bLoT, X, bX, Yw, bYw, addto=(X0t, bX0t))
        sweep_mm(TdT, bTdT, Yw, bYw, X, bX)
    QeT, bQe = E.tile("QeT", [64, 512])
    ps, bps = E.PS()
    for c in range(nb):
        cs = slice(c * 64, (c + 1) * 64)
        P.op("pe", MM(ps[:, cs], X[:, c, 0:64], MqT[:, cs]), reads=[bX, bMq], writes=[bps])
    P.op("dve", TT(QeT[:, :n], ps[:, :n], rF[:, :n], ALU.add), reads=[bps, brF], writes=[bQe])
    P.mark()
    YcT, bYc = E.tile("YcT", [64, 512])
    ps, bps = E.PS()
    for c in range(nb):
        cs = slice(c * 64, (c + 1) * 64)
        P.op("pe", MM(ps[:, cs], X[:, c, 64:128], MqT[:, cs], True, not extra), reads=[bX, bMq], writes=[bps])
        if extra:
            P.op("pe", MM(ps[:, cs], ops["vT"][0][:, cs], ops["MrkT"][0][:, cs], False, True),
                 reads=[ops["vT"][1], ops["MrkT"][1]], writes=[bps])
    E.copy(YcT[:, :n], bYc, ps[:, :n], bps)
    AeT, bAe = E.tile("AeT", [64, 512])
    ps, bps = E.PS()
    for c in range(nb):
        cs = slice(c * 64, (c + 1) * 64)
        P.op("pe", MM(ps[:, cs], X[:, c, 0:64], btT[:, cs]), reads=[bX, bbt], writes=[bps])
    dg, bdg = E.tile("dg", [64, 512])
    P.op("dve", TT(v3(dg, nb), E.ident[0:64, 0:64].unsqueeze(1).broadcast_to([64, nb, 64]),
                   decC[:, 0:nb].unsqueeze(2).broadcast_to([64, nb, 64]), ALU.mult),
         reads=[E.bident, bdec], writes=[bdg])
    P.op("dve", TT(AeT[:, :n], ps[:, :n], dg[:, :n], ALU.add), reads=[bps, bdg], writes=[bAe])
    P.mark()
    Hc, bHc = E.tile("Hc", [64, 512])
    ps, bps = E.PS()
    for c in range(nb):
        cs = slice(c * 64, (c + 1) * 64)
        P.op("pe", MM(ps[:, cs], btT[:, cs], X[:, c, 64:128], True, not extra), reads=[bX, bbt], writes=[bps])
        if extra:
            P.op("pe", MM(ps[:, cs], ops["ktT"][0][:, cs], ops["vT"][0][:, cs], False, True),
                 reads=[ops["ktT"][1], ops["vT"][1]], writes=[bps])
    E.copy(Hc[:, :n], bHc, ps[:, :n], bps)
    for c in order:
        cs = slice(c * 64, (c + 1) * 64)
        psy, bpsy = E.PS()
        P.op("pe", MM(psy[:, 0:64], H[:, :], QeT[:, cs]), reads=[bH, bQe], writes=[bpsy])
        psh, bpsh = E.PS()
        P.op("pe", MM(psh[:, 0:64], AeT[:, cs], H[:, :]), reads=[bH, bAe], writes=[bpsh])
        P.op("dve", TT(Yout[:, cs], psy[:, 0:64], YcT[:, cs], ALU.add), reads=[bpsy, bYc], writes=[bY])
        P.op("dve", TT(H[:, :], psh[:, 0:64], Hc[:, cs], ALU.add), reads=[bpsh, bHc], writes=[bH])
        P.mark()


def transpose_batch(E, nb, src, bsrc, dst3, bdst):
    P = E.K.P
    ps, bps = E.PS()
    for c in range(nb):
        cs = slice(c * 64, (c + 1) * 64)
        P.op("pe", MM(ps[:, cs], src[:, cs], E.ident[0:64, 0:64]), reads=[bsrc, E.bident], writes=[bps])
    E.copy(dst3, bdst, ps[:, 0:nb * 64].rearrange("p (c t) -> p c t", t=64), bps)


def stream_batches(cfg, d):
    C, T = cfg["C"], cfg["T"]
    out = []
    for (s0, s1) in ((0, C), (C, C + T)):
        bs = []
        t = s0
        while t < s1:
            nb = min(8, (s1 - t) // 64)
            bs.append((t, nb))
            t += nb * 64
        if d == 1:
            bs = bs[::-1]
        for (t0, nb) in bs:
            out.append((t0, nb, list(range(nb)) if d == 0 else list(range(nb - 1, -1, -1))))
    return out


def rw_frontend(E, l, h, d, t0, nb):
    K = E.K
    P, dr = K.P, K.dram
    n = nb * 64
    rows = slice(h * 64, (h + 1) * 64)
    ld = {}
    for nm, src in (("r", dr["RW_R"][rows, t0:t0 + n]), ("v", dr["RW_V"][rows, t0:t0 + n]),
                    ("kk", dr["RW_KK"][rows, t0:t0 + n]), ("kd", dr["RW_KD"][d, rows, t0:t0 + n]),
                    ("as", dr["RW_AS"][d, rows, t0:t0 + n]), ("lam", dr["RW_LAM"][d, rows, t0:t0 + n])):
        t, b = E.tile("in_" + nm, [64, 512])
        K.ld(t[:, :n], b, "RW_R" if nm == "r" else {"v": "RW_V", "kk": "RW_KK", "kd": "RW_KD", "as": "RW_AS", "lam": "RW_LAM"}[nm], src)
        ld[nm] = (t, b)
    lam, blam = ld["lam"]
    Lm, bLm = E.tile("Lam", [64, 512])
    if d == 0:
        P.op("dve", lambda e, Lm=Lm, lam=lam, n=n: e.tensor_tensor_scan(
            Lm[:, :n], E.rmask[:, :n], lam[:, :n], 0.0, ALU.mult, ALU.add), reads=[blam, E.brmask], writes=[bLm])
        last = 63
    else:
        pre, bpre = E.tile("pre", [64, 512])
        P.op("dve", lambda e, pre=pre, lam=lam, n=n: e.tensor_tensor_scan(
            pre[:, :n], E.rmask[:, :n], lam[:, :n], 0.0, ALU.mult, ALU.add), reads=[blam, E.brmask], writes=[bpre])
        P.op("dve", TT(v3(Lm, nb), v3(pre, nb)[:, :, 63:64].broadcast_to([64, nb, 64]), v3(pre, nb), ALU.subtract),
             reads=[bpre], writes=[bLm])
        P.op("dve", TT(Lm[:, :n], Lm[:, :n], lam[:, :n], ALU.add), reads=[bLm, blam], writes=[bLm])
        last = 0
    Lp, bLp = E.tile("Lp", [64, 512])
    P.op("dve", TT(Lp[:, :n], Lm[:, :n], lam[:, :n], ALU.subtract), reads=[bLm, blam], writes=[bLp])
    Ep, bEp = E.tile("Ep", [64, 512])
    Em, bEm = E.tile("Em", [64, 512])
    Epr, bEpr = E.tile("Epr", [64, 512])
    P.op("act", ACT(Ep[:, :n], Lm[:, :n], AF.Exp), reads=[bLm], writes=[bEp])
    P.op("act", ACT(Em[:, :n], Lm[:, :n], AF.Exp, scale=-1.0), reads=[bLm], writes=[bEm])
    P.op("act", ACT(Epr[:, :n], Lp[:, :n], AF.Exp), reads=[bLp], writes=[bEpr])
    decC, bdec = E.tile("decC", [64, 8])
    P.op("dve", CP(decC[:, 0:nb], v3(Ep, nb)[:, :, last]), reads=[bEp], writes=[bdec])
    decb = decC[:, 0:nb].unsqueeze(2).broadcast_to([64, nb, 64])
    kk, bkk = ld["kk"]
    aF, baF = E.tile("aF", [64, 512])
    P.op("dve", STT(aF[:, :n], kk[:, :n], -1.0, Epr[:, :n], ALU.mult, ALU.mult), reads=[bkk, bEpr], writes=[baF])
    bF, bbF = E.tile("bF", [64, 512])
    P.op("dve", TT(bF[:, :n], kk[:, :n], ld["as"][0][:, :n], ALU.mult), reads=[bkk, ld["as"][1]], writes=[bbF])
    P.op("dve", TT(bF[:, :n], bF[:, :n], Em[:, :n], ALU.mult), reads=[bbF, bEm], writes=[bbF])
    kF, bkF = E.tile("kF", [64, 512])
    P.op("dve", TT(kF[:, :n], ld["kd"][0][:, :n], Em[:, :n], ALU.mult), reads=[ld["kd"][1], bEm], writes=[bkF])
    rF, brF = E.tile("rF", [64, 512])
    P.op("dve", TT(rF[:, :n], ld["r"][0][:, :n], Ep[:, :n], ALU.mult), reads=[ld["r"][1], bEp], writes=[brF])
    btF, bbtF = E.tile("btF", [64, 512])
    P.op("dve", TT(v3(btF, nb), v3(bF, nb), decb, ALU.mult), reads=[bbF, bdec], writes=[bbtF])
    ktF, bktF = E.tile("ktF", [64, 512])
    P.op("dve", TT(v3(ktF, nb), v3(kF, nb), decb, ALU.mult), reads=[bkF, bdec], writes=[bktF])
    X0, bX0 = E.tile("X0", [64, 8, 128])
    transpose_batch(E, nb, aF, baF, X0[:, 0:nb, 0:64], bX0)
    btT, bbtT = E.tile("btT", [64, 512])
    transpose_batch(E, nb, btF, bbtF, v3(btT, nb), bbtT)
    ktT, bktT = E.tile("ktT", [64, 512])
    transpose_batch(E, nb, ktF, bktF, v3(ktT, nb), bktT)
    vT, bvT = E.tile("vT", [64, 512])
    transpose_batch(E, nb, ld["v"][0], ld["v"][1], v3(vT, nb), bvT)
    mL, mLT, mM = (0, 1, 3) if d == 0 else (1, 0, 2)

    def pair(name, lt, blt, rt, brt, mi):
        ps, bps = E.PS()
        for c in range(nb):
            cs = slice(c * 64, (c + 1) * 64)
            P.op("pe", MM(ps[:, cs], lt[:, cs], rt[:, cs]), reads=[blt, brt], writes=[bps])
        o, bo = E.tile(name, [64, 512])
        P.op("dve", TT(o[:, :n], ps[:, :n], E.masks[:, mi, :n], ALU.mult), reads=[bps, E.bmasks], writes=[bo])
        P.mark()
        return o, bo

    L_ = pair("L", aF, baF, bF, bbF, mL)
    LT_ = pair("LT", bF, bbF, aF, baF, mLT)
    LakT = pair("LakT", kF, bkF, aF, baF, mLT)
    MqT = pair("MqT", bF, bbF, rF, brF, mM)
    MrkT = pair("MrkT", kF, bkF, rF, brF, mM)
    ps, bps = E.PS()
    for c in range(nb):
        cs = slice(c * 64, (c + 1) * 64)
        P.op("pe", MM(ps[:, cs], LakT[0][:, cs], vT[:, cs]), reads=[LakT[1], bvT], writes=[bps])
    E.copy(X0[:, 0:nb, 64:128], bX0, ps[:, 0:n].rearrange("p (c t) -> p c t", t=64), bps)
    return {"L": L_, "LT": LT_, "MqT": MqT, "X0": (X0, bX0), "rF": (rF, brF), "btT": (btT, bbtT),
            "decC": (decC, bdec), "MrkT": MrkT, "ktT": (ktT, bktT), "vT": (vT, bvT)}


def stage_rw_scan(K, l):
    P, cfg, dr = K.P, K.cfg, K.dram
    with contextlib.ExitStack() as st:
        E = ChunkEnv(K, st)
        for h in range(6):
            for d in range(2):
                H, bH = E.tile("H", [64, 64])
                P.op("dve", MS(H[:, :], 0.0), writes=[bH])
                for (t0, nb, order) in stream_batches(cfg, d):
                    ops = rw_frontend(E, l, h, d, t0, nb)
                    Y, bY = E.tile("Y", [64, 512])
                    chunk_template(E, nb, order, ops, H, bH, Y, bY)
                    K.stq("YRW", dr["YRW"][d, h * 64:(h + 1) * 64, t0:t0 + nb * 64], Y[:, :nb * 64], bY)
    P.barrier()


def stage_gd_pre(K, l):
    P, cfg, dr = K.P, K.cfg, K.dram
    with contextlib.ExitStack() as st:
        cw, bcw = K.sb(st, [128, 3, 9], F32, "convw")
        bones, bbones = K.sb(st, [128, 128], F32, "bones")
        sel, bsel = K.sb(st, [12, 2, 3, 128], F32, "sel")
        prm, bprm = K.sb(st, [12, 4], F32, "gdprm")
        K.ld(cw[:], bcw, "gd_conv", dr["gd_conv"][:, l])
        K.ld(bones[:], bbones, "c_bones", dr["c_bones"])
        K.ld(sel[:], bsel, "c_sel", dr["c_sel"])
        K.ld(prm[:, 0:2], bprm, "gd_prm", dr["gd_prm"][:, l, :])
        P.op("act", ACT(prm[:, 2:3], prm[:, 0:1], AF.Exp), reads=[bprm], writes=[bprm])
        P.op("dve", TS(prm[:, 2:3], prm[:, 2:3], -1.0, ALU.mult), reads=[bprm], writes=[bprm])
        zts = [K.sb(st, [128, 9, 514], F32, "zt") for _ in range(2)]
        tmps = [K.sb(st, [128, 512], F32, "tmp") for _ in range(6)]
        outs = [K.sb(st, [128, 512], F32, "out") for _ in range(6)]
        bas = [K.sb(st, [12, 2, 512], F32, "ba") for _ in range(2)]
        pss = [K.ps(st) for _ in range(4)]
        ctr = {"t": 0, "o": 0, "p": 0}

        def T_():
            ctr["t"] += 1
            return tmps[ctr["t"] % 6]

        def O_():
            ctr["o"] += 1
            return outs[ctr["o"] % 6]

        def PS_():
            ctr["p"] += 1
            return pss[ctr["p"] % 4]

        Zr = dr["Z"][Z_GDQ:Z_GDQ + 1152, :].rearrange("(k p) t -> p k t", p=128)
        for ti, (t0, n, m) in enumerate(token_tiles(cfg)):
            zt, bzt = zts[ti % 2]
            load_halo(K, zt, bzt, "Z", Zr, t0, n, cfg)
            ba, bba = bas[ti % 2]
            K.ld(ba[:, 0, :n], bba, "Z", dr["Z"][Z_GDBA:Z_GDBA + 12, t0:t0 + n])
            K.ld(ba[:, 1, :n], bba, "Z", dr["Z"][Z_GDBA + 12:Z_GDBA + 24, t0:t0 + n])
            P.op("act", ACT(ba[:, 0, :n], ba[:, 0, :n], AF.Sigmoid), reads=[bba], writes=[bba])
            P.op("act", ACT(ba[:, 1, :n], ba[:, 1, :n], AF.Exp, bias=prm[:, 1:2], scale=1.0), reads=[bba, bprm], writes=[bba])
            P.op("act", ACT(ba[:, 1, :n], ba[:, 1, :n], AF.Ln, bias=1.0, scale=1.0), reads=[bba], writes=[bba])
            P.op("dve", TS(ba[:, 1, :n], ba[:, 1, :n], prm[:, 2:3], ALU.mult), reads=[bba, bprm], writes=[bba])
            for d in range(2):
                for c3 in range(3):
                    rows = slice(c3 * 128, (c3 + 1) * 128)
                    for which, nm in ((0, "GD_BETA"), (1, "GD_G")):
                        ps, bps = PS_()
                        P.op("pe", MM(ps[:, :n], sel[:, d, c3, :], ba[:, which, :n]), reads=[bsel, bba], writes=[bps])
                        o, bo = O_()
                        P.op("act", ACT(o[:, :n], ps[:, :n], AF.Copy), reads=[bps], writes=[bo])
                        K.stq(nm, dr[nm][d, rows, t0:t0 + n], o[:, :n], bo)
            for k in range(9):
                c3 = k % 3
                rows = slice(c3 * 128, (c3 + 1) * 128)
                t1, bt1 = T_()
                P.op("dve", TS(t1[:, :n], zt[:, k, 0:n], cw[:, 0, k:k + 1], ALU.mult), reads=[bzt, bcw], writes=[bt1])
                P.op("dve", STT(t1[:, :n], zt[:, k, 1:n + 1], cw[:, 1, k:k + 1], t1[:, :n], ALU.mult, ALU.add),
                     reads=[bzt, bcw, bt1], writes=[bt1])
                P.op("dve", STT(t1[:, :n], zt[:, k, 2:n + 2], cw[:, 2, k:k + 1], t1[:, :n], ALU.mult, ALU.add),
                     reads=[bzt, bcw, bt1], writes=[bt1])
                if k >= 6:
                    o, bo = O_()
                    P.op("act", ACT(o[:, :n], t1[:, :n], AF.Silu), reads=[bt1], writes=[bo])
                    K.stq("GD_V", dr["GD_V"][rows, t0:t0 + n], o[:, :n], bo)
                    continue
                t2, bt2 = T_()
                P.op("act", ACT(t2[:, :n], t1[:, :n], AF.Silu), reads=[bt1], writes=[bt2])
                t3, bt3 = T_()
                P.op("act", ACT(t3[:, :n], t2[:, :n], AF.Square), reads=[bt2], writes=[bt3])
                ps, bps = PS_()
                P.op("pe", MM(ps[:, :n], bones[:, :], t3[:, :n]), reads=[bbones, bt3], writes=[bps])
                P.op("dve", TS(t3[:, :n], ps[:, :n], NORM_EPS, ALU.add), reads=[bps], writes=[bt3])
                P.op("act", ACT(t3[:, :n], t3[:, :n], AF.Ln), reads=[bt3], writes=[bt3]); P.op("act", ACT(t3[:, :n], t3[:, :n], AF.Exp, scale=-0.5), reads=[bt3], writes=[bt3])
                o, bo = O_()
                if k < 3:
                    P.op("dve", STT(o[:, :n], t2[:, :n], 0.125, t3[:, :n], ALU.mult, ALU.mult), reads=[bt2, bt3], writes=[bo])
                    K.stq("GD_Q", dr["GD_Q"][rows, t0:t0 + n], o[:, :n], bo)
                else:
                    P.op("dve", TT(o[:, :n], t2[:, :n], t3[:, :n], ALU.mult), reads=[bt2, bt3], writes=[bo])
                    K.stq("GD_K", dr["GD_K"][rows, t0:t0 + n], o[:, :n], bo)
    P.barrier()


def cumsum_chunks(E, d, nb, lam, blam):
    P = E.K.P
    n = nb * 64
    Lm, bLm = E.tile("Lam", [64, 512])
    if d == 0:
        P.op("dve", lambda e: e.tensor_tensor_scan(Lm[:, :n], E.rmask[:, :n], lam[:, :n], 0.0, ALU.mult, ALU.add),
             reads=[blam, E.brmask], writes=[bLm])
        return Lm, bLm, 63
    pre, bpre = E.tile("pre", [64, 512])
    P.op("dve", lambda e: e.tensor_tensor_scan(pre[:, :n], E.rmask[:, :n], lam[:, :n], 0.0, ALU.mult, ALU.add),
         reads=[blam, E.brmask], writes=[bpre])
    P.op("dve", TT(v3(Lm, nb), v3(pre, nb)[:, :, 63:64].broadcast_to([64, nb, 64]), v3(pre, nb), ALU.subtract),
         reads=[bpre], writes=[bLm])
    P.op("dve", TT(Lm[:, :n], Lm[:, :n], lam[:, :n], ALU.add), reads=[bLm, blam], writes=[bLm])
    return Lm, bLm, 0


def gd_frontend(E, l, h, d, t0, nb):
    K = E.K
    P, dr = K.P, K.dram
    n = nb * 64
    rows = slice(h * 64, (h + 1) * 64)
    ld = {}
    for nm, dn, src in (("q", "GD_Q", dr["GD_Q"][rows, t0:t0 + n]), ("k", "GD_K", dr["GD_K"][rows, t0:t0 + n]),
                        ("v", "GD_V", dr["GD_V"][rows, t0:t0 + n]), ("be", "GD_BETA", dr["GD_BETA"][d, rows, t0:t0 + n]),
                        ("g", "GD_G", dr["GD_G"][d, rows, t0:t0 + n])):
        t, b = E.tile("in_" + nm, [64, 512])
        K.ld(t[:, :n], b, dn, src)
        ld[nm] = (t, b)
    qF, bqF = ld["q"]
    kF, bkF = ld["k"]
    be, bbe = ld["be"]
    Lm, bLm, last = cumsum_chunks(E, d, nb, ld["g"][0], ld["g"][1])
    Ep, bEp = E.tile("Ep", [64, 512])
    P.op("act", ACT(Ep[:, :n], Lm[:, :n], AF.Exp), reads=[bLm], writes=[bEp])
    decC, bdec = E.tile("decC", [64, 8])
    P.op("dve", CP(decC[:, 0:nb], v3(Ep, nb)[:, :, last]), reads=[bEp], writes=[bdec])
    rF, brF = E.tile("rF", [64, 512])
    P.op("dve", TT(rF[:, :n], qF[:, :n], Ep[:, :n], ALU.mult), reads=[bqF, bEp], writes=[brF])
    Et, bEt = E.tile("Et", [64, 512])
    P.op("dve", TT(v3(Et, nb), v3(Lm, nb)[:, :, last:last + 1].broadcast_to([64, nb, 64]), v3(Lm, nb), ALU.subtract),
         reads=[bLm], writes=[bEt])
    P.op("act", ACT(Et[:, :n], Et[:, :n], AF.Exp), reads=[bEt], writes=[bEt])
    btF, bbtF = E.tile("btF", [64, 512])
    P.op("dve", TT(btF[:, :n], kF[:, :n], Et[:, :n], ALU.mult), reads=[bkF, bEt], writes=[bbtF])
    btT, bbtT = E.tile("btT", [64, 512])
    transpose_batch(E, nb, btF, bbtF, v3(btT, nb), bbtT)
    kT, bkT = E.tile("ktT", [64, 512])
    transpose_batch(E, nb, kF, bkF, v3(kT, nb), bkT)
    vT, bvT = E.tile("vT", [64, 512])
    transpose_batch(E, nb, ld["v"][0], ld["v"][1], v3(vT, nb), bvT)
    GT, bGT = E.tile("GT", [64, 512])
    transpose_batch(E, nb, Lm, bLm, v3(GT, nb), bGT)
    bT, bbT = E.tile("bT", [64, 512])
    transpose_batch(E, nb, be, bbe, v3(bT, nb), bbT)
    X0, bX0 = E.tile("X0", [64, 8, 128])
    eg, beg = E.tile("eg", [64, 512])
    P.op("act", ACT(eg[:, :n], GT[:, :n], AF.Exp), reads=[bGT], writes=[beg])
    P.op("dve", TT(eg[:, :n], eg[:, :n], bT[:, :n], ALU.mult), reads=[beg, bbT], writes=[beg])
    P.op("dve", STT(X0[:, 0:nb, 0:64], v3(kT, nb), -1.0, v3(eg, nb), ALU.mult, ALU.mult), reads=[bkT, beg], writes=[bX0])
    P.op("dve", TT(X0[:, 0:nb, 64:128], v3(vT, nb), v3(bT, nb), ALU.mult), reads=[bvT, bbT], writes=[bX0])
    mL, mLT, mM = (0, 1, 3) if d == 0 else (1, 0, 2)
    Dx, bDx = E.tile("Dx", [64, 512])
    P.op("dve", TT(Dx[:, :n], GT[:, :n], Lm[:, :n], ALU.subtract), reads=[bGT, bLm], writes=[bDx])
    D1, bD1 = E.tile("D1", [64, 512])
    P.op("dve", TT(D1[:, :n], Dx[:, :n], E.negm[:, mL, :n], ALU.min), reads=[bDx, E.bnegm], writes=[bD1])
    P.op("act", ACT(D1[:, :n], D1[:, :n], AF.Exp), reads=[bD1], writes=[bD1])
    P.op("dve", TT(D1[:, :n], D1[:, :n], bT[:, :n], ALU.mult), reads=[bD1, bbT], writes=[bD1])
    D2, bD2 = E.tile("D2", [64, 512])
    P.op("dve", STT(D2[:, :n], Dx[:, :n], -1.0, E.negm[:, mLT, :n], ALU.mult, ALU.min), reads=[bDx, E.bnegm], writes=[bD2])
    P.op("act", ACT(D2[:, :n], D2[:, :n], AF.Exp), reads=[bD2], writes=[bD2])
    P.op("dve", TT(D2[:, :n], D2[:, :n], be[:, :n], ALU.mult), reads=[bD2, bbe], writes=[bD2])
    D3, bD3 = E.tile("D3", [64, 512])
    P.op("dve", STT(D3[:, :n], Dx[:, :n], -1.0, E.negm[:, mM, :n], ALU.mult, ALU.min), reads=[bDx, E.bnegm], writes=[bD3])
    P.op("act", ACT(D3[:, :n], D3[:, :n], AF.Exp), reads=[bD3], writes=[bD3])
    ps, bps = E.PS()
    for c in range(nb):
        cs = slice(c * 64, (c + 1) * 64)
        P.op("pe", MM(ps[:, cs], kF[:, cs], kF[:, cs]), reads=[bkF], writes=[bps])
    L_, bL_ = E.tile("L", [64, 512])
    P.op("dve", STT(L_[:, :n], ps[:, :n], -1.0, D1[:, :n], ALU.mult, ALU.mult), reads=[bps, bD1], writes=[bL_])
    LT_, bLT_ = E.tile("LT", [64, 512])
    P.op("dve", STT(LT_[:, :n], ps[:, :n], -1.0, D2[:, :n], ALU.mult, ALU.mult), reads=[bps, bD2], writes=[bLT_])
    P.mark()
    ps, bps = E.PS()
    for c in range(nb):
        cs = slice(c * 64, (c + 1) * 64)
        P.op("pe", MM(ps[:, cs], kF[:, cs], qF[:, cs]), reads=[bkF, bqF], writes=[bps])
    Mq, bMq = E.tile("MqT", [64, 512])
    P.op("dve", TT(Mq[:, :n], ps[:, :n], D3[:, :n], ALU.mult), reads=[bps, bD3], writes=[bMq])
    P.mark()
    return {"L": (L_, bL_), "LT": (LT_, bLT_), "MqT": (Mq, bMq), "X0": (X0, bX0), "rF": (rF, brF),
            "btT": (btT, bbtT), "decC": (decC, bdec)}


def stage_scan(K, l, which):
    cfg, dr = K.cfg, K.dram
    fe = rw_frontend if which == "rw" else gd_frontend
    yn = "YRW" if which == "rw" else "YGD"
    realP = K.P
    with contextlib.ExitStack() as st:
        E = ChunkEnv(K, st)
        E.nslots = 2
        for h in range(6):
            recs = []
            for d in range(2):
                E.slot = d
                K.P = Rec()
                P = K.P
                H, bH = E.tile("H", [64, 64])
                P.op("dve", MS(H[:, :], 0.0), writes=[bH])
                for (t0, nb, order) in stream_batches(cfg, d):
                    ops = fe(E, l, h, d, t0, nb)
                    Y, bY = E.tile("Y", [64, 512])
                    chunk_template(E, nb, order, ops, H, bH, Y, bY)
                    K.stq(yn, dr[yn][d, h * 64:(h + 1) * 64, t0:t0 + nb * 64], Y[:, :nb * 64], bY)
                    P.mark()
                recs.append(K.P)
            K.P = realP
            replay_interleaved(realP, recs)
    K.P = realP
    realP.barrier()


def s5_pieces(cfg):
    T = cfg["T"]
    npc = cfg.get("s5_pieces", max(1, T // 4096))
    wpp = 64 // npc
    return [(i * wpp, wpp) for i in range(npc)]


def stage_s5(K, l):
    P, cfg, dr = K.P, K.cfg, K.dram
    C, T = cfg["C"], cfg["T"]
    R = T // 64
    PI = math.pi
    with contextlib.ExitStack() as st:
        ident, bident = K.sb(st, [128, 128], F32, "ident")
        K.ld(ident[:], bident, "c_ident", dr["c_ident"])
        prm, bprm = K.sb(st, [128, 3, 16], F32, "s5prm")
        K.ld(prm[:], bprm, "s5_prm", dr["s5_prm"][:, l])
        Bp, bBp = K.sb(st, [128, 2, 16, 32], F32, "s5B")
        K.ld(Bp[:], bBp, "s5_B", dr["s5_B"][:, l])
        Cp, bCp = K.sb(st, [128, 2, 16, 32], F32, "s5C")
        K.ld(Cp[:], bCp, "s5_C", dr["s5_C"][:, l])
        P.op("dve", TS(Cp[:, 1], Cp[:, 1], -1.0, ALU.mult), reads=[bCp], writes=[bCp])
        w, bw = K.sb(st, [128, 12, 16], F32, "s5w")
        hpi, bhpi = K.sb(st, [128, 1], F32, "hpi")
        P.op("dve", MS(hpi[:], PI / 2), writes=[bhpi])
        lr, li, ldt = prm[:, 0, :], prm[:, 1, :], prm[:, 2, :]
        dt, xr, xi, mag, sn, cs_, abr, abi, den, fr, fi, t0_ = [w[:, i, :] for i in range(12)]

        def dv(fn):
            P.op("dve", fn, reads=[bw, bprm], writes=[bw])

        def ac(fn):
            P.op("act", fn, reads=[bw, bprm, bhpi], writes=[bw])

        ac(ACT(dt, ldt, AF.Exp))
        dv(TT(xr, lr, dt, ALU.mult))
        dv(TT(xi, li, dt, ALU.mult))
        ac(ACT(mag, xr, AF.Exp))
        ac(ACT(sn, xi, AF.Sin, scale=1.0 / 16))
        ac(ACT(cs_, xi, AF.Sin, bias=hpi[:, 0:1], scale=1.0 / 16))
        for _ in range(4):
            dv(TT(t0_, cs_, sn, ALU.mult))
            dv(TT(cs_, cs_, cs_, ALU.mult))
            dv(TT(sn, sn, sn, ALU.mult))
            dv(TT(cs_, cs_, sn, ALU.subtract))
            dv(TS(sn, t0_, 2.0, ALU.mult))
        dv(TT(abr, mag, cs_, ALU.mult))
        dv(TT(abi, mag, sn, ALU.mult))
        dv(TT(den, lr, lr, ALU.mult))
        dv(TT(t0_, li, li, ALU.mult))
        dv(TT(den, den, t0_, ALU.add))
        P.op("dve", lambda e: e.reciprocal(den, den), reads=[bw], writes=[bw])
        dv(TS(t0_, abr, -1.0, ALU.add))
        dv(TT(fr, t0_, lr, ALU.mult))
        dv(TT(xr, abi, li, ALU.mult))
        dv(TT(fr, fr, xr, ALU.add))
        dv(TT(fr, fr, den, ALU.mult))
        dv(TT(fi, abi, lr, ALU.mult))
        dv(TT(xr, t0_, li, ALU.mult))
        dv(TT(fi, fi, xr, ALU.subtract))
        dv(TT(fi, fi, den, ALU.mult))
        bb, bbb = K.sb(st, [128, 2, 16, 32], F32, "s5bb")
        tmpb, btmpb = K.sb(st, [128, 16, 32], F32, "s5tb")
        frb = w[:, 9, :].unsqueeze(2).broadcast_to([128, 16, 32])
        fib = w[:, 10, :].unsqueeze(2).broadcast_to([128, 16, 32])
        P.op("dve", TT(bb[:, 0], Bp[:, 0], frb, ALU.mult), reads=[bBp, bw], writes=[bbb])
        P.op("dve", TT(tmpb[:], Bp[:, 1], fib, ALU.mult), reads=[bBp, bw], writes=[btmpb])
        P.op("dve", TT(bb[:, 0], bb[:, 0], tmpb[:], ALU.subtract), reads=[bbb, btmpb], writes=[bbb])
        P.op("dve", TT(bb[:, 1], Bp[:, 1], frb, ALU.mult), reads=[bBp, bw], writes=[bbb])
        P.op("dve", TT(tmpb[:], Bp[:, 0], fib, ALU.mult), reads=[bBp, bw, btmpb], writes=[btmpb])
        P.op("dve", TT(bb[:, 1], bb[:, 1], tmpb[:], ALU.add), reads=[bbb, btmpb], writes=[bbb])
        bbT, bbbT = K.sb(st, [32, 2, 16, 128], F32, "s5bbT", multi=True)
        pst = [K.ps(st) for _ in range(2)]
        it = 0
        for ri in range(2):
            for q4 in range(4):
                ps, bps = pst[it % 2]
                it += 1
                for j in range(4):
                    P.op("pe", MM(ps[0:32, j * 128:(j + 1) * 128], bb[:, ri, q4 * 4 + j, :], ident[:, :]),
                         reads=[bbb, bident], writes=[bps])
                P.op("act", ACT(bbT[:, ri, q4 * 4:q4 * 4 + 4, :], ps[0:32, :].rearrange("p (j t) -> p j t", t=128), AF.Copy),
                     reads=[bps], writes=[bbbT])
        segs = [("ctx", 0, C)]
        pcs = s5_pieces(cfg)
        maxn = max(C, pcs[0][1] * R)
        NLV = max(1, (maxn - 1).bit_length())
        pw, bpw = K.sb(st, [128, 16, NLV + 1, 3], F32, "s5pw")
        P.op("dve", CP(pw[:, :, 0, 0], abr), reads=[bw], writes=[bpw])
        P.op("dve", CP(pw[:, :, 0, 1], abi), reads=[bw, bpw], writes=[bpw])
        sq1, bsq1 = K.sb(st, [128, 16, 2], F32, "s5sq")
        for lv in range(NLV):
            P.op("dve", TT(sq1[:, :, 0], pw[:, :, lv, 0], pw[:, :, lv, 0], ALU.mult), reads=[bpw], writes=[bsq1])
            P.op("dve", TT(sq1[:, :, 1], pw[:, :, lv, 1], pw[:, :, lv, 1], ALU.mult), reads=[bpw, bsq1], writes=[bsq1])
            P.op("dve", TT(pw[:, :, lv + 1, 0], sq1[:, :, 0], sq1[:, :, 1], ALU.subtract), reads=[bsq1, bpw], writes=[bpw])
            P.op("dve", STT(pw[:, :, lv + 1, 1], pw[:, :, lv, 0], 2.0, pw[:, :, lv, 1], ALU.mult, ALU.mult),
                 reads=[bpw], writes=[bpw])
        P.op("dve", TS(pw[:, :, :, 2], pw[:, :, :, 1], -1.0, ALU.mult), reads=[bpw], writes=[bpw])
        u, bu = K.sb(st, [32, C + T], F32, "s5u")
        yr, byr = K.sb(st, [32, C + T], F32, "s5yr", multi=True)
        NB = maxn
        bufs = [K.sb(st, [128, NB], F32, "s5h") for _ in range(4)]
        hp, bhp = K.sb(st, [128, 4], F32, "s5hp")
        psx = [K.ps(st) for _ in range(4)]
        pix = [0]

        def PSX():
            pix[0] += 1
            return psx[pix[0] % 4]

        ulat = u[:, C:C + T].rearrange("p (r w) -> p w r", w=64)
        yrlat = yr[:, C:C + T].rearrange("p (r w) -> p w r", w=64)
        for gp in range(8):
            K.ld(u[:, :], bu, "Z", dr["Z"][Z_S5 + gp * 32:Z_S5 + gp * 32 + 32, :])
            for d in range(2):
                q = d * 8 + gp
                seglist = [("ctx", 0, 0)] + [("lat", w0, wn) for (w0, wn) in (pcs if d == 0 else pcs[::-1])]
                first = True
                for (kind, w0, wn) in seglist:
                    n = C if kind == "ctx" else wn * R
                    (re_, bre), (im_, bim), (re2, bre2), (im2, bim2) = bufs
                    blk = 512 if kind == "ctx" else max(1, 512 // R) * R
                    c0 = 0
                    while c0 < n:
                        m = min(blk, n - c0)
                        if kind == "ctx":
                            rhs = u[:, c0:c0 + m]
                        else:
                            rhs = ulat[:, w0 + c0 // R:w0 + (c0 + m) // R, :]
                        for ri, (dst, bdst) in enumerate(((re_, bre), (im_, bim))):
                            ps, bps = PSX()
                            P.op("pe", MM(ps[:, :m], bbT[:, ri, q, :], rhs), reads=[bbbT, bu], writes=[bps])
                            if ri:
                                P.op("act", ACT(dst[:, c0:c0 + m], ps[:, :m], AF.Copy), reads=[bps], writes=[bdst])
                            else:
                                P.op("dve", CP(dst[:, c0:c0 + m], ps[:, :m]), reads=[bps], writes=[bdst])
                        c0 += m
                    e0 = 0 if d == 0 else n - 1
                    e1 = n - 1 if d == 0 else 0
                    if not first:
                        ar, ai, nai = pw[:, q, 0, 0:1], pw[:, q, 0, 1:2], pw[:, q, 0, 2:3]
                        P.op("dve", STT(re_[:, e0:e0 + 1], hp[:, 0:1], ar, re_[:, e0:e0 + 1], ALU.mult, ALU.add),
                             reads=[bhp, bpw, bre], writes=[bre])
                        P.op("dve", STT(re_[:, e0:e0 + 1], hp[:, 1:2], nai, re_[:, e0:e0 + 1], ALU.mult, ALU.add),
                             reads=[bhp, bpw, bre], writes=[bre])
                        P.op("dve", STT(im_[:, e0:e0 + 1], hp[:, 1:2], ar, im_[:, e0:e0 + 1], ALU.mult, ALU.add),
                             reads=[bhp, bpw, bim], writes=[bim])
                        P.op("dve", STT(im_[:, e0:e0 + 1], hp[:, 0:1], ai, im_[:, e0:e0 + 1], ALU.mult, ALU.add),
                             reads=[bhp, bpw, bim], writes=[bim])
                    first = False
                    cur = ((re_, bre), (im_, bim))
                    nxt = ((re2, bre2), (im2, bim2))
                    sft = 1
                    lv = 0
                    while sft < n:
                        (cr, bcr), (ci, bci) = cur
                        (nr, bnr), (ni, bni) = nxt
                        ar, ai, nai = pw[:, q, lv, 0:1], pw[:, q, lv, 1:2], pw[:, q, lv, 2:3]
                        if d == 0:
                            dst, src, keep = slice(sft, n), slice(0, n - sft), slice(0, sft)
                        else:
                            dst, src, keep = slice(0, n - sft), slice(sft, n), slice(n - sft, n)
                        P.op("dve", STT(nr[:, dst], cr[:, src], ar, cr[:, dst], ALU.mult, ALU.add), reads=[bcr, bpw], writes=[bnr])
                        P.op("dve", STT(nr[:, dst], ci[:, src], nai, nr[:, dst], ALU.mult, ALU.add), reads=[bci, bpw, bnr], writes=[bnr])
                        P.op("dve", CP(nr[:, keep], cr[:, keep]), reads=[bcr, bnr], writes=[bnr])
                        P.op("dve", STT(ni[:, dst], ci[:, src], ar, ci[:, dst], ALU.mult, ALU.add), reads=[bci, bpw], writes=[bni])
                        P.op("dve", STT(ni[:, dst], cr[:, src], ai, ni[:, dst], ALU.mult, ALU.add), reads=[bcr, bpw, bni], writes=[bni])
                        P.op("pool", CP(ni[:, keep], ci[:, keep]), reads=[bci, bni], writes=[bni])
                        cur, nxt = nxt, cur
                        sft *= 2
                        lv += 1
                    (cr, bcr), (ci, bci) = cur
                    P.op("dve", CP(hp[:, 0:1], cr[:, e1:e1 + 1]), reads=[bcr], writes=[bhp])
                    P.op("dve", CP(hp[:, 1:2], ci[:, e1:e1 + 1]), reads=[bci, bhp], writes=[bhp])
                    c0 = 0
                    while c0 < n:
                        m = min(blk, n - c0)
                        ps, bps = PSX()
                        P.op("pe", MM(ps[0:32, :m], Cp[:, 0, q, :], cr[:, c0:c0 + m], True, False), reads=[bCp, bcr], writes=[bps])
                        P.op("pe", MM(ps[0:32, :m], Cp[:, 1, q, :], ci[:, c0:c0 + m], False, True), reads=[bCp, bci], writes=[bps])
                        if kind == "ctx":
                            P.op("act", ACT(yr[:, c0:c0 + m], ps[0:32, :m], AF.Copy), reads=[bps], writes=[byr])
                        else:
                            P.op("act", ACT(yrlat[:, w0 + c0 // R:w0 + (c0 + m) // R, :],
                                            ps[0:32, :m].rearrange("p (w r) -> p w r", r=R), AF.Copy), reads=[bps], writes=[byr])
                        c0 += m
                K.stq("YS5", dr["YS5"][d, gp * 32:(gp + 1) * 32, :], yr[:, :], byr)
    P.barrier()


def stage_post(K, l):
    P, cfg, dr = K.P, K.cfg, K.dram
    GC = math.sqrt(2.0 / math.pi)
    with contextlib.ExitStack() as st:
        bavg, bbavg = K.sb(st, [128, 128], F32, "bavg")
        K.ld(bavg[:], bbavg, "c_bones", dr["c_bones"])
        P.op("dve", TS(bavg[:], bavg[:], 1.0 / 64, ALU.mult), reads=[bbavg], writes=[bbavg])
        vec, bvec = K.sb(st, [128, 8, 3], F32, "pvec")
        K.ld(vec[:, 0, :], bvec, "rw_lnx_g", dr["rw_lnx_g"][:, l, :])
        K.ld(vec[:, 1, :], bvec, "rw_lnx_b", dr["rw_lnx_b"][:, l, :])
        K.ld(vec[:, 2, 0:1], bvec, "gd_ng", dr["gd_ng"][l])
        K.ld(vec[:, 3, 0:2], bvec, "s5_d", dr["s5_d"][:, l, :])
        K.ld(vec[:, 4, 0:2], bvec, "s5_glu_b", dr["s5_glu_b"][:, l, :])
        gw, bgw = K.sb(st, [128, 2, 256], F32, "gluw")
        K.ld(gw[:], bgw, "s5_glu_w", dr["s5_glu_w"][:, l])
        NI = 8
        ins = [K.sb(st, [128, 512], F32, "pin") for _ in range(NI)]
        tmps = [K.sb(st, [128, 512], F32, "ptmp") for _ in range(8)]
        outs = [K.sb(st, [128, 512], F32, "pout") for _ in range(4)]
        gl, bgl = K.sb(st, [128, 2, 512], F32, "gel")
        pss = [K.ps(st) for _ in range(4)]
        ctr = {"i": 0, "t": 0, "o": 0, "p": 0}

        def IN_(dname, ap, n):
            ctr["i"] += 1
            t, b = ins[ctr["i"] % NI]
            K.ld(t[:, :n], b, dname, ap)
            return t, b

        def T_():
            ctr["t"] += 1
            return tmps[ctr["t"] % 8]

        def O_():
            ctr["o"] += 1
            return outs[ctr["o"] % 4]

        def PS_():
            ctr["p"] += 1
            return pss[ctr["p"] % 4]

        def rstd_of(src, bsrc, n, eps):
            t2, bt2 = T_()
            P.op("act", ACT(t2[:, :n], src[:, :n], AF.Square), reads=[bsrc], writes=[bt2])
            ps, bps = PS_()
            P.op("pe", MM(ps[:, :n], bavg[:, :], t2[:, :n]), reads=[bbavg, bt2], writes=[bps])
            t3, bt3 = T_()
            P.op("dve", TS(t3[:, :n], ps[:, :n], eps, ALU.add), reads=[bps], writes=[bt3])
            P.op("act", ACT(t3[:, :n], t3[:, :n], AF.Ln), reads=[bt3], writes=[bt3]); P.op("act", ACT(t3[:, :n], t3[:, :n], AF.Exp, scale=-0.5), reads=[bt3], writes=[bt3])
            return t3, bt3

        for ti, (t0, n, m) in enumerate(token_tiles(cfg)):
            ts = slice(t0, t0 + n)
            for c3 in range(3):
                rows = slice(c3 * 128, (c3 + 1) * 128)
                y0, by0 = IN_("YRW", dr["YRW"][0, rows, ts], n)
                y1, by1 = IN_("YRW", dr["YRW"][1, rows, ts], n)
                bon, bbon = IN_("RW_BONUS", dr["RW_BONUS"][rows, ts], n)
                gat, bgat = IN_("RW_GATE", dr["RW_GATE"][rows, ts], n)
                y, by = T_()
                P.op("dve", TT(y[:, :n], y0[:, :n], y1[:, :n], ALU.add), reads=[by0, by1], writes=[by])
                ps, bps = PS_()
                P.op("pe", MM(ps[:, :n], bavg[:, :], y[:, :n]), reads=[bbavg, by], writes=[bps])
                yc, byc = T_()
                P.op("dve", TT(yc[:, :n], y[:, :n], ps[:, :n], ALU.subtract), reads=[by, bps], writes=[byc])
                rs, brs = rstd_of(yc, byc, n, LNX_EPS)
                t1, bt1 = T_()
                P.op("dve", STT(t1[:, :n], yc[:, :n], vec[:, 0, c3:c3 + 1], rs[:, :n], ALU.mult, ALU.mult),
                     reads=[byc, bvec, brs], writes=[bt1])
                P.op("dve", STT(t1[:, :n], t1[:, :n], vec[:, 1, c3:c3 + 1], bon[:, :n], ALU.add, ALU.add),
                     reads=[bt1, bvec, bbon], writes=[bt1])
                o, bo = O_()
                P.op("dve", TT(o[:, :n], t1[:, :n], gat[:, :n], ALU.mult), reads=[bt1, bgat], writes=[bo])
                K.stq("MX", dr["MX"][rows, ts], o[:, :n], bo)
                y0, by0 = IN_("YGD", dr["YGD"][0, rows, ts], n)
                y1, by1 = IN_("YGD", dr["YGD"][1, rows, ts], n)
                gz, bgz = IN_("Z", dr["Z"][Z_GDG + c3 * 128:Z_GDG + (c3 + 1) * 128, ts], n)
                y, by = T_()
                P.op("dve", TT(y[:, :n], y0[:, :n], y1[:, :n], ALU.add), reads=[by0, by1], writes=[by])
                rs, brs = rstd_of(y, by, n, NORM_EPS)
                t1, bt1 = T_()
                P.op("dve", STT(t1[:, :n], y[:, :n], vec[:, 2, 0:1], rs[:, :n], ALU.mult, ALU.mult),
                     reads=[by, bvec, brs], writes=[bt1])
                t2, bt2 = T_()
                P.op("act", ACT(t2[:, :n], gz[:, :n], AF.Silu), reads=[bgz], writes=[bt2])
                o, bo = O_()
                P.op("dve", TT(o[:, :n], t1[:, :n], t2[:, :n], ALU.mult), reads=[bt1, bt2], writes=[bo])
                K.stq("MX", dr["MX"][384 + c3 * 128:384 + (c3 + 1) * 128, ts], o[:, :n], bo)
            for c2 in range(2):
                rows = slice(c2 * 128, (c2 + 1) * 128)
                y0, by0 = IN_("YS5", dr["YS5"][0, rows, ts], n)
                y1, by1 = IN_("YS5", dr["YS5"][1, rows, ts], n)
                uz, buz = IN_("Z", dr["Z"][Z_S5 + c2 * 128:Z_S5 + (c2 + 1) * 128, ts], n)
                y, by = T_()
                P.op("dve", TT(y[:, :n], y0[:, :n], y1[:, :n], ALU.add), reads=[by0, by1], writes=[by])
                P.op("dve", STT(y[:, :n], uz[:, :n], vec[:, 3, c2:c2 + 1], y[:, :n], ALU.mult, ALU.add),
                     reads=[buz, bvec, by], writes=[by])
                t1, bt1 = T_()
                P.op("act", ACT(t1[:, :n], y[:, :n], AF.Square), reads=[by], writes=[bt1])
                P.op("dve", TS(t1[:, :n], t1[:, :n], 0.044715 * GC, ALU.mult, GC, ALU.add), reads=[bt1], writes=[bt1])
                P.op("dve", TT(t1[:, :n], t1[:, :n], y[:, :n], ALU.mult), reads=[bt1, by], writes=[bt1])
                P.op("act", ACT(t1[:, :n], t1[:, :n], AF.Tanh), reads=[bt1], writes=[bt1])
                P.op("dve", TS(t1[:, :n], t1[:, :n], 1.0, ALU.add, 0.5, ALU.mult), reads=[bt1], writes=[bt1])
                P.op("dve", TT(gl[:, c2, :n], t1[:, :n], y[:, :n], ALU.mult), reads=[bt1, by], writes=[bgl])
            for c2 in range(2):
                ps, bps = PS_()
                for k in range(2):
                    P.op("pe", MM(ps[:, :n], gw[:, k, c2 * 128:(c2 + 1) * 128], gl[:, k, :n], k == 0, k == 1),
                         reads=[bgw, bgl], writes=[bps])
                t1, bt1 = T_()
                P.op("act", ACT(t1[:, :n], ps[:, :n], AF.Sigmoid, bias=vec[:, 4, c2:c2 + 1], scale=1.0),
                     reads=[bps, bvec], writes=[bt1])
                o, bo = O_()
                P.op("dve", TT(o[:, :n], t1[:, :n], gl[:, c2, :n], ALU.mult), reads=[bt1, bgl], writes=[bo])
                K.stq("MX", dr["MX"][768 + c2 * 128:768 + (c2 + 1) * 128, ts], o[:, :n], bo)
    P.barrier()


def stage_C(K, l, xin, xout):
    P, cfg, dr = K.P, K.cfg, K.dram
    L = cfg["L"]
    last = (l == L - 1)
    NJ = FFN // 128
    with contextlib.ExitStack() as st:
        ones, bones = K.sb(st, [128, 128], F32, "ones")
        P.op("dve", MS(ones[:], 1.0), writes=[bones])
        Wo, bWo = K.sb(st, [128, 8, 1024], BF16, "wout", multi=True)
        Wd, bWd = K.sb(st, [128, NJ, 1024], BF16, "wdown", multi=True)
        sq, bsq = K.sb(st, [128, 8, 512], F32, "sq")
        tmp, btmp = K.sb(st, [128, 8, 512], F32, "tmp")
        stg = [(sq, bsq), (tmp, btmp)]
        it = 0
        for cb in range(2):
            w, bw = stg[it % 2]
            it += 1
            K.ld(w[:], bw, "w_out", dr["w_out"][l, :, cb * 512:(cb + 1) * 512].rearrange("(k p) n -> p k n", p=128))
            P.op("act" if it % 2 else "dve", ACT(Wo[:, :, cb * 512:(cb + 1) * 512], w[:], AF.Copy) if it % 2 else
                 CP(Wo[:, :, cb * 512:(cb + 1) * 512], w[:]), reads=[bw], writes=[bWo])
        for j0 in range(0, NJ, 4):
            jn = min(4, NJ - j0)
            w, bw = stg[it % 2]
            it += 1
            wv = w[:].rearrange("p k n -> p (k n)")[:, 0:jn * 1024].rearrange("p (j n) -> p j n", n=1024)
            K.ld(wv, bw, "ffn_w_down",
                 dr["ffn_w_down"][l, j0 * 128:(j0 + jn) * 128, :].rearrange("(j p) n -> p j n", p=128))
            P.op("act" if it % 2 else "dve", ACT(Wd[:, j0:j0 + jn, :], wv, AF.Copy) if it % 2 else
                 CP(Wd[:, j0:j0 + jn, :], wv), reads=[bw], writes=[bWd])
        for nm, dn in (("ffn_w_gate", "WGb"), ("ffn_w_up", "WUb")):
            c0 = 0
            while c0 < FFN:
                mcol = min(512, FFN - c0)
                w, bw = stg[it % 2]
                it += 1
                K.ld(w[:, :, :mcol], bw, nm, dr[nm][l, :, c0:c0 + mcol].rearrange("(k p) n -> p k n", p=128))
                wb, bwb = K.sb(st, [128, 8, 512], BF16, "wb") if c0 == 0 and nm == "ffn_w_gate" else (None, None)
                if wb is not None:
                    K._wb = (wb, bwb)
                wb, bwb = K._wb
                P.op("act" if it % 2 else "dve", ACT(wb[:, :, :mcol], w[:, :, :mcol], AF.Copy) if it % 2 else
                     CP(wb[:, :, :mcol], w[:, :, :mcol]), reads=[bw], writes=[bwb])
                K.stq(dn, dr[dn][:, :, c0:c0 + mcol], wb[:, :, :mcol], bwb)
                c0 += mcol
        P.barrier()
        x, bx = K.sb(st, [128, 8, 512], F32, "x")
        mx, bmx = K.sb(st, [128, 8, 512], F32, "mx")
        mxb, bmxb = K.sb(st, [128, 8, 512], BF16, "mxb")
        h2, bh2 = K.sb(st, [128, 8, 512], BF16, "h2")
        act, bact = K.sb(st, [128, NJ, 512], BF16, "act", multi=True)
        rstd, brstd = K.sb(st, [128, 512], F32, "rstd")
        sgs = [K.sb(st, [128, 512], F32, "sg") for _ in range(2)]
        wgs = [K.sb(st, [128, 2, 8, 128], BF16, "wgu") for _ in range(3)]
        fg, bfg = K.sb(st, [128, 8], F32, "fg")
        K.ld(fg[:], bfg, "final_g", dr["final_g"])
        psn = K.ps(st)
        pw = [K.ps(st) for _ in range(2)]
        pg = [K.ps(st) for _ in range(2)]
        pu = [K.ps(st) for _ in range(2)]
        pd = K.ps(st)
        xin3 = dr[xin].rearrange("(k p) t -> p k t", p=128)
        mx3 = dr["MX"].rearrange("(k p) t -> p k t", p=128)
        gi = 0
        C = cfg["C"]
        for ti, (t0, n, m) in enumerate(token_tiles(cfg)):
            if last and m == 1:
                continue
            K.ld(x[:, :, :n], bx, xin, xin3[:, :, t0:t0 + n])
            K.ld(mx[:, :, :n], bmx, "MX", mx3[:, :, t0:t0 + n])
            P.op("act", ACT(mxb[:, 0:4, :n], mx[:, 0:4, :n], AF.Copy), reads=[bmx], writes=[bmxb])
            P.op("dve", CP(mxb[:, 4:8, :n], mx[:, 4:8, :n]), reads=[bmx], writes=[bmxb])
            for oc in range(8):
                ps, bps = pw[oc % 2]
                for k in range(8):
                    P.op("pe", MM(ps[:, :n], Wo[:, k, oc * 128:(oc + 1) * 128], mxb[:, k, :n], k == 0, k == 7),
                         reads=[bWo, bmxb], writes=[bps])
                P.op("dve", STT(x[:, oc, :n], ps[:, :n], K.mod[:, l, 16 + oc, m:m + 1], x[:, oc, :n], ALU.mult, ALU.add),
                     reads=[bps, K.bmod, bx], writes=[bx])
            norm_mod(K, (sq, bsq, rstd, brstd, tmp, btmp), x, bx, n, K.A2, K.bA2, 24, l, m, h2, bh2, ones, bones, psn)
            for j in range(NJ):
                wg, bwg = wgs[gi % 3]
                K.ld(wg[:, 0], bwg, "WGb", dr["WGb"][:, :, j * 128:(j + 1) * 128])
                K.ld(wg[:, 1], bwg, "WUb", dr["WUb"][:, :, j * 128:(j + 1) * 128])
                psg, bpsg = pg[gi % 2]
                psu, bpsu = pu[gi % 2]
                sg, bsg = sgs[gi % 2]
                gi += 1
                for k in range(8):
                    P.op("pe", MM(psg[:, :n], wg[:, 0, k, :], h2[:, k, :n], k == 0, k == 7), reads=[bwg, bh2], writes=[bpsg])
                for k in range(8):
                    P.op("pe", MM(psu[:, :n], wg[:, 1, k, :], h2[:, k, :n], k == 0, k == 7), reads=[bwg, bh2], writes=[bpsu])
                P.op("act", ACT(sg[:, :n], psg[:, :n], AF.Silu), reads=[bpsg], writes=[bsg])
                P.op("dve", TT(act[:, j, :n], sg[:, :n], psu[:, :n], ALU.mult), reads=[bsg, bpsu], writes=[bact])
            for oc in range(8):
                for j in range(NJ):
                    P.op("pe", MM(pd[0][:, :n], Wd[:, j, oc * 128:(oc + 1) * 128], act[:, j, :n], j == 0, j == NJ - 1),
                         reads=[bWd, bact], writes=[pd[1]])
                P.op("dve", STT(x[:, oc, :n], pd[0][:, :n], K.mod[:, l, 40 + oc, m:m + 1], x[:, oc, :n], ALU.mult, ALU.add),
                     reads=[pd[1], K.bmod, bx], writes=[bx])
            if not last:
                K.stq(xout, dr[xout].rearrange("(k p) t -> p k t", p=128)[:, :, t0:t0 + n], x[:, :, :n], bx)
            else:
                P.op("act", ACT(sq[:, :, :n], x[:, :, :n], AF.Square), reads=[bx], writes=[bsq])
                for k in range(8):
                    P.op("pe", MM(psn[0][:, :n], ones[:, :], sq[:, k, :n], k == 0, k == 7), reads=[bsq, bones], writes=[psn[1]])
                P.op("dve", TS(rstd[:, :n], psn[0][:, :n], 1.0 / D, ALU.mult, NORM_EPS, ALU.add), reads=[psn[1]], writes=[brstd])
                P.op("act", ACT(rstd[:, :n], rstd[:, :n], AF.Ln), reads=[brstd], writes=[brstd]); P.op("act", ACT(rstd[:, :n], rstd[:, :n], AF.Exp, scale=-0.5), reads=[brstd], writes=[brstd])
                for k in range(8):
                    P.op("dve", STT(tmp[:, k, :n], x[:, k, :n], fg[:, k:k + 1], rstd[:, :n], ALU.mult, ALU.mult),
                         reads=[bx, bfg, brstd], writes=[btmp])
                K.stq("yT", dr["yT"].rearrange("(k p) t -> p k t", p=128)[:, :, t0 - C:t0 - C + n], tmp[:, :, :n], btmp)
    P.barrier()


def declare(K):
    cfg = K.cfg
    L, TT = cfg["L"], cfg["C"] + cfg["T"]
    K.din("xT", [D, TT])
    K.din("condT", [128, 8, 2])
    K.din("ada_w", [L, D, 6 * D])
    K.din("ada_b", [128, L, 48])
    K.din("norm1_g", [128, L, 8])
    K.din("norm2_g", [128, L, 8])
    K.din("w_in", [L, D, IN_TOTAL])
    K.dscr("Z", [IN_TOTAL, TT])
    K.din("c_bones", [128, 128])
    K.din("rw_mu", [128, L, 11])
    K.din("rw_w0", [128, L, 2, 3])
    K.din("rw_a0", [128, L, 2, 3])
    K.din("rw_wup", [64, L, 2, 384])
    K.din("rw_aup", [64, L, 2, 384])
    K.din("rw_gup", [128, L, 384])
    for nm in ("rw_kk", "rw_ka", "rw_rk", "rw_lnx_g", "rw_lnx_b"):
        K.din(nm, [128, L, 3])
    for nm in ("RW_R", "RW_V", "RW_KK", "RW_BONUS", "RW_GATE"):
        K.dscr(nm, [384, TT])
    for nm in ("RW_LAM", "RW_AS", "RW_KD", "YRW"):
        K.dscr(nm, [2, 384, TT])
    for nm in ("GD_Q", "GD_K", "GD_V"):
        K.dscr(nm, [384, TT])
    for nm in ("GD_BETA", "GD_G", "YGD"):
        K.dscr(nm, [2, 384, TT])
    K.dscr("YS5", [2, 256, TT])
    K.dscr("MX", [D, TT])
    K.dscr("XR0", [D, TT])
    K.dscr("XR1", [D, TT])
    K.dscr("WGb", [128, 8, FFN], BF16)
    K.dscr("WUb", [128, 8, FFN], BF16)
    K.dout("yT", [D, cfg["T"]])
    K.din("w_out", [L, D, D])
    K.din("ffn_w_gate", [L, D, FFN])
    K.din("ffn_w_up", [L, D, FFN])
    K.din("ffn_w_down", [L, FFN, D])
    K.din("final_g", [128, 8])
    K.din("gd_ng", [L, 128, 1])
    K.din("s5_d", [128, L, 2])
    K.din("s5_glu_b", [128, L, 2])
    K.din("s5_glu_w", [128, L, 2, 256])
    K.din("s5_prm", [128, L, 3, 16])
    K.din("s5_B", [128, L, 2, 16, 32])
    K.din("s5_C", [128, L, 2, 16, 32])
    K.din("gd_conv", [128, L, 3, 9])
    K.din("gd_prm", [12, L, 2])
    K.din("c_sel", [12, 2, 3, 128])
    K.din("c_ident", [128, 128])
    K.din("c_masks", [64, 4, 512])
    K.din("c_negm", [64, 4, 512])
    K.din("c_rmask", [64, 512])
    K.din("c_bd", [64, 2, 512])


def build(cfg):
    nc = bass.Bass("TRN2", target_bir_lowering=False)
    K = Ctx(nc, cfg)
    declare(K)
    stop = cfg.get("stop", "")
    skip = cfg.get("skip", "")
    with contextlib.ExitStack() as gst:
        stage_mods(K, gst)
        xin = "xT"
        for l in range(cfg["L"]):
            xout = "XR%d" % (l % 2)
            K.xin = xin
            stage_A(K, l, xin)
            if stop == "A":
                break
            stage_rw_pre(K, l)
            if "rw" not in skip:
                stage_scan(K, l, "rw")
            stage_gd_pre(K, l)
            if "gd" not in skip:
                stage_scan(K, l, "gd")
            stage_s5(K, l)
            if stop == "B":
                break
            stage_post(K, l)
            if stop == "P":
                break
            stage_C(K, l, xin, xout)
            xin = xout
        K.P.emit()
    return nc


OUT_NAMES = ["yT"]


def core_inputs(inp, b, cfg):
    C, T = cfg["C"], cfg["T"]
    f = lambda a: np.ascontiguousarray(a, dtype=np.float32)
    d = {}
    d["xT"] = f(np.concatenate([inp["ctx"][b][:C], inp["x"][b][:T]], 0).T)
    L = cfg["L"]
    pl = lambda a: f(a[:L].reshape(L, -1, 128).transpose(2, 0, 1))
    d["condT"] = f(np.stack([inp["c"][b], inp["c_ctx"]], 1).reshape(8, 128, 2).transpose(1, 0, 2))
    d["ada_b"] = pl(inp["ada_b"])
    d["norm1_g"] = pl(inp["norm1_g"])
    d["norm2_g"] = pl(inp["norm2_g"])
    for k in ("ada_w", "w_in"):
        d[k] = f(inp[k][:L])
    bo = np.zeros((128, 128), np.float32)
    bo[:64, :64] = 1.0
    bo[64:, 64:] = 1.0
    d["c_bones"] = bo
    for k in ("w_out", "ffn_w_gate", "ffn_w_up", "ffn_w_down"):
        d[k] = f(inp[k][:L])
    d["final_g"] = f(inp["final_g"].reshape(8, 128).T)
    d["gd_ng"] = f(np.tile(inp["gd_norm_g"][:L], (1, 2))[:, :, None])
    d["s5_d"] = pl(inp["s5_d"])
    d["s5_glu_b"] = pl(inp["s5_glu_b"])
    d["s5_glu_w"] = f(inp["s5_glu_w"][:L].reshape(L, 2, 128, 256).transpose(2, 0, 1, 3))
    d["c_ident"] = np.eye(128, dtype=np.float32)
    def s5v(a):
        return a[:L].reshape(L, 2, 8, 2, 64).transpose(3, 4, 0, 1, 2).reshape(128, L, 16)
    ldt = np.repeat(inp["s5_log_dt"][:L][..., None], 64, axis=-1)
    d["s5_prm"] = f(np.stack([s5v(inp["s5_lam_re"]), s5v(inp["s5_lam_im"]), s5v(ldt)], 2))
    def s5blk(a, kind):
        out = np.zeros((128, L, 16, 32), np.float32)
        for dd in range(2):
            for gp in range(8):
                for gl in range(2):
                    blk = a[:L, dd, gp * 2 + gl]
                    if kind == "C":
                        blk = blk.transpose(0, 2, 1)
                    out[gl * 64:(gl + 1) * 64, :, dd * 8 + gp, gl * 16:(gl + 1) * 16] = blk.transpose(1, 0, 2)
        return out
    d["s5_B"] = f(np.stack([s5blk(inp["s5_b_re"], "B"), s5blk(inp["s5_b_im"], "B")], 2))
    d["s5_C"] = f(np.stack([s5blk(inp["s5_c_re"], "C"), s5blk(inp["s5_c_im"], "C")], 2))
    d["gd_conv"] = f(inp["gd_conv"][:L].reshape(L, 3, 9, 128).transpose(3, 0, 1, 2))
    d["gd_prm"] = f(np.stack([inp["gd_a_log"][:L].reshape(L, 12), inp["gd_dt_bias"][:L].reshape(L, 12)], -1).transpose(1, 0, 2))
    sel = np.zeros((12, 2, 3, 128), np.float32)
    for dd in range(2):
        for c3 in range(3):
            for p in range(128):
                sel[dd * 6 + 2 * c3 + p // 64, dd, c3, p] = 1.0
    d["c_sel"] = sel
    i = np.arange(64)[:, None]
    j = np.arange(64)[None, :]
    ms = np.stack([j < i, j > i, j <= i, j >= i]).astype(np.float32)
    d["c_masks"] = f(np.tile(ms.transpose(1, 0, 2), (1, 1, 8)))
    d["c_negm"] = f((np.tile(ms.transpose(1, 0, 2), (1, 1, 8)) - 1.0) * 30000.0)
    rm = np.ones((64, 512), np.float32)
    rm[:, ::64] = 0.0
    d["c_rmask"] = rm
    bdm = ((i // 16) == (j // 16)).astype(np.float32)
    d["c_bd"] = f(np.tile(np.stack([bdm, 1.0 - bdm], 1), (1, 1, 8)))
    d["rw_mu"] = pl(inp["rw_mu"])
    pl2 = lambda a: f(a[:L].reshape(L, 2, 3, 128).transpose(3, 0, 1, 2))
    d["rw_w0"] = pl2(inp["rw_w0"])
    d["rw_a0"] = pl2(inp["rw_a0"])
    d["rw_wup"] = f(inp["rw_wup"][:L].transpose(2, 0, 1, 3))
    d["rw_aup"] = f(inp["rw_aup"][:L].transpose(2, 0, 1, 3))
    d["rw_gup"] = f(inp["rw_gup"][:L].transpose(1, 0, 2))
    for nm in ("rw_kk", "rw_ka", "rw_lnx_g", "rw_lnx_b"):
        d[nm] = pl(inp[nm])
    d["rw_rk"] = pl(inp["rw_rk"].reshape(inp["rw_rk"].shape[0], 384))
    return d


FULL_CFG = {"T": 8192, "C": 256, "L": 4}
_NC_CACHE = {}


def kernel(**inputs):
    cfg = dict(FULL_CFG)
    inp = {k: np.asarray(v) for k, v in inputs.items()}
    B = inp["x"].shape[0]
    if "nc" not in _NC_CACHE:
        _NC_CACHE["nc"] = build(cfg)
    nc = _NC_CACHE["nc"]
    per_b = [core_inputs(inp, b, cfg) for b in range(B)]
    in_maps = [per_b[i % B] for i in range(8)]
    res = run_bass_kernel_spmd(nc, in_maps, core_ids=list(range(8)))
    out = np.stack([np.ascontiguousarray(res.results[b]["yT"].T) for b in range(B)], 0)
    return out.astype(np.float32)
```
